# Optimizing a Trainium2 kernel written in Bass

```python
import math
import jax, jax.numpy as jnp
from jax import lax
import numpy as np

D_MODEL = 2048
BATCH = 8
SEQ = 2048
DEPTH = 1

N_META = 16
Q_BLOCK = 128
EPS = 1e-6

MLA_HEADS = 16
MLA_Q_RANK = 768
MLA_KV_RANK = 512
MLA_NOPE = 128
MLA_ROPE = 64
MLA_V = 128
MLA_QK = MLA_NOPE + MLA_ROPE
MLA_WIDTH = MLA_HEADS * MLA_V
ROPE_THETA = 10000.0

DIFF_HEADS = 8
DIFF_QK = 128
DIFF_V = 2 * DIFF_QK
DIFF_WIDTH = DIFF_HEADS * DIFF_V
DIFF_QK_WIDTH = DIFF_HEADS * 2 * DIFF_QK

REL_BUCKETS = 32
REL_MAX_DIST = 128

IN_SPLITS = (MLA_Q_RANK, MLA_KV_RANK, MLA_ROPE, MLA_WIDTH,
             DIFF_QK_WIDTH, DIFF_QK_WIDTH, DIFF_WIDTH, DIFF_WIDTH,
             D_MODEL, D_MODEL)
IN_WIDTH = sum(IN_SPLITS)

kernel_name = "hybrid_mla_diffattn_gated_encoder"


def rmsnorm(x, g):
    xf = x.astype(jnp.float32)
    y = xf * lax.rsqrt(jnp.mean(xf * xf, axis=-1, keepdims=True) + EPS)
    return (y * g.astype(jnp.float32)).astype(x.dtype)


def apply_rope(x, pos):
    half = x.shape[-1] // 2
    inv = ROPE_THETA ** (-jnp.arange(half, dtype=jnp.float32) / half)
    ang = pos.astype(jnp.float32)[:, None] * inv[None, :]
    cos = jnp.cos(ang)[None, :, None, :]
    sin = jnp.sin(ang)[None, :, None, :]
    x1 = x[..., :half].astype(jnp.float32)
    x2 = x[..., half:].astype(jnp.float32)
    out = jnp.concatenate([x1 * cos - x2 * sin, x2 * cos + x1 * sin], axis=-1)
    return out.astype(x.dtype)


def t5_bucket(rel):
    nb = REL_BUCKETS // 2
    max_exact = nb // 2
    ret = jnp.where(rel > 0, nb, 0)
    n = jnp.abs(rel)
    nf = jnp.maximum(n, 1).astype(jnp.float32)
    large = max_exact + (jnp.log(nf / max_exact) / math.log(REL_MAX_DIST / max_exact)
                         * (nb - max_exact)).astype(jnp.int32)
    large = jnp.minimum(large, nb - 1)
    return ret + jnp.where(n < max_exact, n, large)


def _to_blocks(t, n_blk):
    pad = n_blk * Q_BLOCK - t.shape[1]
    t = jnp.pad(t, [(0, 0), (0, pad)] + [(0, 0)] * (t.ndim - 2))
    t = t.reshape((t.shape[0], n_blk, Q_BLOCK) + t.shape[2:])
    return jnp.moveaxis(t, 1, 0)


def _from_blocks(o, L):
    o = jnp.moveaxis(o, 0, 1)
    o = o.reshape((o.shape[0], -1) + o.shape[3:])
    return o[:, :L]


def mla_attention(q, k, v):
    L = q.shape[1]
    n_blk = -(-L // Q_BLOCK)
    scale = MLA_QK ** -0.5

    def body(qb):
        s = jnp.einsum("bqhd,bkhd->bhqk", qb, k, preferred_element_type=jnp.float32) * scale
        p = jax.nn.softmax(s, axis=-1).astype(v.dtype)
        return jnp.einsum("bhqk,bkhd->bqhd", p, v)

    return _from_blocks(lax.map(body, _to_blocks(q, n_blk)), L)


def diff_attention(q1, q2, k1, k2, v, lam, rel_bias):
    L = q1.shape[1]
    n_blk = -(-L // Q_BLOCK)
    scale = DIFF_QK ** -0.5
    kpos = jnp.arange(L, dtype=jnp.int32)
    qpos_blocks = jnp.arange(n_blk * Q_BLOCK, dtype=jnp.int32).reshape(n_blk, Q_BLOCK)

    def body(args):
        q1b, q2b, qpos = args
        bias = rel_bias[t5_bucket(kpos[None, :] - qpos[:, None])]
        bias = jnp.transpose(bias, (2, 0, 1)).astype(jnp.float32)[None]
        s1 = jnp.einsum("bqhd,bkhd->bhqk", q1b, k1, preferred_element_type=jnp.float32) * scale + bias
        s2 = jnp.einsum("bqhd,bkhd->bhqk", q2b, k2, preferred_element_type=jnp.float32) * scale + bias
        a = jax.nn.softmax(s1, axis=-1) - lam * jax.nn.softmax(s2, axis=-1)
        return jnp.einsum("bhqk,bkhd->bqhd", a.astype(v.dtype), v)

    out = lax.map(body, (_to_blocks(q1, n_blk), _to_blocks(q2, n_blk), qpos_blocks))
    return _from_blocks(out, L)


def hybrid_layer(h, pos, rel_bias, norm_in, w_in, q_a_norm, kv_a_norm, w_uq, w_ukv,
                 mla_q_norm, mla_k_norm, diff_q_norm, diff_k_norm, diff_lambda,
                 diff_subln, w_branch_a, w_branch_b, w_out, layer_idx):
    B, L, _ = h.shape
    u = rmsnorm(h, norm_in)
    proj = u @ w_in
    split_at = [int(i) for i in np.cumsum(IN_SPLITS)[:-1]]
    c_q, c_kv, k_rope, z_a, q_d, k_d, v_d, z_b, g_a, g_b = jnp.split(proj, split_at, axis=-1)

    q = (rmsnorm(c_q, q_a_norm) @ w_uq).reshape(B, L, MLA_HEADS, MLA_QK)
    kv = (rmsnorm(c_kv, kv_a_norm) @ w_ukv).reshape(B, L, MLA_HEADS, MLA_NOPE + MLA_V)
    k_nope, v_a = kv[..., :MLA_NOPE], kv[..., MLA_NOPE:]
    k_r = jnp.broadcast_to(k_rope[:, :, None, :], (B, L, MLA_HEADS, MLA_ROPE))
    k = jnp.concatenate([k_nope, k_r], axis=-1)
    q = rmsnorm(q, mla_q_norm)
    k = rmsnorm(k, mla_k_norm)
    q = jnp.concatenate([q[..., :MLA_NOPE], apply_rope(q[..., MLA_NOPE:], pos)], axis=-1)
    k = jnp.concatenate([k[..., :MLA_NOPE], apply_rope(k[..., MLA_NOPE:], pos)], axis=-1)
    o_a = mla_attention(q, k, v_a).reshape(B, L, MLA_WIDTH) * jax.nn.silu(z_a)

    q_d = rmsnorm(q_d.reshape(B, L, DIFF_HEADS, 2, DIFF_QK), diff_q_norm)
    k_d = rmsnorm(k_d.reshape(B, L, DIFF_HEADS, 2, DIFF_QK), diff_k_norm)
    v_d = v_d.reshape(B, L, DIFF_HEADS, DIFF_V)
    lam_init = 0.8 - 0.6 * math.exp(-0.3 * layer_idx)
    lv = diff_lambda.astype(jnp.float32)
    lam = jnp.exp(jnp.sum(lv[0] * lv[1])) - jnp.exp(jnp.sum(lv[2] * lv[3])) + lam_init
    o_b = diff_attention(q_d[..., 0, :], q_d[..., 1, :], k_d[..., 0, :], k_d[..., 1, :],
                         v_d, lam, rel_bias)
    o_b = rmsnorm(o_b, diff_subln) * (1.0 - lam_init)
    o_b = o_b.reshape(B, L, DIFF_WIDTH) * jax.nn.silu(z_b)

    m = jax.nn.sigmoid(g_a) * (o_a @ w_branch_a) + jax.nn.sigmoid(g_b) * (o_b @ w_branch_b)
    return h + m @ w_out


def setup_inputs(seed: int = 0) -> dict:
    key = jax.random.key(seed)
    ks = jax.random.split(key, 18)
    f32 = jnp.float32

    def nrm(k, shape, scale):
        return jax.random.normal(k, shape, f32) * scale

    def gain(k, shape):
        return 1.0 + 0.02 * jax.random.normal(k, shape, f32)

    Lr = DEPTH
    return {
        "x": nrm(ks[0], (BATCH, SEQ, D_MODEL), 1.0),
        "meta_tokens": nrm(ks[1], (N_META, D_MODEL), 1.0),
        "rel_bias": nrm(ks[2], (REL_BUCKETS, DIFF_HEADS), 0.5),
        "norm_in": gain(ks[3], (Lr, D_MODEL)),
        "w_in": nrm(ks[4], (Lr, D_MODEL, IN_WIDTH), D_MODEL ** -0.5),
        "q_a_norm": gain(ks[5], (Lr, MLA_Q_RANK)),
        "kv_a_norm": gain(ks[6], (Lr, MLA_KV_RANK)),
        "w_uq": nrm(ks[7], (Lr, MLA_Q_RANK, MLA_HEADS * MLA_QK), MLA_Q_RANK ** -0.5),
        "w_ukv": nrm(ks[8], (Lr, MLA_KV_RANK, MLA_HEADS * (MLA_NOPE + MLA_V)), MLA_KV_RANK ** -0.5),
        "mla_q_norm": gain(ks[9], (Lr, MLA_QK)),
        "mla_k_norm": gain(ks[10], (Lr, MLA_QK)),
        "diff_q_norm": gain(ks[11], (Lr, DIFF_QK)),
        "diff_k_norm": gain(ks[12], (Lr, DIFF_QK)),
        "diff_lambda": nrm(ks[13], (Lr, 4, DIFF_QK), 0.1),
        "diff_subln": gain(ks[14], (Lr, DIFF_V)),
        "w_branch_a": nrm(ks[15], (Lr, MLA_WIDTH, D_MODEL), MLA_WIDTH ** -0.5),
        "w_branch_b": nrm(ks[16], (Lr, DIFF_WIDTH, D_MODEL), DIFF_WIDTH ** -0.5),
        "w_out": nrm(ks[17], (Lr, D_MODEL, D_MODEL), D_MODEL ** -0.5),
    }


def reference(x, meta_tokens, rel_bias, norm_in, w_in, q_a_norm, kv_a_norm, w_uq, w_ukv,
              mla_q_norm, mla_k_norm, diff_q_norm, diff_k_norm, diff_lambda, diff_subln,
              w_branch_a, w_branch_b, w_out):
    B = x.shape[0]
    meta = jnp.broadcast_to(meta_tokens[None].astype(x.dtype), (B, N_META, x.shape[-1]))
    h = jnp.concatenate([meta, x], axis=1)
    pos = jnp.arange(h.shape[1], dtype=jnp.int32)
    for l in range(DEPTH):
        h = hybrid_layer(h, pos, rel_bias, norm_in[l], w_in[l], q_a_norm[l], kv_a_norm[l],
                         w_uq[l], w_ukv[l], mla_q_norm[l], mla_k_norm[l], diff_q_norm[l],
                         diff_k_norm[l], diff_lambda[l], diff_subln[l], w_branch_a[l],
                         w_branch_b[l], w_out[l], l)
    return h[:, N_META:]
```

```python
import math
from contextlib import ExitStack

import numpy as np
import concourse.bass as bass
import concourse.mybir as mybir
from concourse.bass_utils import run_bass_kernel_spmd

F32 = mybir.dt.float32
BF16 = mybir.dt.bfloat16
AF = mybir.ActivationFunctionType
ALU = mybir.AluOpType
AX = mybir.AxisListType

D = 2048
SEQ = 2048
NMETA = 16
L = SEQ + NMETA
EPS = 1e-6
IN_W = 15680
OFF_CQ, OFF_CKV, OFF_KR, OFF_ZA, OFF_QD, OFF_KD, OFF_VD, OFF_ZB, OFF_GA, OFF_GB = (
    0, 768, 1280, 1344, 3392, 5440, 7488, 9536, 11584, 13632)
G_QA, G_KVA, G_MQ, G_MK, G_DQ, G_DK, G_LAM, G_END = 0, 768, 1280, 1472, 1664, 1792, 1920, 2432
NA = 1535
GW = 1408
XLO = -640
LAM_INIT = 0.8 - 0.6 * math.exp(-0.3 * 0)


class _Op:
    __slots__ = ("eng", "fn", "waits", "signal", "sigval", "dma", "pos", "dcount")


class Sched:
    ENGS = ("pe", "act", "dve", "pool", "sp")
    MAP = {"pe": "tensor", "act": "scalar", "dve": "vector", "pool": "gpsimd", "sp": "sync"}

    def __init__(self, nc, es):
        self.nc, self.es = nc, es
        self.sem = {e: es.enter_context(nc.semaphore("sg_" + e)) for e in self.ENGS}
        self.sigcnt = {e: 0 for e in self.ENGS}
        self.npos = {e: 0 for e in self.ENGS}
        self.dsem, self.dcnt = {}, {}
        self.waited = {e: {} for e in self.ENGS}
        self._reset()

    def _reset(self):
        self.ops = {e: [] for e in self.ENGS}
        self.lw, self.rd = {}, {}

    def _add_wait(self, o, ref, kind):
        eng = o.eng
        if ref[0] == "c":
            _, e2, op2 = ref
            if e2 == eng and o.dma is None:
                if eng == "pe" or kind != "raw":
                    return
            if op2.pos <= self.waited[eng].get(e2, -1):
                return
            self.waited[eng][e2] = op2.pos
            op2.signal = True
            o.waits.append(ref)
        else:
            _, sk, cnt = ref
            if cnt <= self.waited[eng].get(("d", sk), 0):
                return
            self.waited[eng][("d", sk)] = cnt
            o.waits.append(ref)

    def op(self, eng, fn, reads=(), writes=(), dma=None, ndma=1):
        o = _Op()
        o.eng, o.fn, o.waits, o.signal, o.sigval, o.dma = eng, fn, [], False, None, dma
        o.pos = self.npos[eng]
        self.npos[eng] += 1
        for k in reads:
            for ref in self.lw.get(k, {}).values():
                self._add_wait(o, ref, "raw")
        for k in writes:
            for ref in self.lw.get(k, {}).values():
                self._add_wait(o, ref, "waw")
            for ref in self.rd.get(k, {}).values():
                self._add_wait(o, ref, "war")
        if dma is not None:
            if dma not in self.dsem:
                self.dsem[dma] = self.es.enter_context(self.nc.semaphore("sd%d" % len(self.dsem)))
                self.dcnt[dma] = 0
            self.dcnt[dma] += ndma
            me, mk = ("d", dma, self.dcnt[dma]), ("d", dma)
        else:
            me, mk = ("c", eng, o), eng
        for k in reads:
            self.rd.setdefault(k, {})[mk] = me
        for k in writes:
            self.lw[k] = {mk: me}
            self.rd[k] = {}
        self.ops[eng].append(o)
        return o

    def pe(self, fn, r=(), w=()):
        return self.op("pe", fn, r, w)

    def act(self, fn, r=(), w=()):
        return self.op("act", fn, r, w)

    def dve(self, fn, r=(), w=()):
        return self.op("dve", fn, r, w)

    def pool(self, fn, r=(), w=()):
        return self.op("pool", fn, r, w)

    def flush(self):
        lastc = {}
        for e in self.ENGS:
            for o in reversed(self.ops[e]):
                if o.dma is None:
                    lastc[e] = o
                    break
        for o in lastc.values():
            o.signal = True
        for e in self.ENGS:
            for o in self.ops[e]:
                if o.dma is None and o.signal:
                    self.sigcnt[e] += 1
                    o.sigval = self.sigcnt[e]
        dcnt = dict(self.dcnt)
        with self.nc.Block() as block:
            for e in self.ENGS:
                ops = self.ops[e]

                def body(eng, e=e, ops=ops):
                    for o in ops:
                        for ref in o.waits:
                            if ref[0] == "c":
                                eng.wait_ge(self.sem[ref[1]], ref[2].sigval)
                            else:
                                eng.wait_ge(self.dsem[ref[1]], 16 * ref[2])
                        r = o.fn(eng)
                        if o.dma is not None:
                            for ins in (r if isinstance(r, (list, tuple)) else [r]):
                                ins.then_inc(self.dsem[o.dma], 16)
                        elif o.signal:
                            r.then_inc(self.sem[e], 1)
                    for e2, o2 in lastc.items():
                        if e2 != e:
                            eng.wait_ge(self.sem[e2], o2.sigval)
                    for sk, c in dcnt.items():
                        eng.wait_ge(self.dsem[sk], 16 * c)

                getattr(block, self.MAP[e])(body)
        for e in self.ENGS:
            for e2, o2 in lastc.items():
                self.waited[e][e2] = max(self.waited[e].get(e2, -1), o2.pos)
            for sk, c in dcnt.items():
                self.waited[e][("d", sk)] = c
        self._reset()


def tok(tt):
    return (128, tt * 128) if tt < 16 else (NMETA, SEQ)


def build(dbg=False, stop=None, n_mla_groups=8, n_diff_heads=8):
    nc = bass.Bass("TRN2", target_bir_lowering=False)

    def din(name, shape, dtype=F32):
        return nc.dram_tensor(name, shape, dtype, kind="ExternalInput")

    x_d = din("x", [SEQ, D]).ap()
    meta_d = din("meta", [NMETA, D]).ap()
    w_in_d = din("w_in", [D, IN_W]).ap()
    w_uq_d = din("w_uq", [768, 3072]).ap()
    w_ukv_d = din("w_ukv", [512, 4096]).ap()
    w_a_d = din("w_a", [D, D]).ap()
    w_b_d = din("w_b", [D, D]).ap()
    w_o_d = din("w_o", [D, D]).ap()
    gA_d = din("gA", [128, D]).ap()
    gsm_d = din("gsm", [128, G_END]).ap()
    gcol_d = din("gcol", [128, 2]).ap()
    relb_d = din("relb", [32, 8]).ap()
    rope_d = din("ropeT", [128, 17 * 64]).ap()
    oneh_d = din("onehot", [32, NA]).ap()
    out_d = nc.dram_tensor("out", [SEQ, D], F32, kind="ExternalOutput").ap()
    skind = "ExternalOutput" if dbg else "Internal"
    oa_scr = nc.dram_tensor("oa_scr", [D, SEQ], BF16, kind=skind).ap()
    ob_scr = nc.dram_tensor("ob_scr", [D, SEQ], BF16, kind=skind).ap()
    tA_h = nc.dram_tensor("tA", [8, NA], F32, kind="Internal")
    tA_d = tA_h.ap()
    dbg_out = {}
    if dbg:
        dbg_out["d_uT"] = nc.dram_tensor("d_uT", [128, 16 * L], BF16, kind="ExternalOutput").ap()
        dbg_out["d_G"] = nc.dram_tensor("d_G", [128, GW], F32, kind="ExternalOutput").ap()

    w_in_v = w_in_d.rearrange("(k p) n -> p k n", p=128)
    w_uq_v = w_uq_d.rearrange("(k p) n -> p k n", p=128)
    w_ukv_v = w_ukv_d.rearrange("(k p) n -> p k n", p=128)

    with ExitStack() as es:
        S = Sched(nc, es)

        def sb(name, shape, dtype, stack=es):
            return stack.enter_context(nc.sbuf_tensor("s_" + name, shape, dtype))

        def ps(name, shape, dtype, stack):
            return stack.enter_context(nc.psum_tensor("p_" + name, shape, dtype))

        def dma(q, out, in_, r=(), w=(), sem=None):
            return S.op(q, lambda e: e.dma_start(out=out, in_=in_), r, w, dma=sem)

        def wload(dst, src, nk, r=(), w=(), sem=None, nsplit=2):
            step = max(1, nk // nsplit)
            parts = [(k0, min(nk, k0 + step)) for k0 in range(0, nk, step)]

            def fn(e):
                return [e.dma_start(out=dst[:, a:b, :], in_=src[:, a:b, :]) for a, b in parts]
            return S.op("pool", fn, r, w, dma=sem, ndma=len(parts))

        uT = sb("uT", [128, 16, L], BF16)
        ident = sb("ident", [128, 128], BF16)
        ones = sb("ones", [128, 128], BF16)
        gsm = sb("gsm", [128, G_END], F32)
        gqp = sb("gqp", [128, 192], F32)
        gsub = sb("gsub", [128, 2], F32)
        ropeT = sb("ropeT", [128, 17, 64], F32)
        neglam = sb("neglam", [128, 1], F32)

        def ukeys(c0, n):
            return [("uT", t) for t in range(c0 // 128, (c0 + n - 1) // 128 + 1)]

        with ExitStack() as sAC:
            cqnT = sb("cqnT", [128, 6, SEQ], BF16, sAC)
            ckvnT = sb("ckvnT", [128, 4, L], BF16, sAC)
            krT = sb("krT", [64, L], BF16, sAC)
            sskr = sb("sskr", [128, 17], F32, sAC)

            with ExitStack() as sAB:
                identf = sb("identf", [128, 128], F32, sAB)
                gA = sb("gA", [128, D], F32, sAB)
                relb = sb("relb", [32, 8], F32, sAB)
                oneh = sb("oneh", [32, NA], F32, sAB)
                tsb = sb("tsb", [8, NA], F32, sAB)
                lamt = sb("lamt", [128, 256], F32, sAB)
                lams = sb("lams", [128, 4], F32, sAB)
                xt = [sb("xt%d" % i, [128, D], F32, sAB) for i in range(2)]
                ub = [sb("ub%d" % i, [128, D], BF16, sAB) for i in range(2)]
                junk = sb("junk", [128, D], BF16, sAB)
                ssx = sb("ssx", [128, 17], F32, sAB)
                lnx = sb("lnx", [128, 17], F32, sAB)
                rsx = sb("rsx", [128, 17], F32, sAB)
                B = [ps("pB%d" % i, [128, 512], F32, sAB) for i in range(4)]
                T = [ps("pT%d" % i, [128, 1024], BF16, sAB) for i in range(2)]

                dma("sp", gsm[:, :], gsm_d, w=["gsm"], sem="k1")
                dma("sp", gsub[:, :], gcol_d, w=["gsub"], sem="k2")
                dma("sp", ropeT[:, :, :], rope_d.rearrange("p (t f) -> p t f", f=64), w=["rope"], sem="k3")
                dma("sp", relb[:, :], relb_d, w=["relb"], sem="k4")
                dma("sp", oneh[:, :], oneh_d, w=["oneh"], sem="k5")
                dma("sp", gA[:, :], gA_d, w=["gA"], sem="k6")
                S.pool(lambda e: e.memset(identf[:, :], 0.0), w=["identf"])
                S.pool(lambda e: e.affine_select(out=identf[:, :], in_=identf[:, :], pattern=[[-1, 128]],
                                                 compare_op=ALU.not_equal, fill=1.0, base=0, channel_multiplier=1),
                       r=["identf"], w=["identf"])
                S.dve(lambda e: e.tensor_copy(out=ident[:, :], in_=identf[:, :]), r=["identf"], w=["ident"])
                S.dve(lambda e: e.memset(ones[:, :], 1.0), w=["ones"])
                S.dve(lambda e: e.tensor_scalar_mul(out=gA[:, :], in0=gA[:, :], scalar1=math.sqrt(D)), r=["gA"], w=["gA"])
                S.dve(lambda e: e.tensor_scalar_mul(out=gsm[:, G_QA:G_KVA], in0=gsm[:, G_QA:G_KVA], scalar1=math.sqrt(768.0)),
                      r=["gsm"], w=["gsm"])
                S.dve(lambda e: e.tensor_scalar_mul(out=gsm[:, G_KVA:G_MQ], in0=gsm[:, G_KVA:G_MQ], scalar1=math.sqrt(512.0)),
                      r=["gsm"], w=["gsm"])
                S.dve(lambda e: e.scalar_tensor_tensor(out=gqp[:, 0:128], in0=gsm[:, G_MQ:G_MQ + 128], scalar=math.sqrt(192.0),
                                                       in1=gsm[:, G_MK:G_MK + 128], op0=ALU.mult, op1=ALU.mult),
                      r=["gsm"], w=["gqp"])
                S.dve(lambda e: e.tensor_scalar_mul(out=gqp[:, 128:192], in0=gsm[:, G_MQ + 128:G_MQ + 192], scalar1=math.sqrt(192.0)),
                      r=["gsm"], w=["gqp"])
                S.dve(lambda e: e.scalar_tensor_tensor(out=gsm[:, G_DQ:G_DK], in0=gsm[:, G_DQ:G_DK], scalar=math.sqrt(128.0),
                                                       in1=gsm[:, G_DK:G_LAM], op0=ALU.mult, op1=ALU.mult),
                      r=["gsm"], w=["gsm"])
                S.dve(lambda e: e.tensor_scalar_mul(out=gsub[:, :], in0=gsub[:, :], scalar1=(1.0 - LAM_INIT) * 16.0),
                      r=["gsub"], w=["gsub"])
                S.dve(lambda e: e.tensor_tensor(out=lamt[:, 0:128], in0=gsm[:, G_LAM:G_LAM + 128], in1=gsm[:, G_LAM + 128:G_LAM + 256],
                                                op=ALU.mult), r=["gsm"], w=["lamt"])
                S.dve(lambda e: e.tensor_tensor(out=lamt[:, 128:256], in0=gsm[:, G_LAM + 256:G_LAM + 384], in1=gsm[:, G_LAM + 384:G_LAM + 512],
                                                op=ALU.mult), r=["gsm"], w=["lamt"])
                S.dve(lambda e: e.tensor_reduce(out=lams[:, 0:2], in_=lamt[:, :].rearrange("p (a f) -> p a f", a=2), axis=AX.X, op=ALU.add),
                      r=["lamt"], w=["lams"])
                S.act(lambda e: e.activation(out=lams[:, 2:4], in_=lams[:, 0:2], func=AF.Exp), r=["lams"], w=["lams2"])
                S.dve(lambda e: e.tensor_scalar(out=neglam[:, :], in0=lams[:, 3:4], scalar1=lams[:, 2:3], scalar2=-LAM_INIT,
                                                op0=ALU.subtract, op1=ALU.add), r=["lams2"], w=["neglam"])
                for j, (a0, a1) in enumerate(((0, 512), (512, 1024), (1024, NA))):
                    S.pe(lambda e, a0=a0, a1=a1, j=j: e.matmul(B[j][0:8, 0:a1 - a0], lhsT=relb[:, :], rhs=oneh[:, a0:a1], start=True, stop=True),
                         r=["relb", "oneh"], w=["B%d" % j])
                    S.dve(lambda e, a0=a0, a1=a1, j=j: e.tensor_copy(out=tsb[:, a0:a1], in_=B[j][0:8, 0:a1 - a0]), r=["B%d" % j], w=["tsb"])
                dma("sp", tA_d, tsb[:, :], r=["tsb"], w=["tA"], sem="c1")

                for tt in range(17):
                    P, c0 = tok(tt)
                    b = tt % 2
                    src = x_d[tt * 128:(tt + 1) * 128, :] if tt < 16 else meta_d
                    dma("sp", xt[b][:P, :], src, w=[("xt", b)], sem="xt%d" % b)
                    S.act(lambda e, P=P, b=b, tt=tt: e.activation(out=junk[:P, :], in_=xt[b][:P, :], func=AF.Square, accum_out=ssx[:P, tt:tt + 1]),
                          r=[("xt", b)], w=["junk", ("ssx", tt)])
                    S.act(lambda e, P=P, tt=tt: e.activation(out=lnx[:P, tt:tt + 1], in_=ssx[:P, tt:tt + 1], func=AF.Ln, bias=D * EPS, scale=1.0),
                          r=[("ssx", tt)], w=[("lnx", tt)])
                    S.act(lambda e, P=P, tt=tt: e.activation(out=rsx[:P, tt:tt + 1], in_=lnx[:P, tt:tt + 1], func=AF.Exp, scale=-0.5),
                          r=[("lnx", tt)], w=[("rsx", tt)])
                    S.dve(lambda e, P=P, b=b, tt=tt: e.scalar_tensor_tensor(out=ub[b][:P, :], in0=xt[b][:P, :], scalar=rsx[:P, tt:tt + 1],
                                                                          in1=gA[:P, :], op0=ALU.mult, op1=ALU.mult),
                          r=[("xt", b), ("rsx", tt), "gA"], w=[("ub", b)])
                    for half in range(2):
                        for k in range(8):
                            kk = half * 8 + k
                            S.pe(lambda e, P=P, b=b, k=k, kk=kk, half=half: e.transpose(
                                out=T[half][:, k * 128:k * 128 + P], in_=ub[b][:P, kk * 128:(kk + 1) * 128], identity=ident[:P, :P]),
                                r=[("ub", b), "ident"], w=[("T", half)])
                        cp = (lambda e, P=P, c0=c0, half=half: e.tensor_copy(
                            out=uT[:, half * 8:(half + 1) * 8, c0:c0 + P],
                            in_=T[half][:, :].rearrange("p (k t) -> p k t", k=8)[:, :, 0:P]))
                        ca = (lambda e, P=P, c0=c0, half=half: e.activation(
                            out=uT[:, half * 8:(half + 1) * 8, c0:c0 + P],
                            in_=T[half][:, :].rearrange("p (k t) -> p k t", k=8)[:, :, 0:P], func=AF.Copy))
                        if half == 0:
                            S.dve(cp, r=[("T", half)], w=[("uT", tt)])
                        else:
                            S.act(ca, r=[("T", half)], w=[("uT", tt)])

                if dbg:
                    dma("sp", dbg_out["d_uT"], uT[:, :, :].rearrange("p k t -> p (k t)"), r=[("uT", t) for t in range(17)], w=["dbg"], sem="dbg")
                S.flush()
            if stop == "A":
                return nc
            with ExitStack() as sAB:
                wB = sb("wB", [128, 16, 1344], BF16, sAB)
                cB = [sb("cB%d" % i, [128, 1344], F32, sAB) for i in range(2)]
                junk = sb("junkB", [128, 1344], BF16, sAB)
                ss3 = sb("ss3", [128, 17, 4], F32, sAB)
                ln3 = sb("ln3", [128, 17, 2], F32, sAB)
                rs3 = sb("rs3", [128, 17, 2], F32, sAB)
                cqb = [sb("cqb%d" % i, [128, 768], BF16, sAB) for i in range(2)]
                ckvb = [sb("ckvb%d" % i, [128, 512], BF16, sAB) for i in range(2)]
                krg = sb("krg", [128, 64], F32, sAB)
                rt = sb("rt", [128, 4, 32], F32, sAB)
                krb = [sb("krb%d" % i, [128, 64], BF16, sAB) for i in range(2)]
                B = [ps("bB%d" % i, [128, 512], F32, sAB) for i in range(4)]
                T = [ps("bT%d" % i, [128, 1024], BF16, sAB) for i in range(2)]
                wload(wB, w_in_v[:, :, 0:1344], 16, w=["wB"], sem="wB", nsplit=4)
                groups = ((0, 384), (384, 768), (768, 1280), (1280, 1344))
                for tt in range(17):
                    P, c0 = tok(tt)
                    b = tt % 2
                    for g, (a0, a1) in enumerate(groups):
                        if tt == 16 and g < 2:
                            continue
                        for k in range(16):
                            S.pe(lambda e, P=P, c0=c0, g=g, a0=a0, a1=a1, k=k: e.matmul(
                                B[g][:P, 0:a1 - a0], lhsT=uT[:, k, c0:c0 + P], rhs=wB[:, k, a0:a1], start=(k == 0), stop=(k == 15)),
                                r=[("uT", tt), "wB"], w=["B%d" % g])
                        S.act(lambda e, P=P, g=g, a0=a0, a1=a1, b=b: e.activation(out=cB[b][:P, a0:a1], in_=B[g][:P, 0:a1 - a0], func=AF.Copy),
                              r=["B%d" % g], w=[("cB", b, g)])
                    if tt < 16:
                        S.act(lambda e, P=P, b=b, tt=tt: e.activation(out=junk[:P, 0:768], in_=cB[b][:P, 0:768], func=AF.Square,
                                                                     accum_out=ss3[:P, tt, 0:1]),
                              r=[("cB", b, 0), ("cB", b, 1)], w=["junk", ("ss3", tt, 0)])
                        S.act(lambda e, P=P, tt=tt: e.activation(out=ln3[:P, tt, 0:1], in_=ss3[:P, tt, 0:1], func=AF.Ln, bias=768 * EPS, scale=1.0),
                              r=[("ss3", tt, 0)], w=[("ln3", tt, 0)])
                        S.act(lambda e, P=P, tt=tt: e.activation(out=rs3[:P, tt, 0:1], in_=ln3[:P, tt, 0:1], func=AF.Exp, scale=-0.5),
                              r=[("ln3", tt, 0)], w=[("rs3", tt, 0)])
                    S.act(lambda e, P=P, b=b, tt=tt: e.activation(out=junk[:P, 768:1280], in_=cB[b][:P, 768:1280], func=AF.Square,
                                                                 accum_out=ss3[:P, tt, 1:2]),
                          r=[("cB", b, 2)], w=["junk", ("ss3", tt, 1)])
                    S.act(lambda e, P=P, b=b, tt=tt: e.activation(out=junk[:P, 1280:1344], in_=cB[b][:P, 1280:1344], func=AF.Square,
                                                                 accum_out=ss3[:P, tt, 2:3]),
                          r=[("cB", b, 3)], w=["junk", ("ss3", tt, 2)])
                    S.act(lambda e, P=P, tt=tt: e.activation(out=ln3[:P, tt, 1:2], in_=ss3[:P, tt, 1:2], func=AF.Ln, bias=512 * EPS, scale=1.0),
                          r=[("ss3", tt, 1)], w=[("ln3", tt, 1)])
                    S.act(lambda e, P=P, tt=tt: e.activation(out=rs3[:P, tt, 1:2], in_=ln3[:P, tt, 1:2], func=AF.Exp, scale=-0.5),
                          r=[("ln3", tt, 1)], w=[("rs3", tt, 1)])
                    S.dve(lambda e, P=P, tt=tt: e.tensor_scalar_add(out=sskr[:P, tt:tt + 1], in0=ss3[:P, tt, 2:3], scalar1=192 * EPS),
                          r=[("ss3", tt, 2)], w=[("sskr", tt)])
                    if tt < 16:
                        S.dve(lambda e, P=P, b=b, tt=tt: e.scalar_tensor_tensor(out=cqb[b][:P, :], in0=cB[b][:P, 0:768], scalar=rs3[:P, tt, 0:1],
                                                                              in1=gsm[:P, G_QA:G_KVA], op0=ALU.mult, op1=ALU.mult),
                              r=[("cB", b, 0), ("cB", b, 1), ("rs3", tt, 0), "gsm"], w=[("cqb", b)])
                    S.dve(lambda e, P=P, b=b, tt=tt: e.scalar_tensor_tensor(out=ckvb[b][:P, :], in0=cB[b][:P, 768:1280], scalar=rs3[:P, tt, 1:2],
                                                                          in1=gsm[:P, G_KVA:G_MQ], op0=ALU.mult, op1=ALU.mult),
                          r=[("cB", b, 2), ("rs3", tt, 1), "gsm"], w=[("ckvb", b)])
                    S.dve(lambda e, P=P, b=b: e.tensor_tensor(out=krg[:P, :], in0=cB[b][:P, 1280:1344], in1=gsm[:P, G_MK + 128:G_MK + 192], op=ALU.mult),
                          r=[("cB", b, 3), "gsm"], w=["krg"])
                    cosv = lambda P=P, tt=tt: ropeT[:P, tt, 0:32]
                    sinv = lambda P=P, tt=tt: ropeT[:P, tt, 32:64]
                    S.dve(lambda e, P=P, tt=tt: e.tensor_tensor(out=rt[:P, 0, :], in0=krg[:P, 0:32], in1=ropeT[:P, tt, 0:32], op=ALU.mult),
                          r=["krg", "rope"], w=[("rt", 0)])
                    S.dve(lambda e, P=P, tt=tt: e.tensor_tensor(out=rt[:P, 1, :], in0=krg[:P, 32:64], in1=ropeT[:P, tt, 32:64], op=ALU.mult),
                          r=["krg", "rope"], w=[("rt", 1)])
                    S.dve(lambda e, P=P, tt=tt: e.tensor_tensor(out=rt[:P, 2, :], in0=krg[:P, 32:64], in1=ropeT[:P, tt, 0:32], op=ALU.mult),
                          r=["krg", "rope"], w=[("rt", 2)])
                    S.dve(lambda e, P=P, tt=tt: e.tensor_tensor(out=rt[:P, 3, :], in0=krg[:P, 0:32], in1=ropeT[:P, tt, 32:64], op=ALU.mult),
                          r=["krg", "rope"], w=[("rt", 3)])
                    S.dve(lambda e, P=P, b=b: e.tensor_tensor(out=krb[b][:P, 0:32], in0=rt[:P, 0, :], in1=rt[:P, 1, :], op=ALU.subtract),
                          r=[("rt", 0), ("rt", 1)], w=[("krb", b)])
                    S.dve(lambda e, P=P, b=b: e.tensor_tensor(out=krb[b][:P, 32:64], in0=rt[:P, 2, :], in1=rt[:P, 3, :], op=ALU.add),
                          r=[("rt", 2), ("rt", 3)], w=[("krb", b)])
                    if tt < 16:
                        for k in range(6):
                            S.pe(lambda e, P=P, b=b, k=k: e.transpose(out=T[0][:, k * 128:k * 128 + P], in_=cqb[b][:P, k * 128:(k + 1) * 128],
                                                                      identity=ident[:P, :P]), r=[("cqb", b), "ident"], w=[("T", 0)])
                        S.dve(lambda e, P=P, c0=c0: e.tensor_copy(out=cqnT[:, 0:6, c0:c0 + P],
                                                                  in_=T[0][:, 0:768].rearrange("p (k t) -> p k t", k=6)[:, :, 0:P]),
                              r=[("T", 0)], w=[("cqnT", tt)])
                    for k in range(4):
                        S.pe(lambda e, P=P, b=b, k=k: e.transpose(out=T[1][:, k * 128:k * 128 + P], in_=ckvb[b][:P, k * 128:(k + 1) * 128],
                                                                  identity=ident[:P, :P]), r=[("ckvb", b), "ident"], w=[("T", 1)])
                    S.pe(lambda e, P=P, b=b: e.transpose(out=T[1][0:64, 512:512 + P], in_=krb[b][:P, 0:64], identity=ident[:P, :P]),
                         r=[("krb", b), "ident"], w=[("T", 1)])
                    S.act(lambda e, P=P, c0=c0: e.activation(out=ckvnT[:, 0:4, c0:c0 + P],
                                                             in_=T[1][:, 0:512].rearrange("p (k t) -> p k t", k=4)[:, :, 0:P], func=AF.Copy),
                          r=[("T", 1)], w=[("ckvnT", tt)])
                    S.act(lambda e, P=P, c0=c0: e.activation(out=krT[0:64, c0:c0 + P], in_=T[1][0:64, 512:512 + P], func=AF.Copy),
                          r=[("T", 1)], w=[("krT", tt)])
                S.flush()
            if stop == "B":
                return nc

            with ExitStack() as sC:
                wq = [sb("wq%d" % i, [128, 6, 384], BF16, sC) for i in range(2)]
                wkv = [sb("wkv%d" % i, [128, 4, 512], BF16, sC) for i in range(2)]
                wz = [sb("wz%d" % i, [128, 16, 128], BF16, sC) for i in range(2)]
                qTn = sb("qTn", [128, 2, SEQ], BF16, sC)
                qTr = sb("qTr", [64, 2, SEQ], BF16, sC)
                kTn = sb("kTn", [128, 2, L], BF16, sC)
                Vt = sb("Vt", [128, 17, 2, 128], BF16, sC)
                rstdk = sb("rstdk", [128, 17, 2], F32, sC)
                junkq = sb("junkq", [128, 512], BF16, sC)
                ssq = sb("ssq", [128, 17, 4], F32, sC)
                lnq = sb("lnq", [128, 17, 4], F32, sC)
                rq = sb("rq", [128, 17, 2], F32, sC)
                qb_ = [sb("qb%d" % i, [128, 2, 192], BF16, sC) for i in range(2)]
                qr = sb("qr", [128, 2, 64], F32, sC)
                rt4 = sb("rt4", [128, 4, 2, 32], F32, sC)
                NPT = 4
                PT = [sb("PT%d" % i, [128, 512], BF16, sC) for i in range(NPT)]
                ez = sb("ez", [128, 512], F32, sC)
                sz = sb("sz", [128, 512], F32, sC)
                rec = sb("rec", [128, 512], F32, sC)
                og = [sb("og%d" % i, [128, 512], BF16, sC) for i in range(2)]
                B = [ps("qB%d" % i, [128, 512], F32, sC) for i in range(6)]
                T = [ps("qT%d" % i, [128, 1024], BF16, sC) for i in range(2)]

                def load_group(g):
                    s = g % 2
                    h0 = 2 * g
                    wload(wq[s], w_uq_v[:, :, h0 * 192:(h0 + 2) * 192], 6, w=[("wq", s)], sem="wq%d" % s)
                    wload(wkv[s], w_ukv_v[:, :, h0 * 256:(h0 + 2) * 256], 4, w=[("wkv", s)], sem="wkv%d" % s)

                load_group(0)
                og_i = 0
                for g in range(n_mla_groups):
                    s = g % 2
                    h0 = 2 * g
                    if g + 1 < n_mla_groups:
                        load_group(g + 1)
                    for tt in range(17):
                        P, c0 = tok(tt)
                        b = tt % 2
                        if tt < 16:
                            for k in range(6):
                                S.pe(lambda e, P=P, c0=c0, k=k, s=s: e.matmul(B[0][:P, 0:384], lhsT=cqnT[:, k, c0:c0 + P], rhs=wq[s][:, k, :],
                                                                                start=(k == 0), stop=(k == 5)),
                                     r=[("wq", s)], w=["B0"])
                        for k in range(4):
                            S.pe(lambda e, P=P, c0=c0, k=k, s=s: e.matmul(B[1][:P, 0:512], lhsT=ckvnT[:, k, c0:c0 + P], rhs=wkv[s][:, k, :],
                                                                            start=(k == 0), stop=(k == 3)),
                                 r=[("wkv", s)], w=["B1"])
                        if tt < 16:
                            for hh in range(2):
                                S.act(lambda e, P=P, hh=hh, tt=tt: e.activation(out=junkq[:P, 0:192], in_=B[0][:P, hh * 192:(hh + 1) * 192], func=AF.Square,
                                                                               accum_out=ssq[:P, tt, hh:hh + 1]),
                                      r=["B0"], w=["junkq", ("ssq", tt, hh)])
                            S.act(lambda e, P=P, tt=tt: e.activation(out=lnq[:P, tt, 0:2], in_=ssq[:P, tt, 0:2], func=AF.Ln, bias=192 * EPS, scale=1.0),
                                  r=[("ssq", tt, 0), ("ssq", tt, 1)], w=[("lnq", tt)])
                            S.act(lambda e, P=P, tt=tt: e.activation(out=rq[:P, tt, 0:2], in_=lnq[:P, tt, 0:2], func=AF.Exp, scale=-0.5),
                                  r=[("lnq", tt)], w=[("rq", tt)])
                            for hh in range(2):
                                S.dve(lambda e, P=P, hh=hh, tt=tt, b=b: e.scalar_tensor_tensor(
                                    out=qb_[b][:P, hh, 0:128], in0=B[0][:P, hh * 192:hh * 192 + 128], scalar=rq[:P, tt, hh:hh + 1],
                                    in1=gqp[:P, 0:128], op0=ALU.mult, op1=ALU.mult),
                                    r=["B0", ("rq", tt), "gqp"], w=[("qb", b)])
                                S.dve(lambda e, P=P, hh=hh, tt=tt: e.scalar_tensor_tensor(
                                    out=qr[:P, hh, :], in0=B[0][:P, hh * 192 + 128:hh * 192 + 192], scalar=rq[:P, tt, hh:hh + 1],
                                    in1=gqp[:P, 128:192], op0=ALU.mult, op1=ALU.mult),
                                    r=["B0", ("rq", tt), "gqp"], w=["qr"])
                            cb = lambda P, tt: ropeT[:P, tt, 0:32].unsqueeze(1).to_broadcast([P, 2, 32])
                            sn = lambda P, tt: ropeT[:P, tt, 32:64].unsqueeze(1).to_broadcast([P, 2, 32])
                            S.dve(lambda e, P=P, tt=tt: e.tensor_tensor(out=rt4[:P, 0, :, :], in0=qr[:P, :, 0:32], in1=cb(P, tt), op=ALU.mult),
                                  r=["qr"], w=[("rt4", 0)])
                            S.dve(lambda e, P=P, tt=tt: e.tensor_tensor(out=rt4[:P, 1, :, :], in0=qr[:P, :, 32:64], in1=sn(P, tt), op=ALU.mult),
                                  r=["qr"], w=[("rt4", 1)])
                            S.dve(lambda e, P=P, tt=tt: e.tensor_tensor(out=rt4[:P, 2, :, :], in0=qr[:P, :, 32:64], in1=cb(P, tt), op=ALU.mult),
                                  r=["qr"], w=[("rt4", 2)])
                            S.dve(lambda e, P=P, tt=tt: e.tensor_tensor(out=rt4[:P, 3, :, :], in0=qr[:P, :, 0:32], in1=sn(P, tt), op=ALU.mult),
                                  r=["qr"], w=[("rt4", 3)])
                            S.dve(lambda e, P=P, b=b: e.tensor_tensor(out=qb_[b][:P, :, 128:160], in0=rt4[:P, 0, :, :], in1=rt4[:P, 1, :, :], op=ALU.subtract),
                                  r=[("rt4", 0), ("rt4", 1)], w=[("qb", b)])
                            S.dve(lambda e, P=P, b=b: e.tensor_tensor(out=qb_[b][:P, :, 160:192], in0=rt4[:P, 2, :, :], in1=rt4[:P, 3, :, :], op=ALU.add),
                                  r=[("rt4", 2), ("rt4", 3)], w=[("qb", b)])
                            for hh in range(2):
                                S.pe(lambda e, P=P, b=b, hh=hh: e.transpose(out=T[0][:, hh * 128:hh * 128 + P], in_=qb_[b][:P, hh, 0:128], identity=ident[:P, :P]),
                                     r=[("qb", b)], w=["T0"])
                                S.pe(lambda e, P=P, b=b, hh=hh: e.transpose(out=T[0][0:64, (2 + hh) * 128:(2 + hh) * 128 + P], in_=qb_[b][:P, hh, 128:192],
                                                                          identity=ident[:P, :P]), r=[("qb", b)], w=["T0"])
                            S.dve(lambda e, P=P, c0=c0: e.tensor_copy(out=qTn[:, :, c0:c0 + P],
                                                                      in_=T[0][:, 0:256].rearrange("p (h t) -> p h t", h=2)[:, :, 0:P]),
                                  r=["T0"], w=[("qTn", tt)])
                            S.dve(lambda e, P=P, c0=c0: e.tensor_copy(out=qTr[0:64, :, c0:c0 + P],
                                                                      in_=T[0][0:64, 256:512].rearrange("p (h t) -> p h t", h=2)[:, :, 0:P]),
                                  r=["T0"], w=[("qTr", tt)])
                        S.act(lambda e, P=P, tt=tt: e.activation(out=Vt[:P, tt, :, :], in_=B[1][:P, :].rearrange("p (h f) -> p h f", h=2)[:, :, 128:256],
                                                                 func=AF.Copy), r=["B1"], w=[("Vt", tt)])
                        for hh in range(2):
                            S.act(lambda e, P=P, hh=hh, tt=tt: e.activation(out=junkq[:P, 0:128], in_=B[1][:P, hh * 256:hh * 256 + 128], func=AF.Square,
                                                                           accum_out=ssq[:P, tt, 2 + hh:3 + hh]),
                                  r=["B1"], w=["junkq", ("ssk", tt, hh)])
                        S.act(lambda e, P=P, tt=tt: e.activation(out=lnq[:P, tt, 2:4], in_=ssq[:P, tt, 2:4], func=AF.Ln, bias=sskr[:P, tt:tt + 1], scale=1.0),
                              r=[("ssk", tt, 0), ("ssk", tt, 1)], w=[("lnk", tt)])
                        S.act(lambda e, P=P, tt=tt: e.activation(out=rstdk[:P, tt, 0:2], in_=lnq[:P, tt, 2:4], func=AF.Exp, scale=-0.5),
                              r=[("lnk", tt)], w=[("rstdk", tt)])
                    blocks = [(0, 512), (512, 512), (1024, 512), (1536, 512), (2048, 16)]
                    bi = 0
                    for (c0, n) in blocks:
                        for hh in range(2):
                            bank = 2 + (bi % 2)
                            for k in range(4):
                                S.pe(lambda e, c0=c0, n=n, hh=hh, k=k, s=s, bank=bank: e.matmul(
                                    B[bank][:, 0:n], lhsT=wkv[s][:, k, hh * 256:hh * 256 + 128], rhs=ckvnT[:, k, c0:c0 + n], start=(k == 0), stop=(k == 3)),
                                    r=[("wkv", s)], w=["B%d" % bank])
                            if bi % 2 == 0:
                                S.dve(lambda e, c0=c0, n=n, hh=hh, bank=bank: e.tensor_copy(out=kTn[:, hh, c0:c0 + n], in_=B[bank][:, 0:n]),
                                      r=["B%d" % bank], w=[("kTn", hh, c0)])
                            else:
                                S.act(lambda e, c0=c0, n=n, hh=hh, bank=bank: e.activation(out=kTn[:, hh, c0:c0 + n], in_=B[bank][:, 0:n], func=AF.Copy),
                                      r=["B%d" % bank], w=[("kTn", hh, c0)])
                            bi += 1
                    for hh in range(2):
                        h = h0 + hh
                        wload(wz[hh], w_in_v[:, :, OFF_ZA + h * 128:OFF_ZA + (h + 1) * 128], 16, w=[("wz", hh)], sem="wz%d" % hh)
                        for qb in range(4):
                            q0 = qb * 512
                            rq_keys = [("qTn", t) for t in range(4 * qb, 4 * qb + 4)] + [("qTr", t) for t in range(4 * qb, 4 * qb + 4)]
                            for k in range(16):
                                S.pe(lambda e, k=k, s=s, hh=hh, q0=q0: e.matmul(B[5][:, :], lhsT=wz[hh][:, k, :], rhs=uT[:, k, q0:q0 + 512],
                                                                                start=(k == 0), stop=(k == 15)),
                                     r=[("wz", hh)], w=["B5"])
                            S.act(lambda e: e.activation(out=ez[:, :], in_=B[5][:, :], func=AF.Exp, scale=-1.0), r=["B5"], w=["ez"])
                            S.dve(lambda e: e.tensor_scalar_add(out=ez[:, :], in0=ez[:, :], scalar1=1.0), r=["ez"], w=["ez"])
                            S.dve(lambda e: e.reciprocal(out=ez[:, :], in_=ez[:, :]), r=["ez"], w=["ez"])
                            S.dve(lambda e: e.tensor_tensor(out=sz[:, :], in0=B[5][:, :], in1=ez[:, :], op=ALU.mult), r=["B5", "ez"], w=["sz"])

                            def S_mm(c, hh=hh, q0=q0, rq_keys=rq_keys):
                                P, c0 = tok(c)
                                bank = c % 3
                                S.pe(lambda e: e.matmul(B[bank][:P, :], lhsT=kTn[:, hh, c0:c0 + P], rhs=qTn[:, hh, q0:q0 + 512], start=True, stop=False),
                                     r=rq_keys + [("kTn", hh, (c0 // 512) * 512)], w=["B%d" % bank])
                                S.pe(lambda e: e.matmul(B[bank][:P, :], lhsT=krT[0:64, c0:c0 + P], rhs=qTr[0:64, hh, q0:q0 + 512], start=False, stop=True),
                                     r=rq_keys, w=["B%d" % bank])

                            def EXPc(c, hh=hh):
                                P, c0 = tok(c)
                                bank = c % 3
                                S.act(lambda e: e.activation(out=PT[c % NPT][:P, :], in_=B[bank][:P, :], func=AF.Exp, scale=rstdk[:P, c, hh:hh + 1]),
                                      r=["B%d" % bank, ("rstdk", c)], w=[("PT", c % NPT)])

                            def PVc(c, hh=hh):
                                P, c0 = tok(c)
                                S.pe(lambda e: e.matmul(B[3][:, :], lhsT=Vt[:P, c, hh, :], rhs=PT[c % NPT][:P, :], start=(c == 0), stop=(c == 16)),
                                     r=[("PT", c % NPT), ("Vt", c)], w=["B3"])
                                S.pe(lambda e: e.matmul(B[4][:, :], lhsT=ones[:P, :], rhs=PT[c % NPT][:P, :], start=(c == 0), stop=(c == 16)),
                                     r=[("PT", c % NPT)], w=["B4"])

                            S_mm(0)
                            S_mm(1)
                            for c in range(17):
                                EXPc(c)
                                if c + 2 < 17:
                                    S_mm(c + 2)
                                PVc(c)
                            oi = og_i % 2
                            og_i += 1
                            S.dve(lambda e: e.reciprocal(out=rec[:, :], in_=B[4][:, :]), r=["B4"], w=["rec"])
                            S.dve(lambda e: e.tensor_tensor(out=rec[:, :], in0=B[3][:, :], in1=rec[:, :], op=ALU.mult), r=["B3", "rec"], w=["rec"])
                            S.dve(lambda e, oi=oi: e.tensor_tensor(out=og[oi][:, :], in0=rec[:, :], in1=sz[:, :], op=ALU.mult), r=["rec", "sz"], w=[("og", oi)])
                            dma("sp", oa_scr[h * 128:(h + 1) * 128, q0:q0 + 512], og[oi][:, :], r=[("og", oi)], w=[("oa", h, qb)], sem="og%d" % oi)
                S.flush()
        if stop == "C":
            return nc

        with ExitStack() as sD:
            wr = [sb("wr%d" % i, [128, 16, 256], BF16, sD) for i in range(4)]
            Jf = sb("Jf", [128, 128], F32, sD)
            qdT = sb("qdT", [128, 2, SEQ], BF16, sD)
            kdT = sb("kdT", [128, 2, L], BF16, sD)
            Vd = sb("Vd", [128, 17, 256], BF16, sD)
            Hk = sb("Hk", [128, GW], F32, sD)
            G = sb("G", [128, GW], F32, sD)
            junkd = sb("junkd", [128, 128], BF16, sD)
            ss4 = sb("ss4", [128, 17, 4], F32, sD)
            ln4 = sb("ln4", [128, 17, 4], F32, sD)
            r4 = sb("r4", [128, 17, 4], F32, sD)
            qdb = [sb("qdb%d" % i, [128, 256], BF16, sD) for i in range(2)]
            NSB = 2
            sbias = [sb("sbias%d" % i, [128, 512], F32, sD) for i in range(NSB)]
            NPT = 4
            PT = [sb("PTd%d" % i, [128, 512], BF16, sD) for i in range(NPT)]
            szb = sb("szb", [128, 2, 512], F32, sD)
            ezb = sb("ezb", [128, 512], F32, sD)
            recd = sb("recd", [128, 512], F32, sD)
            t1 = sb("t1", [128, 2, 512], F32, sD)
            t2 = sb("t2", [128, 512], F32, sD)
            od = sb("od", [128, 2, 512], F32, sD)
            sqb = sb("sqb", [128, 2, 512], BF16, sD)
            rsb = sb("rsb", [128, 512], F32, sD)
            onn = sb("onn", [128, 512], F32, sD)
            ogb = [sb("ogb%d" % i, [128, 2, 512], BF16, sD) for i in range(2)]
            B = [ps("dB%d" % i, [128, 512], F32, sD) for i in range(6)]
            T = [ps("dT%d" % i, [128, 1024], BF16, sD) for i in range(2)]
            lnb = rsb
            kdb = qdb

            WOFF = (OFF_QD, OFF_KD, OFF_VD, OFF_ZB)

            def load_w(h, i):
                wload(wr[i], w_in_v[:, :, WOFF[i] + h * 256:WOFF[i] + (h + 1) * 256], 16, w=[("wr", i)], sem="wr%d" % i)

            S.pool(lambda e: e.memset(Jf[:, :], 0.0), w=["Jf"])
            S.pool(lambda e: e.affine_select(out=Jf[:, :], in_=Jf[:, :], pattern=[[1, 128]],
                                             compare_op=ALU.not_equal, fill=1.0, base=-127, channel_multiplier=1),
                   r=["Jf"], w=["Jf"])
            for i in range(4):
                load_w(0, i)
            ogb_i = 0
            for h in range(n_diff_heads):
                S.op("sp", lambda e, h=h: e.dma_start(out=Hk[:, :], in_=bass.AP(tensor=tA_h, offset=h * NA, ap=[[1, 128], [1, GW]])),
                     [], ["Hk"], dma="Hk")
                for j, (a0, a1) in enumerate(((0, 512), (512, 1024), (1024, GW))):
                    bank = 3 + j
                    S.pe(lambda e, a0=a0, a1=a1, bank=bank: e.matmul(B[bank][:, 0:a1 - a0], lhsT=Jf[:, :], rhs=Hk[:, a0:a1], start=True, stop=True),
                         r=["Hk", "Jf"], w=["B%d" % bank])
                    S.dve(lambda e, a0=a0, a1=a1, bank=bank: e.tensor_copy(out=G[:, a0:a1], in_=B[bank][:, 0:a1 - a0]), r=["B%d" % bank], w=["G"])
                if dbg and h == 0:
                    dma("sp", dbg_out["d_G"], G[:, :], r=["G"], w=["dbgG"], sem="dbg")
                for st in range(3):
                    for tt in range(17):
                        if st == 0 and tt == 16:
                            continue
                        P, c0 = tok(tt)
                        b = tt % 2
                        bank = tt % 2
                        for k in range(16):
                            S.pe(lambda e, P=P, c0=c0, k=k, bank=bank, st=st: e.matmul(B[bank][:P, 0:256], lhsT=uT[:, k, c0:c0 + P], rhs=wr[st][:, k, :],
                                                                                         start=(k == 0), stop=(k == 15)),
                                 r=[("wr", st)], w=["B%d" % bank])
                        if st == 2:
                            S.act(lambda e, P=P, tt=tt, bank=bank: e.activation(out=Vd[:P, tt, :], in_=B[bank][:P, 0:256], func=AF.Copy), r=["B%d" % bank], w=[("Vd", tt)])
                            continue
                        for m in range(2):
                            S.act(lambda e, P=P, tt=tt, bank=bank, m=m, st=st: e.activation(out=junkd[:P, :], in_=B[bank][:P, m * 128:(m + 1) * 128], func=AF.Square,
                                                                                            accum_out=ss4[:P, tt, 2 * st + m:2 * st + m + 1]),
                                  r=["B%d" % bank], w=["junkd", ("ss4", tt, st, m)])
                        S.act(lambda e, P=P, tt=tt, st=st: e.activation(out=ln4[:P, tt, 2 * st:2 * st + 2], in_=ss4[:P, tt, 2 * st:2 * st + 2], func=AF.Ln, bias=128 * EPS, scale=1.0),
                              r=[("ss4", tt, st, 0), ("ss4", tt, st, 1)], w=[("ln4", tt, st)])
                        S.act(lambda e, P=P, tt=tt, st=st: e.activation(out=r4[:P, tt, 2 * st:2 * st + 2], in_=ln4[:P, tt, 2 * st:2 * st + 2], func=AF.Exp, scale=-0.5),
                              r=[("ln4", tt, st)], w=[("r4", tt, st)])
                        for m in range(2):
                            if st == 0:
                                S.dve(lambda e, P=P, tt=tt, m=m, b=b, bank=bank: e.scalar_tensor_tensor(out=qdb[b][:P, m * 128:(m + 1) * 128], in0=B[bank][:P, m * 128:(m + 1) * 128],
                                                                                                   scalar=r4[:P, tt, m:m + 1], in1=gsm[:P, G_DQ:G_DK], op0=ALU.mult, op1=ALU.mult),
                                      r=["B%d" % bank, ("r4", tt, st)], w=[("qdb", b)])
                            else:
                                S.dve(lambda e, P=P, tt=tt, m=m, b=b, bank=bank: e.tensor_scalar_mul(out=qdb[b][:P, m * 128:(m + 1) * 128], in0=B[bank][:P, m * 128:(m + 1) * 128],
                                                                                                scalar1=r4[:P, tt, 2 + m:3 + m]),
                                      r=["B%d" % bank, ("r4", tt, st)], w=[("qdb", b)])
                        for m in range(2):
                            S.pe(lambda e, P=P, m=m, b=b: e.transpose(out=T[b][:, m * 128:m * 128 + P], in_=qdb[b][:P, m * 128:(m + 1) * 128], identity=ident[:P, :P]),
                                 r=[("qdb", b)], w=["T%d" % b])
                        dst = qdT if st == 0 else kdT
                        nm = "qdT" if st == 0 else "kdT"
                        if b == 0:
                            S.dve(lambda e, P=P, c0=c0, dst=dst, b=b: e.tensor_copy(out=dst[:, :, c0:c0 + P], in_=T[b][:, 0:256].rearrange("p (h t) -> p h t", h=2)[:, :, 0:P]),
                                  r=["T%d" % b], w=[(nm, tt)])
                        else:
                            S.act(lambda e, P=P, c0=c0, dst=dst, b=b: e.activation(out=dst[:, :, c0:c0 + P], in_=T[b][:, 0:256].rearrange("p (h t) -> p h t", h=2)[:, :, 0:P], func=AF.Copy),
                                  r=["T%d" % b], w=[(nm, tt)])
                    if h + 1 < n_diff_heads:
                        load_w(h + 1, st)
                for qb in range(4):
                    q0 = qb * 512
                    qk = [("qdT", t) for t in range(4 * qb, 4 * qb + 4)]
                    for j in range(2):
                        for k in range(16):
                            S.pe(lambda e, k=k, j=j, q0=q0: e.matmul(B[5][:, :], lhsT=wr[3][:, k, j * 128:(j + 1) * 128], rhs=uT[:, k, q0:q0 + 512],
                                                                          start=(k == 0), stop=(k == 15)), r=[("wr", 3)], w=["B5"])
                        S.act(lambda e: e.activation(out=ezb[:, :], in_=B[5][:, :], func=AF.Exp, scale=-1.0), r=["B5"], w=["ezb"])
                        S.dve(lambda e: e.tensor_scalar_add(out=ezb[:, :], in0=ezb[:, :], scalar1=1.0), r=["ezb"], w=["ezb"])
                        S.dve(lambda e: e.reciprocal(out=ezb[:, :], in_=ezb[:, :]), r=["ezb"], w=["ezb"])
                        S.dve(lambda e, j=j: e.tensor_tensor(out=szb[:, j, :], in0=B[5][:, :], in1=ezb[:, :], op=ALU.mult), r=["B5", "ezb"], w=[("szb", j)])
                    for m in range(2):
                        def S_mm(c, m=m, q0=q0, qk=qk):
                            P, c0 = tok(c)
                            bank = c % 2
                            S.pe(lambda e: e.matmul(B[bank][:P, :], lhsT=kdT[:, m, c0:c0 + P], rhs=qdT[:, m, q0:q0 + 512], start=True, stop=True),
                                 r=qk + [("kdT", c)], w=["B%d" % bank])

                        def EXPc(c, qb=qb):
                            P, c0 = tok(c)
                            bank = c % 2
                            x0 = (512 * qb - 128 * c) if c < 16 else (16 + 512 * qb)
                            w0 = min(max(x0, XLO), 256) - XLO
                            S.dve(lambda e: e.tensor_tensor(out=sbias[c % NSB][:P, :], in0=B[bank][:P, :], in1=G[:P, w0:w0 + 512], op=ALU.add),
                                  r=["B%d" % bank, "G"], w=[("sbias", c % NSB)])
                            S.act(lambda e: e.activation(out=PT[c % NPT][:P, :], in_=sbias[c % NSB][:P, :], func=AF.Exp),
                                  r=[("sbias", c % NSB)], w=[("PTd", c % NPT)])

                        def PVc(c):
                            P, c0 = tok(c)
                            for j in range(2):
                                S.pe(lambda e, j=j: e.matmul(B[2 + j][:, :], lhsT=Vd[:P, c, j * 128:(j + 1) * 128], rhs=PT[c % NPT][:P, :], start=(c == 0), stop=(c == 16)),
                                     r=[("PTd", c % NPT), ("Vd", c)], w=["B%d" % (2 + j)])
                            S.pe(lambda e: e.matmul(B[4][:, :], lhsT=ones[:P, :], rhs=PT[c % NPT][:P, :], start=(c == 0), stop=(c == 16)),
                                 r=[("PTd", c % NPT)], w=["B4"])

                        S_mm(0)
                        for c in range(17):
                            EXPc(c)
                            if c + 1 < 17:
                                S_mm(c + 1)
                            PVc(c)
                        S.dve(lambda e: e.reciprocal(out=recd[:, :], in_=B[4][:, :]), r=["B4"], w=["recd"])
                        for j in range(2):
                            if m == 0:
                                S.dve(lambda e, j=j: e.tensor_tensor(out=t1[:, j, :], in0=B[2 + j][:, :], in1=recd[:, :], op=ALU.mult),
                                      r=["B%d" % (2 + j), "recd"], w=[("t1", j)])
                            else:
                                S.dve(lambda e, j=j: e.tensor_tensor(out=t2[:, :], in0=B[2 + j][:, :], in1=recd[:, :], op=ALU.mult),
                                      r=["B%d" % (2 + j), "recd"], w=["t2"])
                                S.dve(lambda e, j=j: e.scalar_tensor_tensor(out=od[:, j, :], in0=t2[:, :], scalar=neglam[:, 0:1], in1=t1[:, j, :],
                                                                            op0=ALU.mult, op1=ALU.add), r=["t2", ("t1", j)], w=[("od", j)])
                    for j in range(2):
                        S.act(lambda e, j=j: e.activation(out=sqb[:, j, :], in_=od[:, j, :], func=AF.Square), r=[("od", j)], w=[("sqb", j)])
                    for j in range(2):
                        S.pe(lambda e, j=j: e.matmul(B[4][:, :], lhsT=ones[:, :], rhs=sqb[:, j, :], start=(j == 0), stop=(j == 1)), r=[("sqb", j)], w=["B4"])
                    S.act(lambda e: e.activation(out=lnb[:, :], in_=B[4][:, :], func=AF.Ln, bias=256 * EPS, scale=1.0), r=["B4"], w=["rsb"])
                    S.act(lambda e: e.activation(out=rsb[:, :], in_=lnb[:, :], func=AF.Exp, scale=-0.5), r=["rsb"], w=["rsb"])
                    oi = ogb_i % 2
                    ogb_i += 1
                    for j in range(2):
                        S.dve(lambda e, j=j: e.scalar_tensor_tensor(out=onn[:, :], in0=od[:, j, :], scalar=gsub[:, j:j + 1], in1=rsb[:, :], op0=ALU.mult, op1=ALU.mult),
                              r=[("od", j), "rsb"], w=["onn"])
                        S.dve(lambda e, j=j, oi=oi: e.tensor_tensor(out=ogb[oi][:, j, :], in0=onn[:, :], in1=szb[:, j, :], op=ALU.mult),
                              r=["onn", ("szb", j)], w=[("ogb", oi)])
                    dma("sp", ob_scr[h * 256:(h + 1) * 256, q0:q0 + 512].rearrange("(j p) t -> p j t", p=128), ogb[oi][:, :, :],
                        r=[("ogb", oi)], w=[("ob", h, qb)], sem="ogb%d" % oi)
                if h + 1 < n_diff_heads:
                    load_w(h + 1, 3)
            S.flush()
        if stop == "D":
            return nc

        with ExitStack() as sF:
            oh = sb("oh", [128, 16, 1024], BF16, sF)
            mT = sb("mT", [128, 16, 1024], BF16, sF)
            wsb = sb("wsb", [128, 16384], BF16, sF)
            eg = [sb("eg%d" % i, [128, 512], F32, sF) for i in range(2)]
            tm = [sb("tm%d" % i, [128, 512], F32, sF) for i in range(2)]
            xin = [sb("xin%d" % i, [128, 512], F32, sF) for i in range(2)]
            yo = [sb("yo%d" % i, [128, 512], F32, sF) for i in range(2)]
            B = [ps("fB%d" % i, [128, 512], F32, sF) for i in range(8)]

            def ws(i):
                return wsb[:, i * 2048:(i + 1) * 2048].rearrange("p (k c) -> p k c", k=16)

            def wo(s_):
                return wsb[:, s_ * 8192:(s_ + 1) * 8192].rearrange("p (k c) -> p k c", k=16)

            w_a_v = w_a_d.rearrange("(k p) n -> p k n", p=128)
            w_b_v = w_b_d.rearrange("(k p) n -> p k n", p=128)
            w_o_v = w_o_d.rearrange("(k p) n -> p k n", p=128)
            wi = 0
            oi = 0
            xi = 0
            for hf in range(2):
                t0 = hf * 1024
                for br in range(2):
                    scr = (oa_scr, ob_scr)[br]
                    wv = (w_a_v, w_b_v)[br]
                    goff = (OFF_GA, OFF_GB)[br]
                    sv = scr.rearrange("(k p) t -> p k t", p=128)

                    def ldo(e, sv=sv, t0=t0):
                        return [e.dma_start(out=oh[:, a:a + 4, :], in_=sv[:, a:a + 4, t0:t0 + 1024]) for a in range(0, 16, 4)]
                    S.op("sp", ldo, [], ["oh"], dma="oh", ndma=4)
                    slots = {}

                    def ldw(j, wv=wv, goff=goff):
                        nonlocal wi
                        a, b2 = wi % 8, (wi + 1) % 8
                        wi += 2
                        wload(ws(a), wv[:, :, j * 128:(j + 1) * 128], 16, w=[("ws", a)], sem="ws%d" % a)
                        wload(ws(b2), w_in_v[:, :, goff + j * 128:goff + (j + 1) * 128], 16, w=[("ws", b2)], sem="ws%d" % b2)
                        slots[j] = (a, b2)

                    ldw(0)
                    ldw(1)
                    for j in range(16):
                        if j + 2 < 16:
                            ldw(j + 2)
                        a, b2 = slots[j]
                        for blk in range(2):
                            st = (j * 2 + blk) % 4
                            mb, gb = 2 * st, 2 * st + 1
                            c0 = blk * 512
                            ei = (j * 2 + blk) % 2
                            for k in range(16):
                                S.pe(lambda e, k=k, a=a, mb=mb, c0=c0: e.matmul(B[mb][:, :], lhsT=ws(a)[:, k, :], rhs=oh[:, k, c0:c0 + 512], start=(k == 0), stop=(k == 15)),
                                     r=[("ws", a), "oh"], w=["B%d" % mb])
                            for k in range(16):
                                S.pe(lambda e, k=k, b2=b2, gb=gb, c0=c0, t0=t0: e.matmul(B[gb][:, :], lhsT=ws(b2)[:, k, :], rhs=uT[:, k, t0 + c0:t0 + c0 + 512],
                                                                                           start=(k == 0), stop=(k == 15)),
                                     r=[("ws", b2)], w=["B%d" % gb])
                            S.act(lambda e, gb=gb, ei=ei: e.activation(out=eg[ei][:, :], in_=B[gb][:, :], func=AF.Exp, scale=-1.0), r=["B%d" % gb], w=[("eg", ei)])
                            S.dve(lambda e, ei=ei: e.tensor_scalar_add(out=eg[ei][:, :], in0=eg[ei][:, :], scalar1=1.0), r=[("eg", ei)], w=[("eg", ei)])
                            S.dve(lambda e, ei=ei: e.reciprocal(out=eg[ei][:, :], in_=eg[ei][:, :]), r=[("eg", ei)], w=[("eg", ei)])
                            if br == 0:
                                S.dve(lambda e, ei=ei, mb=mb, j=j, c0=c0: e.tensor_tensor(out=mT[:, j, c0:c0 + 512], in0=B[mb][:, :], in1=eg[ei][:, :], op=ALU.mult),
                                      r=["B%d" % mb, ("eg", ei)], w=[("mT", j, blk)])
                            else:
                                S.dve(lambda e, ei=ei, mb=mb: e.tensor_tensor(out=tm[ei][:, :], in0=B[mb][:, :], in1=eg[ei][:, :], op=ALU.mult),
                                      r=["B%d" % mb, ("eg", ei)], w=[("tm", ei)])
                                S.dve(lambda e, ei=ei, j=j, c0=c0: e.tensor_tensor(out=mT[:, j, c0:c0 + 512], in0=tm[ei][:, :], in1=mT[:, j, c0:c0 + 512], op=ALU.add),
                                      r=[("tm", ei), ("mT", j, blk)], w=[("mT", j, blk)])
                for cg in range(4):
                    so = cg % 2
                    wload(wo(so), w_o_v[:, :, cg * 512:(cg + 1) * 512], 16, w=[("ws", 4 * so + i) for i in range(4)], sem="wo%d" % so, nsplit=4)
                    for t in range(8):
                        r0 = t0 + t * 128
                        xs = xi % 2
                        xi += 1
                        bank = xi % 8
                        dma("sp", xin[xs][:, :], x_d[r0:r0 + 128, cg * 512:(cg + 1) * 512], w=[("xin", xs)], sem="xin%d" % xs)
                        for k in range(16):
                            S.pe(lambda e, k=k, t=t, so=so, bank=bank: e.matmul(B[bank][:, :], lhsT=mT[:, k, t * 128:(t + 1) * 128], rhs=wo(so)[:, k, :],
                                                                              start=(k == 0), stop=(k == 15)),
                                 r=[("mT", k, t // 4)] + [("ws", 4 * so + i) for i in range(4)], w=["B%d" % bank])
                        S.dve(lambda e, xs=xs, bank=bank: e.tensor_tensor(out=yo[xs][:, :], in0=B[bank][:, :], in1=xin[xs][:, :], op=ALU.add),
                              r=["B%d" % bank, ("xin", xs)], w=[("yo", xs)])
                        dma("sp", out_d[r0:r0 + 128, cg * 512:(cg + 1) * 512], yo[xs][:, :], r=[("yo", xs)], w=[("out", r0, cg)], sem="yo%d" % xs)
            S.flush()
    return nc


def _t5_bucket_np(rel):
    nb, max_exact = 16, 8
    ret = np.where(rel > 0, nb, 0)
    n = np.abs(rel)
    nf = np.maximum(n, 1).astype(np.float32)
    large = max_exact + (np.log(nf / np.float32(max_exact)) / np.float32(math.log(128 / max_exact))
                         * np.float32(nb - max_exact)).astype(np.int32)
    large = np.minimum(large, nb - 1)
    return ret + np.where(n < max_exact, n, large)


def _const_tables():
    half = 32
    inv = (np.float32(10000.0) ** (-(np.arange(half, dtype=np.float32) / np.float32(half)))).astype(np.float32)
    rope = np.zeros((128, 17, 64), np.float32)
    for tt in range(17):
        pos = (16 + 128 * tt + np.arange(128)) if tt < 16 else np.arange(128)
        ang = (pos.astype(np.float32)[:, None] * inv[None, :]).astype(np.float32)
        rope[:, tt, 0:32] = np.cos(ang.astype(np.float64))
        rope[:, tt, 32:64] = np.sin(ang.astype(np.float64))
    dvals = 767 - np.arange(NA)
    bk = _t5_bucket_np(dvals.astype(np.int32))
    oneh = np.zeros((32, NA), np.float32)
    oneh[bk, np.arange(NA)] = 1.0
    return rope.reshape(128, 17 * 64), oneh


_NC_CACHE = {}


def make_in_maps(inputs, n=8):
    f = lambda a: np.ascontiguousarray(np.asarray(a, dtype=np.float32))
    rope, oneh = _const_tables()
    gsm = np.concatenate([f(inputs["q_a_norm"])[0], f(inputs["kv_a_norm"])[0], f(inputs["mla_q_norm"])[0], f(inputs["mla_k_norm"])[0],
                          f(inputs["diff_q_norm"])[0], f(inputs["diff_k_norm"])[0], f(inputs["diff_lambda"])[0].reshape(-1)])
    shared = {
        "meta": f(inputs["meta_tokens"]),
        "w_in": f(inputs["w_in"])[0], "w_uq": f(inputs["w_uq"])[0], "w_ukv": f(inputs["w_ukv"])[0],
        "w_a": f(inputs["w_branch_a"])[0], "w_b": f(inputs["w_branch_b"])[0], "w_o": f(inputs["w_out"])[0],
        "gA": np.ascontiguousarray(np.broadcast_to(f(inputs["norm_in"])[0][None, :], (128, D))),
        "gsm": np.ascontiguousarray(np.broadcast_to(gsm[None, :], (128, G_END))),
        "gcol": np.ascontiguousarray(f(inputs["diff_subln"])[0].reshape(2, 128).T),
        "relb": f(inputs["rel_bias"]),
        "ropeT": rope, "onehot": oneh,
    }
    x = f(inputs["x"])
    return [dict(shared, x=x[i]) for i in range(n)]


def kernel(**inputs):
    if "nc" not in _NC_CACHE:
        _NC_CACHE["nc"] = build()
    nc = _NC_CACHE["nc"]
    in_maps = make_in_maps(inputs, 8)
    res = run_bass_kernel_spmd(nc, in_maps, core_ids=list(range(8)))
    return np.stack([np.asarray(r["out"], dtype=np.float32) for r in res.results], axis=0)
```

```python
import math
from contextlib import ExitStack

import numpy as np
import concourse.bass as bass
import concourse.mybir as mybir
from concourse.bass_utils import run_bass_kernel_spmd

F32 = mybir.dt.float32
BF16 = mybir.dt.bfloat16
AF = mybir.ActivationFunctionType
ALU = mybir.AluOpType
AX = mybir.AxisListType

D = 2048
SEQ = 2048
NMETA = 16
L = SEQ + NMETA
EPS = 1e-6
IN_W = 15680
OFF_CQ, OFF_CKV, OFF_KR, OFF_ZA, OFF_QD, OFF_KD, OFF_VD, OFF_ZB, OFF_GA, OFF_GB = (
    0, 768, 1280, 1344, 3392, 5440, 7488, 9536, 11584, 13632)
G_QA, G_KVA, G_MQ, G_MK, G_DQ, G_DK, G_LAM, G_END = 0, 768, 1280, 1472, 1664, 1792, 1920, 2432
NA = 1535
GW = 1408
XLO = -640
LAM_INIT = 0.8 - 0.6 * math.exp(-0.3 * 0)
STRICT = False
KDBG = 0


class _Op:
    __slots__ = ("eng", "fn", "waits", "signal", "sigval", "dma", "pos", "dcount")


class Sched:
    ENGS = ("pe", "act", "dve", "pool", "sp")
    MAP = {"pe": "tensor", "act": "scalar", "dve": "vector", "pool": "gpsimd", "sp": "sync"}

    def __init__(self, nc, es):
        self.nc, self.es = nc, es
        self.sem = {e: es.enter_context(nc.semaphore("sg_" + e)) for e in self.ENGS}
        self.sigcnt = {e: 0 for e in self.ENGS}
        self.npos = {e: 0 for e in self.ENGS}
        self.dsem, self.dcnt = {}, {}
        self.waited = {e: {} for e in self.ENGS}
        self._reset()

    def _reset(self):
        self.ops = {e: [] for e in self.ENGS}
        self.lw, self.rd = {}, {}

    def _add_wait(self, o, ref, kind):
        eng = o.eng
        if ref[0] == "c":
            _, e2, op2 = ref
            if e2 == eng and o.dma is None and (eng == "pe" or (kind != "raw" and not STRICT)):
                return
            if op2.pos <= self.waited[eng].get(e2, -1):
                return
            self.waited[eng][e2] = op2.pos
            op2.signal = True
            o.waits.append(ref)
        else:
            _, sk, cnt = ref
            if cnt <= self.waited[eng].get(("d", sk), 0):
                return
            self.waited[eng][("d", sk)] = cnt
            o.waits.append(ref)

    def op(self, eng, fn, reads=(), writes=(), dma=None, ndma=1):
        o = _Op()
        o.eng, o.fn, o.waits, o.signal, o.sigval, o.dma = eng, fn, [], False, None, dma
        o.pos = self.npos[eng]
        self.npos[eng] += 1
        for k in reads:
            for ref in self.lw.get(k, {}).values():
                self._add_wait(o, ref, "raw")
        for k in writes:
            for ref in self.lw.get(k, {}).values():
                self._add_wait(o, ref, "waw")
            for ref in self.rd.get(k, {}).values():
                self._add_wait(o, ref, "war")
        if dma is not None:
            if dma not in self.dsem:
                self.dsem[dma] = self.es.enter_context(self.nc.semaphore("sd%d" % len(self.dsem)))
                self.dcnt[dma] = 0
            self.dcnt[dma] += ndma
            me, mk = ("d", dma, self.dcnt[dma]), ("d", dma)
        else:
            me, mk = ("c", eng, o), eng
        for k in reads:
            self.rd.setdefault(k, {})[mk] = me
        for k in writes:
            self.lw[k] = {mk: me}
            self.rd[k] = {}
        self.ops[eng].append(o)
        return o

    def pe(self, fn, r=(), w=()):
        return self.op("pe", fn, r, w)

    def act(self, fn, r=(), w=()):
        return self.op("act", fn, r, w)

    def dve(self, fn, r=(), w=()):
        return self.op("dve", fn, r, w)

    def pool(self, fn, r=(), w=()):
        return self.op("pool", fn, r, w)

    def flush(self):
        lastc = {}
        for e in self.ENGS:
            for o in reversed(self.ops[e]):
                if o.dma is None:
                    lastc[e] = o
                    break
        for o in lastc.values():
            o.signal = True
        for e in self.ENGS:
            for o in self.ops[e]:
                if o.dma is None and o.signal:
                    self.sigcnt[e] += 1
                    o.sigval = self.sigcnt[e]
        dcnt = dict(self.dcnt)
        with self.nc.Block() as block:
            for e in self.ENGS:
                ops = self.ops[e]

                def body(eng, e=e, ops=ops):
                    for o in ops:
                        for ref in o.waits:
                            if ref[0] == "c":
                                eng.wait_ge(self.sem[ref[1]], ref[2].sigval)
                            else:
                                eng.wait_ge(self.dsem[ref[1]], 16 * ref[2])
                        r = o.fn(eng)
                        if o.dma is not None:
                            for ins in (r if isinstance(r, (list, tuple)) else [r]):
                                ins.then_inc(self.dsem[o.dma], 16)
                        elif o.signal:
                            r.then_inc(self.sem[e], 1)
                    for e2, o2 in lastc.items():
                        if e2 != e:
                            eng.wait_ge(self.sem[e2], o2.sigval)
                    for sk, c in dcnt.items():
                        eng.wait_ge(self.dsem[sk], 16 * c)

                getattr(block, self.MAP[e])(body)
        for e in self.ENGS:
            for e2, o2 in lastc.items():
                self.waited[e][e2] = max(self.waited[e].get(e2, -1), o2.pos)
            for sk, c in dcnt.items():
                self.waited[e][("d", sk)] = c
        self._reset()


def tok(tt):
    return (128, tt * 128) if tt < 16 else (NMETA, SEQ)


def build(dbg=False, stop=None, n_mla_groups=8, n_diff_heads=8):
    nc = bass.Bass("TRN2", target_bir_lowering=False)

    def din(name, shape, dtype=F32):
        return nc.dram_tensor(name, shape, dtype, kind="ExternalInput")

    x_d = din("x", [SEQ, D]).ap()
    meta_d = din("meta", [NMETA, D]).ap()
    w_in_d = din("w_in", [D, IN_W]).ap()
    w_uq_d = din("w_uq", [768, 3072]).ap()
    w_ukv_d = din("w_ukv", [512, 4096]).ap()
    w_a_d = din("w_a", [D, D]).ap()
    w_b_d = din("w_b", [D, D]).ap()
    w_o_d = din("w_o", [D, D]).ap()
    gA_d = din("gA", [128, D]).ap()
    gsm_d = din("gsm", [128, G_END]).ap()
    gcol_d = din("gcol", [128, 2]).ap()
    relb_d = din("relb", [32, 8]).ap()
    rope_d = din("ropeT", [128, 17 * 64]).ap()
    oneh_d = din("onehot", [32, NA]).ap()
    out_d = nc.dram_tensor("out", [SEQ, D], F32, kind="ExternalOutput").ap()
    skind = "ExternalOutput" if dbg else "Internal"
    oa_scr = nc.dram_tensor("oa_scr", [D, SEQ], BF16, kind=skind).ap()
    ob_scr = nc.dram_tensor("ob_scr", [D, SEQ], BF16, kind=skind).ap()
    tA_h = nc.dram_tensor("tA", [8, NA], F32, kind="Internal")
    tA_d = tA_h.ap()
    dbg_out = {}
    if dbg:
        dbg_out["d_uT"] = nc.dram_tensor("d_uT", [128, 16 * L], BF16, kind="ExternalOutput").ap()
        dbg_out["d_G"] = nc.dram_tensor("d_G", [128, GW], F32, kind="ExternalOutput").ap()

    w_in_v = w_in_d.rearrange("(k p) n -> p k n", p=128)
    w_uq_v = w_uq_d.rearrange("(k p) n -> p k n", p=128)
    w_ukv_v = w_ukv_d.rearrange("(k p) n -> p k n", p=128)

    with ExitStack() as es:
        S = Sched(nc, es)

        def sb(name, shape, dtype, stack=es):
            return stack.enter_context(nc.sbuf_tensor("s_" + name, shape, dtype))

        def ps(name, shape, dtype, stack):
            return stack.enter_context(nc.psum_tensor("p_" + name, shape, dtype))

        def dma(q, out, in_, r=(), w=(), sem=None):
            return S.op(q, lambda e: e.dma_start(out=out, in_=in_), r, w, dma=sem)

        def wload(dst, src, nk, r=(), w=(), sem=None, nsplit=2):
            step = max(1, nk // nsplit)
            parts = [(k0, min(nk, k0 + step)) for k0 in range(0, nk, step)]

            def fn(e):
                return [e.dma_start(out=dst[:, a:b, :], in_=src[:, a:b, :]) for a, b in parts]
            return S.op("pool", fn, r, w, dma=sem, ndma=len(parts))

        uT = sb("uT", [128, 16, L], BF16)
        ident = sb("ident", [128, 128], BF16)
        identf = sb("identF", [128, 128], F32)
        ones = sb("ones", [128, 128], BF16)
        gsm = sb("gsm", [128, G_END], F32)
        gqp = sb("gqp", [128, 192], F32)
        gsub = sb("gsub", [128, 2], F32)
        ropeT = sb("ropeT", [128, 17, 64], F32)
        neglam = sb("neglam", [128, 1], F32)

        def ukeys(c0, n):
            return [("uT", t) for t in range(c0 // 128, (c0 + n - 1) // 128 + 1)]

        with ExitStack() as sAC:
            cqnT = sb("cqnT", [128, 6, SEQ], BF16, sAC)
            ckvnT = sb("ckvnT", [128, 4, L], BF16, sAC)
            krT = sb("krT", [64, L], BF16, sAC)
            sskr = sb("sskr", [128, 17], F32, sAC)

            with ExitStack() as sAB:
                gA = sb("gA", [128, D], F32, sAB)
                relb = sb("relb", [32, 8], F32, sAB)
                oneh = sb("oneh", [32, NA], F32, sAB)
                tsb = sb("tsb", [8, NA], F32, sAB)
                lamt = sb("lamt", [128, 256], F32, sAB)
                lams = sb("lams", [128, 4], F32, sAB)
                xt = [sb("xt%d" % i, [128, D], F32, sAB) for i in range(2)]
                ub = [sb("ub%d" % i, [128, D], BF16, sAB) for i in range(2)]
                junk = sb("junk", [128, D], BF16, sAB)
                ssx = sb("ssx", [128, 17], F32, sAB)
                lnx = sb("lnx", [128, 17], F32, sAB)
                rsx = sb("rsx", [128, 17], F32, sAB)
                B = [ps("pB%d" % i, [128, 512], F32, sAB) for i in range(4)]
                T = [ps("pT%d" % i, [128, 1024], BF16, sAB) for i in range(2)]

                dma("sp", gsm[:, :], gsm_d, w=["gsm"], sem="k1")
                dma("sp", gsub[:, :], gcol_d, w=["gsub"], sem="k2")
                dma("sp", ropeT[:, :, :], rope_d.rearrange("p (t f) -> p t f", f=64), w=["rope"], sem="k3")
                dma("sp", relb[:, :], relb_d, w=["relb"], sem="k4")
                dma("sp", oneh[:, :], oneh_d, w=["oneh"], sem="k5")
                dma("sp", gA[:, :], gA_d, w=["gA"], sem="k6")
                S.pool(lambda e: e.memset(identf[:, :], 0.0), w=["identf"])
                S.pool(lambda e: e.affine_select(out=identf[:, :], in_=identf[:, :], pattern=[[-1, 128]],
                                                 compare_op=ALU.not_equal, fill=1.0, base=0, channel_multiplier=1),
                       r=["identf"], w=["identf"])
                S.dve(lambda e: e.tensor_copy(out=ident[:, :], in_=identf[:, :]), r=["identf"], w=["ident"])
                S.dve(lambda e: e.memset(ones[:, :], 1.0), w=["ones"])
                S.dve(lambda e: e.tensor_scalar_mul(out=gA[:, :], in0=gA[:, :], scalar1=math.sqrt(D)), r=["gA"], w=["gA"])
                S.dve(lambda e: e.tensor_scalar_mul(out=gsm[:, G_QA:G_KVA], in0=gsm[:, G_QA:G_KVA], scalar1=math.sqrt(768.0)),
                      r=["gsm"], w=["gsm"])
                S.dve(lambda e: e.tensor_scalar_mul(out=gsm[:, G_KVA:G_MQ], in0=gsm[:, G_KVA:G_MQ], scalar1=math.sqrt(512.0)),
                      r=["gsm"], w=["gsm"])
                S.dve(lambda e: e.scalar_tensor_tensor(out=gqp[:, 0:128], in0=gsm[:, G_MQ:G_MQ + 128], scalar=math.sqrt(192.0),
                                                       in1=gsm[:, G_MK:G_MK + 128], op0=ALU.mult, op1=ALU.mult),
                      r=["gsm"], w=["gqp"])
                S.dve(lambda e: e.tensor_scalar_mul(out=gqp[:, 128:192], in0=gsm[:, G_MQ + 128:G_MQ + 192], scalar1=math.sqrt(192.0)),
                      r=["gsm"], w=["gqp"])
                S.dve(lambda e: e.scalar_tensor_tensor(out=gsm[:, G_DQ:G_DK], in0=gsm[:, G_DQ:G_DK], scalar=math.sqrt(128.0),
                                                       in1=gsm[:, G_DK:G_LAM], op0=ALU.mult, op1=ALU.mult),
                      r=["gsm"], w=["gsm"])
                S.dve(lambda e: e.tensor_scalar_mul(out=gsub[:, :], in0=gsub[:, :], scalar1=(1.0 - LAM_INIT) * 16.0),
                      r=["gsub"], w=["gsub"])
                S.dve(lambda e: e.tensor_tensor(out=lamt[:, 0:128], in0=gsm[:, G_LAM:G_LAM + 128], in1=gsm[:, G_LAM + 128:G_LAM + 256],
                                                op=ALU.mult), r=["gsm"], w=["lamt"])
                S.dve(lambda e: e.tensor_tensor(out=lamt[:, 128:256], in0=gsm[:, G_LAM + 256:G_LAM + 384], in1=gsm[:, G_LAM + 384:G_LAM + 512],
                                                op=ALU.mult), r=["gsm"], w=["lamt"])
                S.dve(lambda e: e.tensor_reduce(out=lams[:, 0:2], in_=lamt[:, :].rearrange("p (a f) -> p a f", a=2), axis=AX.X, op=ALU.add),
                      r=["lamt"], w=["lams"])
                S.act(lambda e: e.activation(out=lams[:, 2:4], in_=lams[:, 0:2], func=AF.Exp), r=["lams"], w=["lams2"])
                S.dve(lambda e: e.tensor_scalar(out=neglam[:, :], in0=lams[:, 3:4], scalar1=lams[:, 2:3], scalar2=-LAM_INIT,
                                                op0=ALU.subtract, op1=ALU.add), r=["lams2"], w=["neglam"])
                for j, (a0, a1) in enumerate(((0, 512), (512, 1024), (1024, NA))):
                    S.pe(lambda e, a0=a0, a1=a1, j=j: e.matmul(B[j][0:8, 0:a1 - a0], lhsT=relb[:, :], rhs=oneh[:, a0:a1], start=True, stop=True),
                         r=["relb", "oneh"], w=["B%d" % j])
                    S.dve(lambda e, a0=a0, a1=a1, j=j: e.tensor_copy(out=tsb[:, a0:a1], in_=B[j][0:8, 0:a1 - a0]), r=["B%d" % j], w=["tsb"])
                dma("sp", tA_d, tsb[:, :], r=["tsb"], w=["tA"], sem="c1")

                for tt in range(17):
                    P, c0 = tok(tt)
                    b = tt % 2
                    src = x_d[tt * 128:(tt + 1) * 128, :] if tt < 16 else meta_d
                    dma("sp", xt[b][:P, :], src, w=[("xt", b)], sem="xt%d" % b)
                    S.act(lambda e, P=P, b=b, tt=tt: e.activation(out=junk[:P, :], in_=xt[b][:P, :], func=AF.Square, accum_out=ssx[:P, tt:tt + 1]),
                          r=[("xt", b)], w=["junk", ("ssx", tt)])
                    S.act(lambda e, P=P, tt=tt: e.activation(out=lnx[:P, tt:tt + 1], in_=ssx[:P, tt:tt + 1], func=AF.Ln, bias=D * EPS, scale=1.0),
                          r=[("ssx", tt)], w=[("lnx", tt)])
                    S.act(lambda e, P=P, tt=tt: e.activation(out=rsx[:P, tt:tt + 1], in_=lnx[:P, tt:tt + 1], func=AF.Exp, scale=-0.5),
                          r=[("lnx", tt)], w=[("rsx", tt)])
                    S.dve(lambda e, P=P, b=b, tt=tt: e.scalar_tensor_tensor(out=ub[b][:P, :], in0=xt[b][:P, :], scalar=rsx[:P, tt:tt + 1],
                                                                          in1=gA[:P, :], op0=ALU.mult, op1=ALU.mult),
                          r=[("xt", b), ("rsx", tt), "gA"], w=[("ub", b)])
                    for half in range(2):
                        for k in range(8):
                            kk = half * 8 + k
                            S.pe(lambda e, P=P, b=b, k=k, kk=kk, half=half: e.transpose(
                                out=T[half][:, k * 128:k * 128 + P], in_=ub[b][:P, kk * 128:(kk + 1) * 128], identity=ident[:P, :P]),
                                r=[("ub", b), "ident"], w=[("T", half)])
                        cp = (lambda e, P=P, c0=c0, half=half: e.tensor_copy(
                            out=uT[:, half * 8:(half + 1) * 8, c0:c0 + P],
                            in_=T[half][:, :].rearrange("p (k t) -> p k t", k=8)[:, :, 0:P]))
                        ca = (lambda e, P=P, c0=c0, half=half: e.activation(
                            out=uT[:, half * 8:(half + 1) * 8, c0:c0 + P],
                            in_=T[half][:, :].rearrange("p (k t) -> p k t", k=8)[:, :, 0:P], func=AF.Copy))
                        if half == 0:
                            S.dve(cp, r=[("T", half)], w=[("uT", tt)])
                        else:
                            S.act(ca, r=[("T", half)], w=[("uT", tt)])

                if dbg:
                    dma("sp", dbg_out["d_uT"], uT[:, :, :].rearrange("p k t -> p (k t)"), r=[("uT", t) for t in range(17)], w=["dbg"], sem="dbg")
                S.flush()
            if stop == "A":
                return nc
            with ExitStack() as sAB:
                wB = sb("wB", [128, 16, 1344], BF16, sAB)
                cB = [sb("cB%d" % i, [128, 1344], F32, sAB) for i in range(2)]
                junk = sb("junkB", [128, 1344], BF16, sAB)
                ss3 = sb("ss3", [128, 17, 4], F32, sAB)
                ln3 = sb("ln3", [128, 17, 2], F32, sAB)
                rs3 = sb("rs3", [128, 17, 2], F32, sAB)
                cqb = [sb("cqb%d" % i, [128, 768], BF16, sAB) for i in range(2)]
                ckvb = [sb("ckvb%d" % i, [128, 512], BF16, sAB) for i in range(2)]
                krg = sb("krg", [128, 64], F32, sAB)
                rt = sb("rt", [128, 4, 32], F32, sAB)
                krb = [sb("krb%d" % i, [128, 64], BF16, sAB) for i in range(2)]
                B = [ps("bB%d" % i, [128, 512], F32, sAB) for i in range(4)]
                T = [ps("bT%d" % i, [128, 1024], BF16, sAB) for i in range(2)]
                wload(wB, w_in_v[:, :, 0:1344], 16, w=["wB"], sem="wB", nsplit=4)
                groups = ((0, 384), (384, 768), (768, 1280), (1280, 1344))
                deferred = []
                for tt in range(17):
                    P, c0 = tok(tt)
                    b = tt % 2
                    for g, (a0, a1) in enumerate(groups):
                        if tt == 16 and g < 2:
                            continue
                        for k in range(16):
                            S.pe(lambda e, P=P, c0=c0, g=g, a0=a0, a1=a1, k=k: e.matmul(
                                B[g][:P, 0:a1 - a0], lhsT=uT[:, k, c0:c0 + P], rhs=wB[:, k, a0:a1], start=(k == 0), stop=(k == 15)),
                                r=[("uT", tt), "wB"], w=["B%d" % g])
                        S.act(lambda e, P=P, g=g, a0=a0, a1=a1, b=b: e.activation(out=cB[b][:P, a0:a1], in_=B[g][:P, 0:a1 - a0], func=AF.Copy),
                              r=["B%d" % g], w=[("cB", b, g)])
                    for f in deferred:
                        f()
                    deferred = []
                    if tt < 16:
                        S.act(lambda e, P=P, b=b, tt=tt: e.activation(out=junk[:P, 0:768], in_=cB[b][:P, 0:768], func=AF.Square,
                                                                     accum_out=ss3[:P, tt, 0:1]),
                              r=[("cB", b, 0), ("cB", b, 1)], w=[("junk", 0), ("ss3", tt, 0)])
                        S.act(lambda e, P=P, tt=tt: e.activation(out=ln3[:P, tt, 0:1], in_=ss3[:P, tt, 0:1], func=AF.Ln, bias=768 * EPS, scale=1.0),
                              r=[("ss3", tt, 0)], w=[("ln3", tt, 0)])
                        S.act(lambda e, P=P, tt=tt: e.activation(out=rs3[:P, tt, 0:1], in_=ln3[:P, tt, 0:1], func=AF.Exp, scale=-0.5),
                              r=[("ln3", tt, 0)], w=[("rs3", tt, 0)])
                    S.act(lambda e, P=P, b=b, tt=tt: e.activation(out=junk[:P, 768:1280], in_=cB[b][:P, 768:1280], func=AF.Square,
                                                                 accum_out=ss3[:P, tt, 1:2]),
                          r=[("cB", b, 2)], w=[("junk", 1), ("ss3", tt, 1)])
                    S.act(lambda e, P=P, b=b, tt=tt: e.activation(out=junk[:P, 1280:1344], in_=cB[b][:P, 1280:1344], func=AF.Square,
                                                                 accum_out=ss3[:P, tt, 2:3]),
                          r=[("cB", b, 3)], w=[("junk", 2), ("ss3", tt, 2)])
                    S.act(lambda e, P=P, tt=tt: e.activation(out=ln3[:P, tt, 1:2], in_=ss3[:P, tt, 1:2], func=AF.Ln, bias=512 * EPS, scale=1.0),
                          r=[("ss3", tt, 1)], w=[("ln3", tt, 1)])
                    S.act(lambda e, P=P, tt=tt: e.activation(out=rs3[:P, tt, 1:2], in_=ln3[:P, tt, 1:2], func=AF.Exp, scale=-0.5),
                          r=[("ln3", tt, 1)], w=[("rs3", tt, 1)])
                    S.dve(lambda e, P=P, tt=tt: e.tensor_scalar_add(out=sskr[:P, tt:tt + 1], in0=ss3[:P, tt, 2:3], scalar1=192 * EPS),
                          r=[("ss3", tt, 2)], w=[("sskr", tt)])
                    if tt < 16:
                        S.dve(lambda e, P=P, b=b, tt=tt: e.scalar_tensor_tensor(out=cqb[b][:P, :], in0=cB[b][:P, 0:768], scalar=rs3[:P, tt, 0:1],
                                                                              in1=gsm[:P, G_QA:G_KVA], op0=ALU.mult, op1=ALU.mult),
                              r=[("cB", b, 0), ("cB", b, 1), ("rs3", tt, 0), "gsm"], w=[("cqb", b)])
                    S.dve(lambda e, P=P, b=b, tt=tt: e.scalar_tensor_tensor(out=ckvb[b][:P, :], in0=cB[b][:P, 768:1280], scalar=rs3[:P, tt, 1:2],
                                                                          in1=gsm[:P, G_KVA:G_MQ], op0=ALU.mult, op1=ALU.mult),
                          r=[("cB", b, 2), ("rs3", tt, 1), "gsm"], w=[("ckvb", b)])
                    S.dve(lambda e, P=P, b=b: e.tensor_tensor(out=krg[:P, :], in0=cB[b][:P, 1280:1344], in1=gsm[:P, G_MK + 128:G_MK + 192], op=ALU.mult),
                          r=[("cB", b, 3), "gsm"], w=["krg"])
                    cosv = lambda P=P, tt=tt: ropeT[:P, tt, 0:32]
                    sinv = lambda P=P, tt=tt: ropeT[:P, tt, 32:64]
                    S.dve(lambda e, P=P, tt=tt: e.tensor_tensor(out=rt[:P, 0, :], in0=krg[:P, 0:32], in1=ropeT[:P, tt, 0:32], op=ALU.mult),
                          r=["krg", "rope"], w=[("rt", 0)])
                    S.dve(lambda e, P=P, tt=tt: e.tensor_tensor(out=rt[:P, 1, :], in0=krg[:P, 32:64], in1=ropeT[:P, tt, 32:64], op=ALU.mult),
                          r=["krg", "rope"], w=[("rt", 1)])
                    S.dve(lambda e, P=P, tt=tt: e.tensor_tensor(out=rt[:P, 2, :], in0=krg[:P, 32:64], in1=ropeT[:P, tt, 0:32], op=ALU.mult),
                          r=["krg", "rope"], w=[("rt", 2)])
                    S.dve(lambda e, P=P, tt=tt: e.tensor_tensor(out=rt[:P, 3, :], in0=krg[:P, 0:32], in1=ropeT[:P, tt, 32:64], op=ALU.mult),
                          r=["krg", "rope"], w=[("rt", 3)])
                    S.dve(lambda e, P=P, b=b: e.tensor_tensor(out=krb[b][:P, 0:32], in0=rt[:P, 0, :], in1=rt[:P, 1, :], op=ALU.subtract),
                          r=[("rt", 0), ("rt", 1)], w=[("krb", b)])
                    S.dve(lambda e, P=P, b=b: e.tensor_tensor(out=krb[b][:P, 32:64], in0=rt[:P, 2, :], in1=rt[:P, 3, :], op=ALU.add),
                          r=[("rt", 2), ("rt", 3)], w=[("krb", b)])
                    def later(tt=tt, P=P, c0=c0, b=b):
                        if tt < 16:
                            for k in range(6):
                                S.pe(lambda e, P=P, b=b, k=k: e.transpose(out=T[0][:, k * 128:k * 128 + P], in_=cqb[b][:P, k * 128:(k + 1) * 128],
                                                                          identity=ident[:P, :P]), r=[("cqb", b), "ident"], w=[("T", 0)])
                            S.dve(lambda e, P=P, c0=c0: e.tensor_copy(out=cqnT[:, 0:6, c0:c0 + P],
                                                                      in_=T[0][:, 0:768].rearrange("p (k t) -> p k t", k=6)[:, :, 0:P]),
                                  r=[("T", 0)], w=[("cqnT", tt)])
                        for k in range(4):
                            S.pe(lambda e, P=P, b=b, k=k: e.transpose(out=T[1][:, k * 128:k * 128 + P], in_=ckvb[b][:P, k * 128:(k + 1) * 128],
                                                                      identity=ident[:P, :P]), r=[("ckvb", b), "ident"], w=[("T", 1)])
                        S.pe(lambda e, P=P, b=b: e.transpose(out=T[1][0:64, 512:512 + P], in_=krb[b][:P, 0:64], identity=ident[:P, :P]),
                             r=[("krb", b), "ident"], w=[("T", 1)])
                        S.act(lambda e, P=P, c0=c0: e.activation(out=ckvnT[:, 0:4, c0:c0 + P],
                                                                 in_=T[1][:, 0:512].rearrange("p (k t) -> p k t", k=4)[:, :, 0:P], func=AF.Copy),
                              r=[("T", 1)], w=[("ckvnT", tt)])
                        S.act(lambda e, P=P, c0=c0: e.activation(out=krT[0:64, c0:c0 + P], in_=T[1][0:64, 512:512 + P], func=AF.Copy),
                              r=[("T", 1)], w=[("krT", tt)])
                    deferred.append(later)
                for f in deferred:
                    f()
                S.flush()
            if stop == "B":
                return nc

            with ExitStack() as sC:
                wq = [sb("wq%d" % i, [128, 6, 384], BF16, sC) for i in range(2)]
                wkv = [sb("wkv%d" % i, [128, 4, 512], BF16, sC) for i in range(2)]
                wz = [sb("wz%d" % i, [128, 16, 128], BF16, sC) for i in range(2)]
                qTn = sb("qTn", [128, 2, SEQ], BF16, sC)
                qTr = sb("qTr", [64, 2, SEQ], BF16, sC)
                kTn = sb("kTn", [128, 2, L], BF16, sC)
                Vt = sb("Vt", [128, 17, 2, 128], BF16, sC)
                rstdk = sb("rstdk", [128, 17, 2], F32, sC)
                junkq = sb("junkq", [128, 4, 192], BF16, sC)
                ssq = sb("ssq", [128, 17, 4], F32, sC)
                lnq = sb("lnq", [128, 17, 4], F32, sC)
                rq = sb("rq", [128, 17, 2], F32, sC)
                qb_ = [sb("qb%d" % i, [128, 2, 192], BF16, sC) for i in range(2)]
                qr = [sb("qr%d" % i, [128, 2, 64], F32, sC) for i in range(2)]
                rt4 = sb("rt4", [128, 4, 2, 32], F32, sC)
                NPT = 4
                PT = [sb("PT%d" % i, [128, 512], BF16, sC) for i in range(NPT)]
                ez = sb("ez", [128, 512], F32, sC)
                sz = sb("sz", [128, 512], F32, sC)
                rec = sb("rec", [128, 512], F32, sC)
                og = [sb("og%d" % i, [128, 512], BF16, sC) for i in range(2)]
                B = [ps("qB%d" % i, [128, 512], F32, sC) for i in range(6)]
                Tv = [ps("qT%d" % i, [128, 1024], BF16, sC) for i in range(2)]

                def load_group(g):
                    s = g % 2
                    h0 = 2 * g
                    wload(wq[s], w_uq_v[:, :, h0 * 192:(h0 + 2) * 192], 6, w=[("wq", s)], sem="wq%d" % s)
                    wload(wkv[s], w_ukv_v[:, :, h0 * 256:(h0 + 2) * 256], 4, w=[("wkv", s)], sem="wkv%d" % s)

                load_group(0)
                og_i = 0
                for g in range(n_mla_groups):
                    s = g % 2
                    h0 = 2 * g
                    if g + 1 < n_mla_groups:
                        load_group(g + 1)
                    for hh in range(2):
                        if not (KDBG & 8):
                            wload(wz[hh], w_in_v[:, :, OFF_ZA + (h0 + hh) * 128:OFF_ZA + (h0 + hh + 1) * 128], 16, w=[("wz", hh)], sem="wz%d" % hh)
                    deferred = []
                    for tt in range(17):
                        P, c0 = tok(tt)
                        b = tt % 2
                        qbk, kvbk = tt % 2, 2 + tt % 2
                        if tt < 16:
                            for k in range(6):
                                S.pe(lambda e, P=P, c0=c0, k=k, s=s, qbk=qbk: e.matmul(B[qbk][:P, 0:384], lhsT=cqnT[:, k, c0:c0 + P], rhs=wq[s][:, k, :],
                                                                                          start=(k == 0), stop=(k == 5)),
                                     r=[("wq", s)], w=["B%d" % qbk])
                        for k in range(4):
                            S.pe(lambda e, P=P, c0=c0, k=k, s=s, kvbk=kvbk: e.matmul(B[kvbk][:P, 0:512], lhsT=ckvnT[:, k, c0:c0 + P], rhs=wkv[s][:, k, :],
                                                                                        start=(k == 0), stop=(k == 3)),
                                 r=[("wkv", s)], w=["B%d" % kvbk])
                        for f in deferred:
                            f()
                        deferred = []
                        if tt < 16:
                            for hh in range(2):
                                S.act(lambda e, P=P, hh=hh, tt=tt, qbk=qbk: e.activation(out=junkq[:P, hh, :], in_=B[qbk][:P, hh * 192:(hh + 1) * 192], func=AF.Square,
                                                                                         accum_out=ssq[:P, tt, hh:hh + 1]),
                                      r=["B%d" % qbk], w=[("junkq", hh), ("ssq", tt, hh)])
                            S.act(lambda e, P=P, tt=tt: e.activation(out=lnq[:P, tt, 0:2], in_=ssq[:P, tt, 0:2], func=AF.Ln, bias=192 * EPS, scale=1.0),
                                  r=[("ssq", tt, 0), ("ssq", tt, 1)], w=[("lnq", tt)])
                            S.act(lambda e, P=P, tt=tt: e.activation(out=rq[:P, tt, 0:2], in_=lnq[:P, tt, 0:2], func=AF.Exp, scale=-0.5),
                                  r=[("lnq", tt)], w=[("rq", tt)])
                            for hh in range(2):
                                S.dve(lambda e, P=P, hh=hh, tt=tt, b=b, qbk=qbk: e.scalar_tensor_tensor(
                                    out=qb_[b][:P, hh, 0:128], in0=B[qbk][:P, hh * 192:hh * 192 + 128], scalar=rq[:P, tt, hh:hh + 1],
                                    in1=gqp[:P, 0:128], op0=ALU.mult, op1=ALU.mult),
                                    r=["B%d" % qbk, ("rq", tt), "gqp"], w=[("qb", b)])
                                S.dve(lambda e, P=P, hh=hh, tt=tt, b=b, qbk=qbk: e.scalar_tensor_tensor(
                                    out=qr[b][:P, hh, :], in0=B[qbk][:P, hh * 192 + 128:hh * 192 + 192], scalar=rq[:P, tt, hh:hh + 1],
                                    in1=gqp[:P, 128:192], op0=ALU.mult, op1=ALU.mult),
                                    r=["B%d" % qbk, ("rq", tt), "gqp"], w=[("qr", b)])
                            cb = lambda P, tt: ropeT[:P, tt, 0:32].unsqueeze(1).to_broadcast([P, 2, 32])
                            sn = lambda P, tt: ropeT[:P, tt, 32:64].unsqueeze(1).to_broadcast([P, 2, 32])
                            S.dve(lambda e, P=P, tt=tt, b=b: e.tensor_tensor(out=rt4[:P, 0, :, :], in0=qr[b][:P, :, 0:32], in1=cb(P, tt), op=ALU.mult),
                                  r=[("qr", b)], w=[("rt4", 0)])
                            S.dve(lambda e, P=P, tt=tt, b=b: e.tensor_tensor(out=rt4[:P, 1, :, :], in0=qr[b][:P, :, 32:64], in1=sn(P, tt), op=ALU.mult),
                                  r=[("qr", b)], w=[("rt4", 1)])
                            S.dve(lambda e, P=P, tt=tt, b=b: e.tensor_tensor(out=rt4[:P, 2, :, :], in0=qr[b][:P, :, 32:64], in1=cb(P, tt), op=ALU.mult),
                                  r=[("qr", b)], w=[("rt4", 2)])
                            S.dve(lambda e, P=P, tt=tt, b=b: e.tensor_tensor(out=rt4[:P, 3, :, :], in0=qr[b][:P, :, 0:32], in1=sn(P, tt), op=ALU.mult),
                                  r=[("qr", b)], w=[("rt4", 3)])
                            S.dve(lambda e, P=P, b=b: e.tensor_tensor(out=qb_[b][:P, :, 128:160], in0=rt4[:P, 0, :, :], in1=rt4[:P, 1, :, :], op=ALU.subtract),
                                  r=[("rt4", 0), ("rt4", 1)], w=[("qb", b)])
                            S.dve(lambda e, P=P, b=b: e.tensor_tensor(out=qb_[b][:P, :, 160:192], in0=rt4[:P, 2, :, :], in1=rt4[:P, 3, :, :], op=ALU.add),
                                  r=[("rt4", 2), ("rt4", 3)], w=[("qb", b)])

                            def later(tt=tt, P=P, c0=c0, b=b):
                                for hh in range(2):
                                    S.pe(lambda e, hh=hh: e.transpose(out=Tv[b][:, hh * 128:hh * 128 + P], in_=qb_[b][:P, hh, 0:128], identity=ident[:P, :P]),
                                         r=[("qb", b)], w=["B%d" % (6 + b)])
                                    S.pe(lambda e, hh=hh: e.transpose(out=Tv[b][0:64, (2 + hh) * 128:(2 + hh) * 128 + P], in_=qb_[b][:P, hh, 128:192],
                                                                      identity=ident[:P, :P]), r=[("qb", b)], w=["B%d" % (6 + b)])
                                S.dve(lambda e: e.tensor_copy(out=qTn[:, :, c0:c0 + P],
                                                              in_=Tv[b][:, 0:256].rearrange("p (h t) -> p h t", h=2)[:, :, 0:P]),
                                      r=["B%d" % (6 + b)], w=[("qTn", tt)])
                                if True:
                                    S.dve(lambda e: e.tensor_copy(out=qTr[0:64, :, c0:c0 + P],
                                                                  in_=Tv[b][0:64, 256:512].rearrange("p (h t) -> p h t", h=2)[:, :, 0:P]),
                                          r=["B%d" % (6 + b)], w=[("qTr", tt)])
                                else:
                                    S.act(lambda e: e.activation(out=qTr[0:64, :, c0:c0 + P],
                                                                 in_=Tv[b][0:64, 256:512].rearrange("p (h t) -> p h t", h=2)[:, :, 0:P], func=AF.Copy),
                                          r=["B%d" % (6 + b)], w=[("qTr", tt)])
                            if KDBG & 2:
                                later()
                            else:
                                deferred.append(later)
                        S.act(lambda e, P=P, tt=tt, kvbk=kvbk: e.activation(out=Vt[:P, tt, :, :], in_=B[kvbk][:P, :].rearrange("p (h f) -> p h f", h=2)[:, :, 128:256],
                                                                            func=AF.Copy), r=["B%d" % kvbk], w=[("Vt", tt)])
                        for hh in range(2):
                            S.act(lambda e, P=P, hh=hh, tt=tt, kvbk=kvbk: e.activation(out=junkq[:P, 2 + hh, 0:128], in_=B[kvbk][:P, hh * 256:hh * 256 + 128], func=AF.Square,
                                                                                       accum_out=ssq[:P, tt, 2 + hh:3 + hh]),
                                  r=["B%d" % kvbk], w=[("junkq", 2 + hh), ("ssk", tt, hh)])
                        S.act(lambda e, P=P, tt=tt: e.activation(out=lnq[:P, tt, 2:4], in_=ssq[:P, tt, 2:4], func=AF.Ln, bias=sskr[:P, tt:tt + 1], scale=1.0),
                              r=[("ssk", tt, 0), ("ssk", tt, 1)], w=[("lnk", tt)])
                        S.act(lambda e, P=P, tt=tt: e.activation(out=rstdk[:P, tt, 0:2], in_=lnq[:P, tt, 2:4], func=AF.Exp, scale=-0.5),
                              r=[("lnk", tt)], w=[("rstdk", tt)])
                    blocks = [(0, 512), (512, 512), (1024, 512), (1536, 512), (2048, 16)]
                    bi = 0
                    for (c0, n) in blocks:
                        for hh in range(2):
                            bank = 4 + (bi % 2)
                            for k in range(4):
                                S.pe(lambda e, c0=c0, n=n, hh=hh, k=k, s=s, bank=bank: e.matmul(
                                    B[bank][:, 0:n], lhsT=wkv[s][:, k, hh * 256:hh * 256 + 128], rhs=ckvnT[:, k, c0:c0 + n], start=(k == 0), stop=(k == 3)),
                                    r=[("wkv", s)], w=["B%d" % bank])
                            if bi % 2 == 0:
                                S.dve(lambda e, c0=c0, n=n, hh=hh, bank=bank: e.tensor_copy(out=kTn[:, hh, c0:c0 + n], in_=B[bank][:, 0:n]),
                                      r=["B%d" % bank], w=[("kTn", hh, c0)])
                            else:
                                S.act(lambda e, c0=c0, n=n, hh=hh, bank=bank: e.activation(out=kTn[:, hh, c0:c0 + n], in_=B[bank][:, 0:n], func=AF.Copy),
                                      r=["B%d" % bank], w=[("kTn", hh, c0)])
                            bi += 1
                    for f in deferred:
                        f()
                    deferred = []
                    for hh in range(2):
                        h = h0 + hh
                        if KDBG & 8:
                            wload(wz[hh], w_in_v[:, :, OFF_ZA + h * 128:OFF_ZA + (h + 1) * 128], 16, w=[("wz", hh)], sem="wz%d" % hh)
                        for qb in range(4):
                            q0 = qb * 512
                            rq_keys = [("qTn", t) for t in range(4 * qb, 4 * qb + 4)] + [("qTr", t) for t in range(4 * qb, 4 * qb + 4)]
                            for k in range(16):
                                S.pe(lambda e, k=k, hh=hh, q0=q0: e.matmul(B[5][:, :], lhsT=wz[hh][:, k, :], rhs=uT[:, k, q0:q0 + 512],
                                                                           start=(k == 0), stop=(k == 15)),
                                     r=[("wz", hh)], w=["B5"])
                            S.act(lambda e: e.activation(out=ez[:, :], in_=B[5][:, :], func=AF.Exp, scale=-1.0), r=["B5"], w=["ez"])
                            if KDBG & 1:
                                S.dve(lambda e: e.tensor_scalar_add(out=ez[:, :], in0=ez[:, :], scalar1=1.0), r=["ez"], w=["ez"])
                                S.dve(lambda e: e.reciprocal(out=ez[:, :], in_=ez[:, :]), r=["ez"], w=["ez"])
                            else:
                                S.act(lambda e: e.activation(out=ez[:, :], in_=ez[:, :], func=AF.Ln, bias=1.0, scale=1.0), r=["ez"], w=["ez"])
                                S.act(lambda e: e.activation(out=ez[:, :], in_=ez[:, :], func=AF.Exp, scale=-1.0), r=["ez"], w=["ez"])
                            S.dve(lambda e: e.tensor_tensor(out=sz[:, :], in0=B[5][:, :], in1=ez[:, :], op=ALU.mult), r=["B5", "ez"], w=["sz"])

                            def S_mm(c, hh=hh, q0=q0, rq_keys=rq_keys):
                                P, c0 = tok(c)
                                bank = c % 3
                                S.pe(lambda e: e.matmul(B[bank][:P, :], lhsT=kTn[:, hh, c0:c0 + P], rhs=qTn[:, hh, q0:q0 + 512], start=True, stop=False),
                                     r=rq_keys + [("kTn", hh, (c0 // 512) * 512)], w=["B%d" % bank])
                                S.pe(lambda e: e.matmul(B[bank][:P, :], lhsT=krT[0:64, c0:c0 + P], rhs=qTr[0:64, hh, q0:q0 + 512], start=False, stop=True),
                                     r=rq_keys, w=["B%d" % bank])

                            def EXPc(c, hh=hh):
                                P, c0 = tok(c)
                                bank = c % 3
                                S.act(lambda e: e.activation(out=PT[c % NPT][:P, :], in_=B[bank][:P, :], func=AF.Exp, scale=rstdk[:P, c, hh:hh + 1]),
                                      r=["B%d" % bank, ("rstdk", c)], w=[("PT", c % NPT)])

                            def PVc(c, hh=hh):
                                P, c0 = tok(c)
                                S.pe(lambda e: e.matmul(B[3][:, :], lhsT=Vt[:P, c, hh, :], rhs=PT[c % NPT][:P, :], start=(c == 0), stop=(c == 16)),
                                     r=[("PT", c % NPT), ("Vt", c)], w=["B3"])
                                S.pe(lambda e: e.matmul(B[4][:, :], lhsT=ones[:P, :], rhs=PT[c % NPT][:P, :], start=(c == 0), stop=(c == 16)),
                                     r=[("PT", c % NPT)], w=["B4"])

                            S_mm(0)
                            S_mm(1)
                            for c in range(17):
                                EXPc(c)
                                if c + 2 < 17:
                                    S_mm(c + 2)
                                PVc(c)
                            oi = og_i % 2
                            og_i += 1
                            if KDBG & 1:
                                S.dve(lambda e: e.reciprocal(out=rec[:, :], in_=B[4][:, :]), r=["B4"], w=["rec"])
                            else:
                                S.act(lambda e: e.activation(out=rec[:, :], in_=B[4][:, :], func=AF.Ln), r=["B4"], w=["rec"])
                                S.act(lambda e: e.activation(out=rec[:, :], in_=rec[:, :], func=AF.Exp, scale=-1.0), r=["rec"], w=["rec"])
                            S.dve(lambda e: e.tensor_tensor(out=rec[:, :], in0=rec[:, :], in1=sz[:, :], op=ALU.mult), r=["rec", "sz"], w=["rec"])
                            S.dve(lambda e, oi=oi: e.tensor_tensor(out=og[oi][:, :], in0=B[3][:, :], in1=rec[:, :], op=ALU.mult), r=["B3", "rec"], w=[("og", oi)])
                            dma("sp", oa_scr[h * 128:(h + 1) * 128, q0:q0 + 512], og[oi][:, :], r=[("og", oi)], w=[("oa", h, qb)], sem="og%d" % oi)
                S.flush()
        if stop == "C":
            return nc

        with ExitStack() as sD:
            wr = [sb("wr%d" % i, [128, 16, 256], BF16, sD) for i in range(4)]
            Jf = sb("Jf", [128, 128], F32, sD)
            qdT = sb("qdT", [128, 2, SEQ], BF16, sD)
            kdT = sb("kdT", [128, 2, L], BF16, sD)
            Vd = sb("Vd", [128, 17, 256], BF16, sD)
            Hk = sb("Hk", [128, GW], F32, sD)
            G = sb("G", [128, GW], F32, sD)
            junkd = sb("junkd", [128, 2, 128], BF16, sD)
            ss4 = sb("ss4", [128, 17, 4], F32, sD)
            ln4 = sb("ln4", [128, 17, 4], F32, sD)
            r4 = sb("r4", [128, 17, 4], F32, sD)
            qdb = [sb("qdb%d" % i, [128, 256], BF16, sD) for i in range(2)]
            NSB = 3
            sbias = [sb("sbias%d" % i, [128, 512], F32, sD) for i in range(NSB)]
            NPT = 5
            PT = [sb("PTd%d" % i, [128, 512], BF16, sD) for i in range(NPT)]
            szb = sb("szb", [128, 2, 512], F32, sD)
            ezb = sb("ezb", [128, 512], F32, sD)
            recd = sb("recd", [128, 512], F32, sD)
            t1 = sb("t1", [128, 2, 512], F32, sD)
            t2 = sb("t2", [128, 512], F32, sD)
            od = sb("od", [128, 2, 512], F32, sD)
            sqb = sb("sqb", [128, 2, 512], BF16, sD)
            rsb = sb("rsb", [128, 512], F32, sD)
            onn = sb("onn", [128, 512], F32, sD)
            ogb = [sb("ogb%d" % i, [128, 2, 512], BF16, sD) for i in range(2)]
            B = [ps("dB%d" % i, [128, 512], F32, sD) for i in range(6)]
            Tv = [ps("dT%d" % i, [128, 1024], BF16, sD) for i in range(2)]
            SBK = (0, 1, 5)
            LA = 2

            WOFF = (OFF_QD, OFF_KD, OFF_VD, OFF_ZB)

            def load_w(h, i):
                wload(wr[i], w_in_v[:, :, WOFF[i] + h * 256:WOFF[i] + (h + 1) * 256], 16, w=[("wr", i)], sem="wr%d" % i)

            S.pool(lambda e: e.memset(Jf[:, :], 0.0), w=["Jf"])
            S.pool(lambda e: e.affine_select(out=Jf[:, :], in_=Jf[:, :], pattern=[[1, 128]],
                                             compare_op=ALU.not_equal, fill=1.0, base=-127, channel_multiplier=1),
                   r=["Jf"], w=["Jf"])
            for i in range(4):
                load_w(0, i)
            ogb_i = 0
            for h in range(n_diff_heads):
                S.op("sp", lambda e, h=h: e.dma_start(out=Hk[:, :], in_=bass.AP(tensor=tA_h, offset=h * NA, ap=[[1, 128], [1, GW]])),
                     [], ["Hk"], dma="Hk")
                for j, (a0, a1) in enumerate(((0, 512), (512, 1024), (1024, GW))):
                    bank = 3 + j
                    S.pe(lambda e, a0=a0, a1=a1, bank=bank: e.matmul(B[bank][:, 0:a1 - a0], lhsT=Jf[:, :], rhs=Hk[:, a0:a1], start=True, stop=True),
                         r=["Hk", "Jf"], w=["B%d" % bank])
                    S.dve(lambda e, a0=a0, a1=a1, bank=bank: e.tensor_copy(out=G[:, a0:a1], in_=B[bank][:, 0:a1 - a0]), r=["B%d" % bank], w=["G"])
                if dbg and h == 0:
                    dma("sp", dbg_out["d_G"], G[:, :], r=["G"], w=["dbgG"], sem="dbg")
                for st in range(3):
                    deferred = []
                    for tt in range(17):
                        if st == 0 and tt == 16:
                            continue
                        P, c0 = tok(tt)
                        b = tt % 2
                        bank = tt % 2
                        for k in range(16):
                            S.pe(lambda e, P=P, c0=c0, k=k, bank=bank, st=st: e.matmul(B[bank][:P, 0:256], lhsT=uT[:, k, c0:c0 + P], rhs=wr[st][:, k, :],
                                                                                         start=(k == 0), stop=(k == 15)),
                                 r=[("wr", st)], w=["B%d" % bank])
                        for f in deferred:
                            f()
                        deferred = []
                        if st == 2:
                            if tt % 2 == 0:
                                S.act(lambda e, P=P, tt=tt, bank=bank: e.activation(out=Vd[:P, tt, :], in_=B[bank][:P, 0:256], func=AF.Copy), r=["B%d" % bank], w=[("Vd", tt)])
                            else:
                                S.dve(lambda e, P=P, tt=tt, bank=bank: e.tensor_copy(out=Vd[:P, tt, :], in_=B[bank][:P, 0:256]), r=["B%d" % bank], w=[("Vd", tt)])
                            continue
                        for m in range(2):
                            S.act(lambda e, P=P, tt=tt, bank=bank, m=m, st=st: e.activation(out=junkd[:P, m, :], in_=B[bank][:P, m * 128:(m + 1) * 128], func=AF.Square,
                                                                                            accum_out=ss4[:P, tt, 2 * st + m:2 * st + m + 1]),
                                  r=["B%d" % bank], w=[("junkd", m), ("ss4", tt, st, m)])
                        S.act(lambda e, P=P, tt=tt, st=st: e.activation(out=ln4[:P, tt, 2 * st:2 * st + 2], in_=ss4[:P, tt, 2 * st:2 * st + 2], func=AF.Ln, bias=128 * EPS, scale=1.0),
                              r=[("ss4", tt, st, 0), ("ss4", tt, st, 1)], w=[("ln4", tt, st)])
                        S.act(lambda e, P=P, tt=tt, st=st: e.activation(out=r4[:P, tt, 2 * st:2 * st + 2], in_=ln4[:P, tt, 2 * st:2 * st + 2], func=AF.Exp, scale=-0.5),
                              r=[("ln4", tt, st)], w=[("r4", tt, st)])
                        for m in range(2):
                            if st == 0:
                                S.dve(lambda e, P=P, tt=tt, m=m, b=b, bank=bank: e.scalar_tensor_tensor(out=qdb[b][:P, m * 128:(m + 1) * 128], in0=B[bank][:P, m * 128:(m + 1) * 128],
                                                                                                   scalar=r4[:P, tt, m:m + 1], in1=gsm[:P, G_DQ:G_DK], op0=ALU.mult, op1=ALU.mult),
                                      r=["B%d" % bank, ("r4", tt, st)], w=[("qdb", b)])
                            else:
                                S.dve(lambda e, P=P, tt=tt, m=m, b=b, bank=bank: e.tensor_scalar_mul(out=qdb[b][:P, m * 128:(m + 1) * 128], in0=B[bank][:P, m * 128:(m + 1) * 128],
                                                                                                scalar1=r4[:P, tt, 2 + m:3 + m]),
                                      r=["B%d" % bank, ("r4", tt, st)], w=[("qdb", b)])

                        def later(tt=tt, P=P, c0=c0, b=b, st=st):
                            for m in range(2):
                                S.pe(lambda e, m=m: e.transpose(out=Tv[b][:, m * 128:m * 128 + P], in_=qdb[b][:P, m * 128:(m + 1) * 128], identity=ident[:P, :P]),
                                     r=[("qdb", b)], w=["B%d" % (6 + b)])
                            dst = qdT if st == 0 else kdT
                            nm = "qdT" if st == 0 else "kdT"
                            if b == 0:
                                S.dve(lambda e: e.tensor_copy(out=dst[:, :, c0:c0 + P], in_=Tv[b][:, 0:256].rearrange("p (h t) -> p h t", h=2)[:, :, 0:P]),
                                      r=["B%d" % (6 + b)], w=[(nm, tt)])
                            else:
                                S.act(lambda e: e.activation(out=dst[:, :, c0:c0 + P], in_=Tv[b][:, 0:256].rearrange("p (h t) -> p h t", h=2)[:, :, 0:P], func=AF.Copy),
                                      r=["B%d" % (6 + b)], w=[(nm, tt)])
                        deferred.append(later)
                    for f in deferred:
                        f()
                    if h + 1 < n_diff_heads:
                        load_w(h + 1, st)
                for qb in range(4):
                    q0 = qb * 512
                    qk = [("qdT", t) for t in range(4 * qb, 4 * qb + 4)]
                    for j in range(2):
                        for k in range(16):
                            S.pe(lambda e, k=k, j=j, q0=q0: e.matmul(B[5][:, :], lhsT=wr[3][:, k, j * 128:(j + 1) * 128], rhs=uT[:, k, q0:q0 + 512],
                                                                     start=(k == 0), stop=(k == 15)), r=[("wr", 3)], w=["B5"])
                        S.act(lambda e: e.activation(out=ezb[:, :], in_=B[5][:, :], func=AF.Exp, scale=-1.0), r=["B5"], w=["ezb"])
                        S.act(lambda e: e.activation(out=ezb[:, :], in_=ezb[:, :], func=AF.Ln, bias=1.0, scale=1.0), r=["ezb"], w=["ezb"])
                        S.act(lambda e: e.activation(out=ezb[:, :], in_=ezb[:, :], func=AF.Exp, scale=-1.0), r=["ezb"], w=["ezb"])
                        S.dve(lambda e, j=j: e.tensor_tensor(out=szb[:, j, :], in0=B[5][:, :], in1=ezb[:, :], op=ALU.mult), r=["B5", "ezb"], w=[("szb", j)])
                    for m in range(2):
                        def S_mm(c, m=m, q0=q0, qk=qk):
                            P, c0 = tok(c)
                            bank = SBK[c % 3]
                            S.pe(lambda e: e.matmul(B[bank][:P, :], lhsT=kdT[:, m, c0:c0 + P], rhs=qdT[:, m, q0:q0 + 512], start=True, stop=True),
                                 r=qk + [("kdT", c)], w=["B%d" % bank])

                        def EXPc(c, qb=qb):
                            P, c0 = tok(c)
                            bank = SBK[c % 3]
                            x0 = (512 * qb - 128 * c) if c < 16 else (16 + 512 * qb)
                            if (x0 <= -602 or x0 >= 218) and not (KDBG & 16):
                                col = 0 if x0 <= -602 else GW - 1
                                S.act(lambda e: e.activation(out=PT[c % NPT][:P, :], in_=B[bank][:P, :], func=AF.Exp, bias=G[:P, col:col + 1], scale=1.0),
                                      r=["B%d" % bank, "G"], w=[("PTd", c % NPT)])
                                return
                            w0 = min(max(x0, XLO), 256) - XLO
                            S.dve(lambda e: e.tensor_tensor(out=sbias[c % NSB][:P, :], in0=B[bank][:P, :], in1=G[:P, w0:w0 + 512], op=ALU.add),
                                  r=["B%d" % bank, "G"], w=[("sbias", c % NSB)])
                            S.act(lambda e: e.activation(out=PT[c % NPT][:P, :], in_=sbias[c % NSB][:P, :], func=AF.Exp),
                                  r=[("sbias", c % NSB)], w=[("PTd", c % NPT)])

                        def PVc(c):
                            P, c0 = tok(c)
                            for j in range(2):
                                S.pe(lambda e, j=j: e.matmul(B[2 + j][:, :], lhsT=Vd[:P, c, j * 128:(j + 1) * 128], rhs=PT[c % NPT][:P, :], start=(c == 0), stop=(c == 16)),
                                     r=[("PTd", c % NPT), ("Vd", c)], w=["B%d" % (2 + j)])
                            S.pe(lambda e: e.matmul(B[4][:, :], lhsT=ones[:P, :], rhs=PT[c % NPT][:P, :], start=(c == 0), stop=(c == 16)),
                                 r=[("PTd", c % NPT)], w=["B4"])

                        for c in range(LA):
                            S_mm(c)
                        for c in range(17):
                            EXPc(c)
                            if c + LA < 17:
                                S_mm(c + LA)
                            PVc(c)
                        S.act(lambda e: e.activation(out=recd[:, :], in_=B[4][:, :], func=AF.Ln), r=["B4"], w=["recd"])
                        S.act(lambda e: e.activation(out=recd[:, :], in_=recd[:, :], func=AF.Exp, scale=-1.0), r=["recd"], w=["recd"])
                        for j in range(2):
                            if m == 0:
                                S.dve(lambda e, j=j: e.tensor_tensor(out=t1[:, j, :], in0=B[2 + j][:, :], in1=recd[:, :], op=ALU.mult),
                                      r=["B%d" % (2 + j), "recd"], w=[("t1", j)])
                            else:
                                S.dve(lambda e, j=j: e.tensor_tensor(out=t2[:, :], in0=B[2 + j][:, :], in1=recd[:, :], op=ALU.mult),
                                      r=["B%d" % (2 + j), "recd"], w=["t2"])
                                S.dve(lambda e, j=j: e.scalar_tensor_tensor(out=od[:, j, :], in0=t2[:, :], scalar=neglam[:, 0:1], in1=t1[:, j, :],
                                                                            op0=ALU.mult, op1=ALU.add), r=["t2", ("t1", j)], w=[("od", j)])
                    for j in range(2):
                        S.act(lambda e, j=j: e.activation(out=sqb[:, j, :], in_=od[:, j, :], func=AF.Square), r=[("od", j)], w=[("sqb", j)])
                    for j in range(2):
                        S.pe(lambda e, j=j: e.matmul(B[5][:, :], lhsT=ones[:, :], rhs=sqb[:, j, :], start=(j == 0), stop=(j == 1)), r=[("sqb", j)], w=["B5"])
                    S.act(lambda e: e.activation(out=rsb[:, :], in_=B[5][:, :], func=AF.Ln, bias=256 * EPS, scale=1.0), r=["B5"], w=["rsb"])
                    S.act(lambda e: e.activation(out=rsb[:, :], in_=rsb[:, :], func=AF.Exp, scale=-0.5), r=["rsb"], w=["rsb"])
                    oi = ogb_i % 2
                    ogb_i += 1
                    for j in range(2):
                        S.dve(lambda e, j=j: e.scalar_tensor_tensor(out=onn[:, :], in0=od[:, j, :], scalar=gsub[:, j:j + 1], in1=rsb[:, :], op0=ALU.mult, op1=ALU.mult),
                              r=[("od", j), "rsb"], w=["onn"])
                        S.dve(lambda e, j=j, oi=oi: e.tensor_tensor(out=ogb[oi][:, j, :], in0=onn[:, :], in1=szb[:, j, :], op=ALU.mult),
                              r=["onn", ("szb", j)], w=[("ogb", oi)])
                    dma("sp", ob_scr[h * 256:(h + 1) * 256, q0:q0 + 512].rearrange("(j p) t -> p j t", p=128), ogb[oi][:, :, :],
                        r=[("ogb", oi)], w=[("ob", h, qb)], sem="ogb%d" % oi)
                if h + 1 < n_diff_heads:
                    load_w(h + 1, 3)
            S.flush()
        if stop == "D":
            return nc

        with ExitStack() as sF:
            oh = sb("oh", [128, 16, 1024], BF16, sF)
            mT = sb("mT", [128, 16, 1024], BF16, sF)
            wsb = sb("wsb", [128, 16384], BF16, sF)
            eg = [sb("eg%d" % i, [128, 512], F32, sF) for i in range(2)]
            tm = [sb("tm%d" % i, [128, 512], F32, sF) for i in range(2)]
            xin = [sb("xin%d" % i, [128, 512], F32, sF) for i in range(2)]
            yo = [sb("yo%d" % i, [128, 512], F32, sF) for i in range(2)]
            B = [ps("fB%d" % i, [128, 512], F32, sF) for i in range(8)]

            def ws(i):
                return wsb[:, i * 2048:(i + 1) * 2048].rearrange("p (k c) -> p k c", k=16)

            def wo(s_):
                return wsb[:, s_ * 8192:(s_ + 1) * 8192].rearrange("p (k c) -> p k c", k=16)

            w_a_v = w_a_d.rearrange("(k p) n -> p k n", p=128)
            w_b_v = w_b_d.rearrange("(k p) n -> p k n", p=128)
            w_o_v = w_o_d.rearrange("(k p) n -> p k n", p=128)
            wi = 0
            oi = 0
            xi = 0
            for hf in range(2):
                t0 = hf * 1024
                for br in range(2):
                    scr = (oa_scr, ob_scr)[br]
                    wv = (w_a_v, w_b_v)[br]
                    goff = (OFF_GA, OFF_GB)[br]
                    sv = scr.rearrange("(k p) t -> p k t", p=128)

                    def ldo(e, sv=sv, t0=t0):
                        return [e.dma_start(out=oh[:, a:a + 4, :], in_=sv[:, a:a + 4, t0:t0 + 1024]) for a in range(0, 16, 4)]
                    S.op("sp", ldo, [], ["oh"], dma="oh", ndma=4)
                    slots = {}

                    def ldw(j, wv=wv, goff=goff):
                        nonlocal wi
                        a, b2 = wi % 8, (wi + 1) % 8
                        wi += 2
                        wload(ws(a), wv[:, :, j * 128:(j + 1) * 128], 16, w=[("ws", a)], sem="ws%d" % a)
                        wload(ws(b2), w_in_v[:, :, goff + j * 128:goff + (j + 1) * 128], 16, w=[("ws", b2)], sem="ws%d" % b2)
                        slots[j] = (a, b2)

                    ldw(0)
                    ldw(1)
                    for j in range(16):
                        if j + 2 < 16:
                            ldw(j + 2)
                        a, b2 = slots[j]
                        for blk in range(2):
                            st = (j * 2 + blk) % 4
                            mb, gb = 2 * st, 2 * st + 1
                            c0 = blk * 512
                            ei = (j * 2 + blk) % 2
                            for k in range(16):
                                S.pe(lambda e, k=k, a=a, mb=mb, c0=c0: e.matmul(B[mb][:, :], lhsT=ws(a)[:, k, :], rhs=oh[:, k, c0:c0 + 512], start=(k == 0), stop=(k == 15)),
                                     r=[("ws", a), "oh"], w=["B%d" % mb])
                            for k in range(16):
                                S.pe(lambda e, k=k, b2=b2, gb=gb, c0=c0, t0=t0: e.matmul(B[gb][:, :], lhsT=ws(b2)[:, k, :], rhs=uT[:, k, t0 + c0:t0 + c0 + 512],
                                                                                           start=(k == 0), stop=(k == 15)),
                                     r=[("ws", b2)], w=["B%d" % gb])
                            S.act(lambda e, gb=gb, ei=ei: e.activation(out=eg[ei][:, :], in_=B[gb][:, :], func=AF.Exp, scale=-1.0), r=["B%d" % gb], w=[("eg", ei)])
                            S.act(lambda e, ei=ei: e.activation(out=eg[ei][:, :], in_=eg[ei][:, :], func=AF.Ln, bias=1.0, scale=1.0), r=[("eg", ei)], w=[("eg", ei)])
                            S.act(lambda e, ei=ei: e.activation(out=eg[ei][:, :], in_=eg[ei][:, :], func=AF.Exp, scale=-1.0), r=[("eg", ei)], w=[("eg", ei)])
                            if br == 0:
                                S.dve(lambda e, ei=ei, mb=mb, j=j, c0=c0: e.tensor_tensor(out=mT[:, j, c0:c0 + 512], in0=B[mb][:, :], in1=eg[ei][:, :], op=ALU.mult),
                                      r=["B%d" % mb, ("eg", ei)], w=[("mT", j, blk)])
                            else:
                                S.dve(lambda e, ei=ei, mb=mb: e.tensor_tensor(out=tm[ei][:, :], in0=B[mb][:, :], in1=eg[ei][:, :], op=ALU.mult),
                                      r=["B%d" % mb, ("eg", ei)], w=[("tm", ei)])
                                S.dve(lambda e, ei=ei, j=j, c0=c0: e.tensor_tensor(out=mT[:, j, c0:c0 + 512], in0=tm[ei][:, :], in1=mT[:, j, c0:c0 + 512], op=ALU.add),
                                      r=[("tm", ei), ("mT", j, blk)], w=[("mT", j, blk)])
                for cg in range(4):
                    so = cg % 2
                    wload(wo(so), w_o_v[:, :, cg * 512:(cg + 1) * 512], 16, w=[("ws", 4 * so + i) for i in range(4)], sem="wo%d" % so, nsplit=4)
                    for t in range(8):
                        r0 = t0 + t * 128
                        xs = xi % 2
                        xi += 1
                        bank = xi % 8
                        dma("sp", xin[xs][:, :], x_d[r0:r0 + 128, cg * 512:(cg + 1) * 512], w=[("xin", xs)], sem="xin%d" % xs)
                        for k in range(16):
                            S.pe(lambda e, k=k, t=t, so=so, bank=bank: e.matmul(B[bank][:, :], lhsT=mT[:, k, t * 128:(t + 1) * 128], rhs=wo(so)[:, k, :],
                                                                              start=(k == 0), stop=(k == 15)),
                                 r=[("mT", k, t // 4)] + [("ws", 4 * so + i) for i in range(4)], w=["B%d" % bank])
                        S.dve(lambda e, xs=xs, bank=bank: e.tensor_tensor(out=yo[xs][:, :], in0=B[bank][:, :], in1=xin[xs][:, :], op=ALU.add),
                              r=["B%d" % bank, ("xin", xs)], w=[("yo", xs)])
                        dma("sp", out_d[r0:r0 + 128, cg * 512:(cg + 1) * 512], yo[xs][:, :], r=[("yo", xs)], w=[("out", r0, cg)], sem="yo%d" % xs)
            S.flush()
    return nc


def _t5_bucket_np(rel):
    nb, max_exact = 16, 8
    ret = np.where(rel > 0, nb, 0)
    n = np.abs(rel)
    nf = np.maximum(n, 1).astype(np.float32)
    large = max_exact + (np.log(nf / np.float32(max_exact)) / np.float32(math.log(128 / max_exact))
                         * np.float32(nb - max_exact)).astype(np.int32)
    large = np.minimum(large, nb - 1)
    return ret + np.where(n < max_exact, n, large)


def _const_tables():
    half = 32
    inv = (np.float32(10000.0) ** (-(np.arange(half, dtype=np.float32) / np.float32(half)))).astype(np.float32)
    rope = np.zeros((128, 17, 64), np.float32)
    for tt in range(17):
        pos = (16 + 128 * tt + np.arange(128)) if tt < 16 else np.arange(128)
        ang = (pos.astype(np.float32)[:, None] * inv[None, :]).astype(np.float32)
        rope[:, tt, 0:32] = np.cos(ang.astype(np.float64))
        rope[:, tt, 32:64] = np.sin(ang.astype(np.float64))
    dvals = 767 - np.arange(NA)
    bk = _t5_bucket_np(dvals.astype(np.int32))
    oneh = np.zeros((32, NA), np.float32)
    oneh[bk, np.arange(NA)] = 1.0
    return rope.reshape(128, 17 * 64), oneh


_NC_CACHE = {}


def make_in_maps(inputs, n=8):
    f = lambda a: np.ascontiguousarray(np.asarray(a, dtype=np.float32))
    rope, oneh = _const_tables()
    gsm = np.concatenate([f(inputs["q_a_norm"])[0], f(inputs["kv_a_norm"])[0], f(inputs["mla_q_norm"])[0], f(inputs["mla_k_norm"])[0],
                          f(inputs["diff_q_norm"])[0], f(inputs["diff_k_norm"])[0], f(inputs["diff_lambda"])[0].reshape(-1)])
    shared = {
        "meta": f(inputs["meta_tokens"]),
        "w_in": f(inputs["w_in"])[0], "w_uq": f(inputs["w_uq"])[0], "w_ukv": f(inputs["w_ukv"])[0],
        "w_a": f(inputs["w_branch_a"])[0], "w_b": f(inputs["w_branch_b"])[0], "w_o": f(inputs["w_out"])[0],
        "gA": np.ascontiguousarray(np.broadcast_to(f(inputs["norm_in"])[0][None, :], (128, D))),
        "gsm": np.ascontiguousarray(np.broadcast_to(gsm[None, :], (128, G_END))),
        "gcol": np.ascontiguousarray(f(inputs["diff_subln"])[0].reshape(2, 128).T),
        "relb": f(inputs["rel_bias"]),
        "ropeT": rope, "onehot": oneh,
    }
    x = f(inputs["x"])
    return [dict(shared, x=x[i]) for i in range(n)]


def kernel(**inputs):
    if "nc" not in _NC_CACHE:
        _NC_CACHE["nc"] = build()
    nc = _NC_CACHE["nc"]
    in_maps = make_in_maps(inputs, 8)
    res = run_bass_kernel_spmd(nc, in_maps, core_ids=list(range(8)))
    return np.stack([np.asarray(r["out"], dtype=np.float32) for r in res.results], axis=0)
```

```python
import math
from contextlib import ExitStack

import numpy as np
import concourse.bass as bass
import concourse.mybir as mybir
from concourse.bass_utils import run_bass_kernel_spmd

F32 = mybir.dt.float32
BF16 = mybir.dt.bfloat16
AF = mybir.ActivationFunctionType
ALU = mybir.AluOpType
AX = mybir.AxisListType

D = 2048
SEQ = 2048
NMETA = 16
L = SEQ + NMETA
EPS = 1e-6
IN_W = 15680
OFF_CQ, OFF_CKV, OFF_KR, OFF_ZA, OFF_QD, OFF_KD, OFF_VD, OFF_ZB, OFF_GA, OFF_GB = (
    0, 768, 1280, 1344, 3392, 5440, 7488, 9536, 11584, 13632)
G_QA, G_KVA, G_MQ, G_MK, G_DQ, G_DK, G_LAM, G_END = 0, 768, 1280, 1472, 1664, 1792, 1920, 2432
NA = 1535
GW = 1408
XLO = -640
LAM_INIT = 0.8 - 0.6 * math.exp(-0.3 * 0)
STRICT = False
KDBG = 0


class _Op:
    __slots__ = ("eng", "fn", "waits", "signal", "sigval", "dma", "pos", "dcount")


class Sched:
    ENGS = ("pe", "act", "dve", "pool", "sp")
    MAP = {"pe": "tensor", "act": "scalar", "dve": "vector", "pool": "gpsimd", "sp": "sync"}

    def __init__(self, nc, es):
        self.nc, self.es = nc, es
        self.sem = {e: es.enter_context(nc.semaphore("sg_" + e)) for e in self.ENGS}
        self.sigcnt = {e: 0 for e in self.ENGS}
        self.npos = {e: 0 for e in self.ENGS}
        self.dsem, self.dcnt = {}, {}
        self.waited = {e: {} for e in self.ENGS}
        self._reset()

    def _reset(self):
        self.ops = {e: [] for e in self.ENGS}
        self.lw, self.rd = {}, {}

    def _add_wait(self, o, ref, kind):
        eng = o.eng
        if ref[0] == "c":
            _, e2, op2 = ref
            if e2 == eng and o.dma is None and (eng == "pe" or (kind != "raw" and not STRICT)):
                return
            if op2.pos <= self.waited[eng].get(e2, -1):
                return
            self.waited[eng][e2] = op2.pos
            op2.signal = True
            o.waits.append(ref)
        else:
            _, sk, cnt = ref
            if cnt <= self.waited[eng].get(("d", sk), 0):
                return
            self.waited[eng][("d", sk)] = cnt
            o.waits.append(ref)

    def op(self, eng, fn, reads=(), writes=(), dma=None, ndma=1):
        o = _Op()
        o.eng, o.fn, o.waits, o.signal, o.sigval, o.dma = eng, fn, [], False, None, dma
        o.pos = self.npos[eng]
        self.npos[eng] += 1
        for k in reads:
            for ref in self.lw.get(k, {}).values():
                self._add_wait(o, ref, "raw")
        for k in writes:
            for ref in self.lw.get(k, {}).values():
                self._add_wait(o, ref, "waw")
            for ref in self.rd.get(k, {}).values():
                self._add_wait(o, ref, "war")
        if dma is not None:
            if dma not in self.dsem:
                self.dsem[dma] = self.es.enter_context(self.nc.semaphore("sd%d" % len(self.dsem)))
                self.dcnt[dma] = 0
            self.dcnt[dma] += ndma
            me, mk = ("d", dma, self.dcnt[dma]), ("d", dma)
        else:
            me, mk = ("c", eng, o), eng
        for k in reads:
            self.rd.setdefault(k, {})[mk] = me
        for k in writes:
            self.lw[k] = {mk: me}
            self.rd[k] = {}
        self.ops[eng].append(o)
        return o

    def pe(self, fn, r=(), w=()):
        return self.op("pe", fn, r, w)

    def act(self, fn, r=(), w=()):
        return self.op("act", fn, r, w)

    def dve(self, fn, r=(), w=()):
        return self.op("dve", fn, r, w)

    def pool(self, fn, r=(), w=()):
        return self.op("pool", fn, r, w)

    def flush(self):
        lastc = {}
        for e in self.ENGS:
            for o in reversed(self.ops[e]):
                if o.dma is None:
                    lastc[e] = o
                    break
        for o in lastc.values():
            o.signal = True
        for e in self.ENGS:
            for o in self.ops[e]:
                if o.dma is None and o.signal:
                    self.sigcnt[e] += 1
                    o.sigval = self.sigcnt[e]
        dcnt = dict(self.dcnt)
        with self.nc.Block() as block:
            for e in self.ENGS:
                ops = self.ops[e]

                def body(eng, e=e, ops=ops):
                    for o in ops:
                        for ref in o.waits:
                            if ref[0] == "c":
                                eng.wait_ge(self.sem[ref[1]], ref[2].sigval)
                            else:
                                eng.wait_ge(self.dsem[ref[1]], 16 * ref[2])
                        r = o.fn(eng)
                        if o.dma is not None:
                            for ins in (r if isinstance(r, (list, tuple)) else [r]):
                                ins.then_inc(self.dsem[o.dma], 16)
                        elif o.signal:
                            r.then_inc(self.sem[e], 1)
                    for e2, o2 in lastc.items():
                        if e2 != e:
                            eng.wait_ge(self.sem[e2], o2.sigval)
                    for sk, c in dcnt.items():
                        eng.wait_ge(self.dsem[sk], 16 * c)

                getattr(block, self.MAP[e])(body)
        for e in self.ENGS:
            for e2, o2 in lastc.items():
                self.waited[e][e2] = max(self.waited[e].get(e2, -1), o2.pos)
            for sk, c in dcnt.items():
                self.waited[e][("d", sk)] = c
        self._reset()


def tok(tt):
    return (128, tt * 128) if tt < 16 else (NMETA, SEQ)


def build(dbg=False, stop=None, n_mla_groups=8, n_diff_heads=8):
    nc = bass.Bass("TRN2", target_bir_lowering=False)

    def din(name, shape, dtype=F32):
        return nc.dram_tensor(name, shape, dtype, kind="ExternalInput")

    x_d = din("x", [SEQ, D]).ap()
    meta_d = din("meta", [NMETA, D]).ap()
    w_in_d = din("w_in", [D, IN_W]).ap()
    w_uq_d = din("w_uq", [768, 3072]).ap()
    w_ukv_d = din("w_ukv", [512, 4096]).ap()
    w_a_d = din("w_a", [D, D]).ap()
    w_b_d = din("w_b", [D, D]).ap()
    w_o_d = din("w_o", [D, D]).ap()
    gA_d = din("gA", [128, D]).ap()
    gsm_d = din("gsm", [128, G_END]).ap()
    gcol_d = din("gcol", [128, 2]).ap()
    relb_d = din("relb", [32, 8]).ap()
    rope_d = din("ropeT", [128, 17 * 64]).ap()
    oneh_d = din("onehot", [32, NA]).ap()
    out_d = nc.dram_tensor("out", [SEQ, D], F32, kind="ExternalOutput").ap()
    skind = "ExternalOutput" if dbg else "Internal"
    oa_scr = nc.dram_tensor("oa_scr", [D, SEQ], BF16, kind=skind).ap()
    ob_scr = nc.dram_tensor("ob_scr", [D, SEQ], BF16, kind=skind).ap()
    tA_h = nc.dram_tensor("tA", [8, NA], F32, kind="Internal")
    tA_d = tA_h.ap()
    dbg_out = {}
    if dbg:
        dbg_out["d_uT"] = nc.dram_tensor("d_uT", [128, 16 * L], BF16, kind="ExternalOutput").ap()
        dbg_out["d_G"] = nc.dram_tensor("d_G", [128, GW], F32, kind="ExternalOutput").ap()

    w_in_v = w_in_d.rearrange("(k p) n -> p k n", p=128)
    w_uq_v = w_uq_d.rearrange("(k p) n -> p k n", p=128)
    w_ukv_v = w_ukv_d.rearrange("(k p) n -> p k n", p=128)

    with ExitStack() as es:
        S = Sched(nc, es)

        def sb(name, shape, dtype, stack=es):
            return stack.enter_context(nc.sbuf_tensor("s_" + name, shape, dtype))

        def ps(name, shape, dtype, stack):
            return stack.enter_context(nc.psum_tensor("p_" + name, shape, dtype))

        def dma(q, out, in_, r=(), w=(), sem=None):
            return S.op(q, lambda e: e.dma_start(out=out, in_=in_), r, w, dma=sem)

        def wload(dst, src, nk, r=(), w=(), sem=None, nsplit=2):
            step = max(1, nk // nsplit)
            parts = [(k0, min(nk, k0 + step)) for k0 in range(0, nk, step)]

            def fn(e):
                return [e.dma_start(out=dst[:, a:b, :], in_=src[:, a:b, :]) for a, b in parts]
            return S.op("pool", fn, r, w, dma=sem, ndma=len(parts))

        uT = sb("uT", [128, 16, L], BF16)
        ident = sb("ident", [128, 128], BF16)
        identf = sb("identF", [128, 128], F32)
        ones = sb("ones", [128, 128], BF16)
        gsm = sb("gsm", [128, G_END], F32)
        gqp = sb("gqp", [128, 192], F32)
        gsub = sb("gsub", [128, 2], F32)
        ropeT = sb("ropeT", [128, 17, 64], F32)
        neglam = sb("neglam", [128, 1], F32)

        def ukeys(c0, n):
            return [("uT", t) for t in range(c0 // 128, (c0 + n - 1) // 128 + 1)]

        with ExitStack() as sAC:
            cqnT = sb("cqnT", [128, 6, SEQ], BF16, sAC)
            ckvnT = sb("ckvnT", [128, 4, L], BF16, sAC)
            krT = sb("krT", [128, L], BF16, sAC)
            sskr = sb("sskr", [128, 17], F32, sAC)

            with ExitStack() as sAB:
                gA = sb("gA", [128, D], F32, sAB)
                relb = sb("relb", [32, 8], F32, sAB)
                oneh = sb("oneh", [32, NA], F32, sAB)
                tsb = sb("tsb", [8, NA], F32, sAB)
                lamt = sb("lamt", [128, 256], F32, sAB)
                lams = sb("lams", [128, 4], F32, sAB)
                xt = [sb("xt%d" % i, [128, D], F32, sAB) for i in range(2)]
                ub = [sb("ub%d" % i, [128, D], BF16, sAB) for i in range(2)]
                junk = sb("junk", [128, D], BF16, sAB)
                ssx = sb("ssx", [128, 17], F32, sAB)
                lnx = sb("lnx", [128, 17], F32, sAB)
                rsx = sb("rsx", [128, 17], F32, sAB)
                B = [ps("pB%d" % i, [128, 512], F32, sAB) for i in range(4)]
                T = [ps("pT%d" % i, [128, 1024], BF16, sAB) for i in range(2)]

                dma("sp", gsm[:, :], gsm_d, w=["gsm"], sem="k1")
                dma("sp", gsub[:, :], gcol_d, w=["gsub"], sem="k2")
                dma("sp", ropeT[:, :, :], rope_d.rearrange("p (t f) -> p t f", f=64), w=["rope"], sem="k3")
                dma("sp", relb[:, :], relb_d, w=["relb"], sem="k4")
                dma("sp", oneh[:, :], oneh_d, w=["oneh"], sem="k5")
                dma("sp", gA[:, :], gA_d, w=["gA"], sem="k6")
                S.pool(lambda e: e.memset(identf[:, :], 0.0), w=["identf"])
                S.pool(lambda e: e.affine_select(out=identf[:, :], in_=identf[:, :], pattern=[[-1, 128]],
                                                 compare_op=ALU.not_equal, fill=1.0, base=0, channel_multiplier=1),
                       r=["identf"], w=["identf"])
                S.dve(lambda e: e.tensor_copy(out=ident[:, :], in_=identf[:, :]), r=["identf"], w=["ident"])
                S.dve(lambda e: e.memset(ones[:, :], 1.0), w=["ones"])
                S.dve(lambda e: e.memset(krT[64:128, :], 0.0), w=["krTpad"])
                S.dve(lambda e: e.tensor_scalar_mul(out=gA[:, :], in0=gA[:, :], scalar1=math.sqrt(D)), r=["gA"], w=["gA"])
                S.dve(lambda e: e.tensor_scalar_mul(out=gsm[:, G_QA:G_KVA], in0=gsm[:, G_QA:G_KVA], scalar1=math.sqrt(768.0)),
                      r=["gsm"], w=["gsm"])
                S.dve(lambda e: e.tensor_scalar_mul(out=gsm[:, G_KVA:G_MQ], in0=gsm[:, G_KVA:G_MQ], scalar1=math.sqrt(512.0)),
                      r=["gsm"], w=["gsm"])
                S.dve(lambda e: e.scalar_tensor_tensor(out=gqp[:, 0:128], in0=gsm[:, G_MQ:G_MQ + 128], scalar=math.sqrt(192.0),
                                                       in1=gsm[:, G_MK:G_MK + 128], op0=ALU.mult, op1=ALU.mult),
                      r=["gsm"], w=["gqp"])
                S.dve(lambda e: e.tensor_scalar_mul(out=gqp[:, 128:192], in0=gsm[:, G_MQ + 128:G_MQ + 192], scalar1=math.sqrt(192.0)),
                      r=["gsm"], w=["gqp"])
                S.dve(lambda e: e.scalar_tensor_tensor(out=gsm[:, G_DQ:G_DK], in0=gsm[:, G_DQ:G_DK], scalar=math.sqrt(128.0),
                                                       in1=gsm[:, G_DK:G_LAM], op0=ALU.mult, op1=ALU.mult),
                      r=["gsm"], w=["gsm"])
                S.dve(lambda e: e.tensor_scalar_mul(out=gsub[:, :], in0=gsub[:, :], scalar1=(1.0 - LAM_INIT) * 16.0),
                      r=["gsub"], w=["gsub"])
                S.dve(lambda e: e.tensor_tensor(out=lamt[:, 0:128], in0=gsm[:, G_LAM:G_LAM + 128], in1=gsm[:, G_LAM + 128:G_LAM + 256],
                                                op=ALU.mult), r=["gsm"], w=["lamt"])
                S.dve(lambda e: e.tensor_tensor(out=lamt[:, 128:256], in0=gsm[:, G_LAM + 256:G_LAM + 384], in1=gsm[:, G_LAM + 384:G_LAM + 512],
                                                op=ALU.mult), r=["gsm"], w=["lamt"])
                S.dve(lambda e: e.tensor_reduce(out=lams[:, 0:2], in_=lamt[:, :].rearrange("p (a f) -> p a f", a=2), axis=AX.X, op=ALU.add),
                      r=["lamt"], w=["lams"])
                S.act(lambda e: e.activation(out=lams[:, 2:4], in_=lams[:, 0:2], func=AF.Exp), r=["lams"], w=["lams2"])
                S.dve(lambda e: e.tensor_scalar(out=neglam[:, :], in0=lams[:, 3:4], scalar1=lams[:, 2:3], scalar2=-LAM_INIT,
                                                op0=ALU.subtract, op1=ALU.add), r=["lams2"], w=["neglam"])
                for j, (a0, a1) in enumerate(((0, 512), (512, 1024), (1024, NA))):
                    S.pe(lambda e, a0=a0, a1=a1, j=j: e.matmul(B[j][0:8, 0:a1 - a0], lhsT=relb[:, :], rhs=oneh[:, a0:a1], start=True, stop=True),
                         r=["relb", "oneh"], w=["B%d" % j])
                    S.dve(lambda e, a0=a0, a1=a1, j=j: e.tensor_copy(out=tsb[:, a0:a1], in_=B[j][0:8, 0:a1 - a0]), r=["B%d" % j], w=["tsb"])
                dma("sp", tA_d, tsb[:, :], r=["tsb"], w=["tA"], sem="c1")

                for tt in range(17):
                    P, c0 = tok(tt)
                    b = tt % 2
                    src = x_d[tt * 128:(tt + 1) * 128, :] if tt < 16 else meta_d
                    dma("sp", xt[b][:P, :], src, w=[("xt", b)], sem="xt%d" % b)
                    S.act(lambda e, P=P, b=b, tt=tt: e.activation(out=junk[:P, :], in_=xt[b][:P, :], func=AF.Square, accum_out=ssx[:P, tt:tt + 1]),
                          r=[("xt", b)], w=["junk", ("ssx", tt)])
                    S.act(lambda e, P=P, tt=tt: e.activation(out=lnx[:P, tt:tt + 1], in_=ssx[:P, tt:tt + 1], func=AF.Ln, bias=D * EPS, scale=1.0),
                          r=[("ssx", tt)], w=[("lnx", tt)])
                    S.act(lambda e, P=P, tt=tt: e.activation(out=rsx[:P, tt:tt + 1], in_=lnx[:P, tt:tt + 1], func=AF.Exp, scale=-0.5),
                          r=[("lnx", tt)], w=[("rsx", tt)])
                    S.dve(lambda e, P=P, b=b, tt=tt: e.scalar_tensor_tensor(out=ub[b][:P, :], in0=xt[b][:P, :], scalar=rsx[:P, tt:tt + 1],
                                                                          in1=gA[:P, :], op0=ALU.mult, op1=ALU.mult),
                          r=[("xt", b), ("rsx", tt), "gA"], w=[("ub", b)])
                    for half in range(2):
                        for k in range(8):
                            kk = half * 8 + k
                            S.pe(lambda e, P=P, b=b, k=k, kk=kk, half=half: e.transpose(
                                out=T[half][:, k * 128:k * 128 + P], in_=ub[b][:P, kk * 128:(kk + 1) * 128], identity=ident[:P, :P]),
                                r=[("ub", b), "ident"], w=[("T", half)])
                        cp = (lambda e, P=P, c0=c0, half=half: e.tensor_copy(
                            out=uT[:, half * 8:(half + 1) * 8, c0:c0 + P],
                            in_=T[half][:, :].rearrange("p (k t) -> p k t", k=8)[:, :, 0:P]))
                        ca = (lambda e, P=P, c0=c0, half=half: e.activation(
                            out=uT[:, half * 8:(half + 1) * 8, c0:c0 + P],
                            in_=T[half][:, :].rearrange("p (k t) -> p k t", k=8)[:, :, 0:P], func=AF.Copy))
                        if half == 0:
                            S.dve(cp, r=[("T", half)], w=[("uT", tt)])
                        else:
                            S.act(ca, r=[("T", half)], w=[("uT", tt)])

                if dbg:
                    dma("sp", dbg_out["d_uT"], uT[:, :, :].rearrange("p k t -> p (k t)"), r=[("uT", t) for t in range(17)], w=["dbg"], sem="dbg")
                S.flush()
            if stop == "A":
                return nc
            with ExitStack() as sAB:
                wB = sb("wB", [128, 16, 1344], BF16, sAB)
                cB = [sb("cB%d" % i, [128, 1344], F32, sAB) for i in range(2)]
                junk = sb("junkB", [128, 1344], BF16, sAB)
                ss3 = sb("ss3", [128, 17, 4], F32, sAB)
                ln3 = sb("ln3", [128, 17, 2], F32, sAB)
                rs3 = sb("rs3", [128, 17, 2], F32, sAB)
                cqb = [sb("cqb%d" % i, [128, 768], BF16, sAB) for i in range(2)]
                ckvb = [sb("ckvb%d" % i, [128, 512], BF16, sAB) for i in range(2)]
                krg = sb("krg", [128, 64], F32, sAB)
                rt = sb("rt", [128, 4, 32], F32, sAB)
                krb = [sb("krb%d" % i, [128, 64], BF16, sAB) for i in range(2)]
                B = [ps("bB%d" % i, [128, 512], F32, sAB) for i in range(4)]
                T = [ps("bT%d" % i, [128, 1024], BF16, sAB) for i in range(2)]
                wload(wB, w_in_v[:, :, 0:1344], 16, w=["wB"], sem="wB", nsplit=4)
                groups = ((0, 384), (384, 768), (768, 1280), (1280, 1344))
                deferred = []
                for tt in range(17):
                    P, c0 = tok(tt)
                    b = tt % 2
                    for g, (a0, a1) in enumerate(groups):
                        if tt == 16 and g < 2:
                            continue
                        for k in range(16):
                            S.pe(lambda e, P=P, c0=c0, g=g, a0=a0, a1=a1, k=k: e.matmul(
                                B[g][:P, 0:a1 - a0], lhsT=uT[:, k, c0:c0 + P], rhs=wB[:, k, a0:a1], start=(k == 0), stop=(k == 15)),
                                r=[("uT", tt), "wB"], w=["B%d" % g])
                        S.act(lambda e, P=P, g=g, a0=a0, a1=a1, b=b: e.activation(out=cB[b][:P, a0:a1], in_=B[g][:P, 0:a1 - a0], func=AF.Copy),
                              r=["B%d" % g], w=[("cB", b, g)])
                    for f in deferred:
                        f()
                    deferred = []
                    if tt < 16:
                        S.act(lambda e, P=P, b=b, tt=tt: e.activation(out=junk[:P, 0:768], in_=cB[b][:P, 0:768], func=AF.Square,
                                                                     accum_out=ss3[:P, tt, 0:1]),
                              r=[("cB", b, 0), ("cB", b, 1)], w=[("junk", 0), ("ss3", tt, 0)])
                        S.act(lambda e, P=P, tt=tt: e.activation(out=ln3[:P, tt, 0:1], in_=ss3[:P, tt, 0:1], func=AF.Ln, bias=768 * EPS, scale=1.0),
                              r=[("ss3", tt, 0)], w=[("ln3", tt, 0)])
                        S.act(lambda e, P=P, tt=tt: e.activation(out=rs3[:P, tt, 0:1], in_=ln3[:P, tt, 0:1], func=AF.Exp, scale=-0.5),
                              r=[("ln3", tt, 0)], w=[("rs3", tt, 0)])
                    S.act(lambda e, P=P, b=b, tt=tt: e.activation(out=junk[:P, 768:1280], in_=cB[b][:P, 768:1280], func=AF.Square,
                                                                 accum_out=ss3[:P, tt, 1:2]),
                          r=[("cB", b, 2)], w=[("junk", 1), ("ss3", tt, 1)])
                    S.act(lambda e, P=P, b=b, tt=tt: e.activation(out=junk[:P, 1280:1344], in_=cB[b][:P, 1280:1344], func=AF.Square,
                                                                 accum_out=ss3[:P, tt, 2:3]),
                          r=[("cB", b, 3)], w=[("junk", 2), ("ss3", tt, 2)])
                    S.act(lambda e, P=P, tt=tt: e.activation(out=ln3[:P, tt, 1:2], in_=ss3[:P, tt, 1:2], func=AF.Ln, bias=512 * EPS, scale=1.0),
                          r=[("ss3", tt, 1)], w=[("ln3", tt, 1)])
                    S.act(lambda e, P=P, tt=tt: e.activation(out=rs3[:P, tt, 1:2], in_=ln3[:P, tt, 1:2], func=AF.Exp, scale=-0.5),
                          r=[("ln3", tt, 1)], w=[("rs3", tt, 1)])
                    S.dve(lambda e, P=P, tt=tt: e.tensor_scalar_add(out=sskr[:P, tt:tt + 1], in0=ss3[:P, tt, 2:3], scalar1=192 * EPS),
                          r=[("ss3", tt, 2)], w=[("sskr", tt)])
                    if tt < 16:
                        S.dve(lambda e, P=P, b=b, tt=tt: e.scalar_tensor_tensor(out=cqb[b][:P, :], in0=cB[b][:P, 0:768], scalar=rs3[:P, tt, 0:1],
                                                                              in1=gsm[:P, G_QA:G_KVA], op0=ALU.mult, op1=ALU.mult),
                              r=[("cB", b, 0), ("cB", b, 1), ("rs3", tt, 0), "gsm"], w=[("cqb", b)])
                    S.dve(lambda e, P=P, b=b, tt=tt: e.scalar_tensor_tensor(out=ckvb[b][:P, :], in0=cB[b][:P, 768:1280], scalar=rs3[:P, tt, 1:2],
                                                                          in1=gsm[:P, G_KVA:G_MQ], op0=ALU.mult, op1=ALU.mult),
                          r=[("cB", b, 2), ("rs3", tt, 1), "gsm"], w=[("ckvb", b)])
                    S.dve(lambda e, P=P, b=b: e.tensor_tensor(out=krg[:P, :], in0=cB[b][:P, 1280:1344], in1=gsm[:P, G_MK + 128:G_MK + 192], op=ALU.mult),
                          r=[("cB", b, 3), "gsm"], w=["krg"])
                    cosv = lambda P=P, tt=tt: ropeT[:P, tt, 0:32]
                    sinv = lambda P=P, tt=tt: ropeT[:P, tt, 32:64]
                    S.dve(lambda e, P=P, tt=tt: e.tensor_tensor(out=rt[:P, 0, :], in0=krg[:P, 0:32], in1=ropeT[:P, tt, 0:32], op=ALU.mult),
                          r=["krg", "rope"], w=[("rt", 0)])
                    S.dve(lambda e, P=P, tt=tt: e.tensor_tensor(out=rt[:P, 1, :], in0=krg[:P, 32:64], in1=ropeT[:P, tt, 32:64], op=ALU.mult),
                          r=["krg", "rope"], w=[("rt", 1)])
                    S.dve(lambda e, P=P, tt=tt: e.tensor_tensor(out=rt[:P, 2, :], in0=krg[:P, 32:64], in1=ropeT[:P, tt, 0:32], op=ALU.mult),
                          r=["krg", "rope"], w=[("rt", 2)])
                    S.dve(lambda e, P=P, tt=tt: e.tensor_tensor(out=rt[:P, 3, :], in0=krg[:P, 0:32], in1=ropeT[:P, tt, 32:64], op=ALU.mult),
                          r=["krg", "rope"], w=[("rt", 3)])
                    S.dve(lambda e, P=P, b=b: e.tensor_tensor(out=krb[b][:P, 0:32], in0=rt[:P, 0, :], in1=rt[:P, 1, :], op=ALU.subtract),
                          r=[("rt", 0), ("rt", 1)], w=[("krb", b)])
                    S.dve(lambda e, P=P, b=b: e.tensor_tensor(out=krb[b][:P, 32:64], in0=rt[:P, 2, :], in1=rt[:P, 3, :], op=ALU.add),
                          r=[("rt", 2), ("rt", 3)], w=[("krb", b)])
                    def later(tt=tt, P=P, c0=c0, b=b):
                        if tt < 16:
                            for k in range(6):
                                S.pe(lambda e, P=P, b=b, k=k: e.transpose(out=T[0][:, k * 128:k * 128 + P], in_=cqb[b][:P, k * 128:(k + 1) * 128],
                                                                          identity=ident[:P, :P]), r=[("cqb", b), "ident"], w=[("T", 0)])
                            S.dve(lambda e, P=P, c0=c0: e.tensor_copy(out=cqnT[:, 0:6, c0:c0 + P],
                                                                      in_=T[0][:, 0:768].rearrange("p (k t) -> p k t", k=6)[:, :, 0:P]),
                                  r=[("T", 0)], w=[("cqnT", tt)])
                        for k in range(4):
                            S.pe(lambda e, P=P, b=b, k=k: e.transpose(out=T[1][:, k * 128:k * 128 + P], in_=ckvb[b][:P, k * 128:(k + 1) * 128],
                                                                      identity=ident[:P, :P]), r=[("ckvb", b), "ident"], w=[("T", 1)])
                        S.pe(lambda e, P=P, b=b: e.transpose(out=T[1][0:64, 512:512 + P], in_=krb[b][:P, 0:64], identity=ident[:P, :P]),
                             r=[("krb", b), "ident"], w=[("T", 1)])
                        S.act(lambda e, P=P, c0=c0: e.activation(out=ckvnT[:, 0:4, c0:c0 + P],
                                                                 in_=T[1][:, 0:512].rearrange("p (k t) -> p k t", k=4)[:, :, 0:P], func=AF.Copy),
                              r=[("T", 1)], w=[("ckvnT", tt)])
                        S.act(lambda e, P=P, c0=c0: e.activation(out=krT[0:64, c0:c0 + P], in_=T[1][0:64, 512:512 + P], func=AF.Copy),
                              r=[("T", 1)], w=[("krT", tt)])
                    deferred.append(later)
                for f in deferred:
                    f()
                S.flush()
            if stop == "B":
                return nc

            with ExitStack() as sC:
                wq = [sb("wq%d" % i, [128, 6, 384], BF16, sC) for i in range(2)]
                wkv = [sb("wkv%d" % i, [128, 4, 512], BF16, sC) for i in range(2)]
                wz = [sb("wz%d" % i, [128, 16, 128], BF16, sC) for i in range(2)]
                qTn = sb("qTn", [128, 2, SEQ], BF16, sC)
                qTr = sb("qTr", [128, 2, SEQ], BF16, sC)
                kTn = sb("kTn", [128, 2, L], BF16, sC)
                Vt = sb("Vt", [128, 17, 2, 128], BF16, sC)
                rstdk = sb("rstdk", [128, 17, 2], F32, sC)
                junkq = sb("junkq", [128, 4, 192], BF16, sC)
                ssq = sb("ssq", [128, 17, 4], F32, sC)
                lnq = sb("lnq", [128, 17, 4], F32, sC)
                rq = sb("rq", [128, 17, 2], F32, sC)
                qb_ = [sb("qb%d" % i, [128, 2, 192], BF16, sC) for i in range(3)]
                qr = [sb("qr%d" % i, [128, 2, 64], F32, sC) for i in range(3)]
                rt4 = sb("rt4", [128, 4, 2, 32], F32, sC)
                NPT = 4
                PT = [sb("PT%d" % i, [128, 512], BF16, sC) for i in range(NPT)]
                ez = sb("ez", [128, 512], F32, sC)
                sz = sb("sz", [128, 512], F32, sC)
                rec = sb("rec", [128, 512], F32, sC)
                og = [sb("og%d" % i, [128, 512], BF16, sC) for i in range(2)]
                B = [ps("qB%d" % i, [128, 512], F32, sC) for i in range(6)]
                Tv = [ps("qT%d" % i, [128, 1024], BF16, sC) for i in range(2)]

                def load_group(g):
                    s = g % 2
                    h0 = 2 * g
                    wload(wq[s], w_uq_v[:, :, h0 * 192:(h0 + 2) * 192], 6, w=[("wq", s)], sem="wq%d" % s)
                    wload(wkv[s], w_ukv_v[:, :, h0 * 256:(h0 + 2) * 256], 4, w=[("wkv", s)], sem="wkv%d" % s)

                S.dve(lambda e: e.memset(qTr[64:128, :, :], 0.0), w=["qTrpad"])
                load_group(0)
                og_i = 0
                for g in range(n_mla_groups):
                    s = g % 2
                    h0 = 2 * g
                    if g + 1 < n_mla_groups:
                        load_group(g + 1)
                    for hh in range(2):
                        if not (KDBG & 8):
                            wload(wz[hh], w_in_v[:, :, OFF_ZA + (h0 + hh) * 128:OFF_ZA + (h0 + hh + 1) * 128], 16, w=[("wz", hh)], sem="wz%d" % hh)
                    deferred = []
                    for tt in range(17):
                        P, c0 = tok(tt)
                        b = tt % 2
                        b3 = tt % 3
                        qbk, kvbk = tt % 2, 2 + tt % 2
                        if tt < 16:
                            for k in range(6):
                                S.pe(lambda e, P=P, c0=c0, k=k, s=s, qbk=qbk: e.matmul(B[qbk][:P, 0:384], lhsT=cqnT[:, k, c0:c0 + P], rhs=wq[s][:, k, :],
                                                                                          start=(k == 0), stop=(k == 5)),
                                     r=[("wq", s)], w=["B%d" % qbk])
                        for k in range(4):
                            S.pe(lambda e, P=P, c0=c0, k=k, s=s, kvbk=kvbk: e.matmul(B[kvbk][:P, 0:512], lhsT=ckvnT[:, k, c0:c0 + P], rhs=wkv[s][:, k, :],
                                                                                        start=(k == 0), stop=(k == 3)),
                                 r=[("wkv", s)], w=["B%d" % kvbk])
                        while deferred and deferred[0][0] <= tt - 2:
                            deferred.pop(0)[1]()
                        if tt < 16:
                            for hh in range(2):
                                S.act(lambda e, P=P, hh=hh, tt=tt, qbk=qbk: e.activation(out=junkq[:P, hh, :], in_=B[qbk][:P, hh * 192:(hh + 1) * 192], func=AF.Square,
                                                                                         accum_out=ssq[:P, tt, hh:hh + 1]),
                                      r=["B%d" % qbk], w=[("junkq", hh), ("ssq", tt, hh)])
                        S.act(lambda e, P=P, tt=tt, kvbk=kvbk: e.activation(out=Vt[:P, tt, :, :], in_=B[kvbk][:P, :].rearrange("p (h f) -> p h f", h=2)[:, :, 128:256],
                                                                            func=AF.Copy), r=["B%d" % kvbk], w=[("Vt", tt)])
                        for hh in range(2):
                            S.act(lambda e, P=P, hh=hh, tt=tt, kvbk=kvbk: e.activation(out=junkq[:P, 2 + hh, 0:128], in_=B[kvbk][:P, hh * 256:hh * 256 + 128], func=AF.Square,
                                                                                       accum_out=ssq[:P, tt, 2 + hh:3 + hh]),
                                  r=["B%d" % kvbk], w=[("junkq", 2 + hh), ("ssk", tt, hh)])
                        if tt < 16:
                            S.act(lambda e, P=P, tt=tt: e.activation(out=lnq[:P, tt, 0:2], in_=ssq[:P, tt, 0:2], func=AF.Ln, bias=192 * EPS, scale=1.0),
                                  r=[("ssq", tt, 0), ("ssq", tt, 1)], w=[("lnq", tt)])
                        S.act(lambda e, P=P, tt=tt: e.activation(out=lnq[:P, tt, 2:4], in_=ssq[:P, tt, 2:4], func=AF.Ln, bias=sskr[:P, tt:tt + 1], scale=1.0),
                              r=[("ssk", tt, 0), ("ssk", tt, 1)], w=[("lnk", tt)])
                        if tt < 16:
                            S.act(lambda e, P=P, tt=tt: e.activation(out=rq[:P, tt, 0:2], in_=lnq[:P, tt, 0:2], func=AF.Exp, scale=-0.5),
                                  r=[("lnq", tt)], w=[("rq", tt)])
                        S.act(lambda e, P=P, tt=tt: e.activation(out=rstdk[:P, tt, 0:2], in_=lnq[:P, tt, 2:4], func=AF.Exp, scale=-0.5),
                              r=[("lnk", tt)], w=[("rstdk", tt)])
                        if tt < 16:
                            for hh in range(2):
                                S.dve(lambda e, P=P, hh=hh, tt=tt, b3=b3, qbk=qbk: e.scalar_tensor_tensor(
                                    out=qb_[b3][:P, hh, 0:128], in0=B[qbk][:P, hh * 192:hh * 192 + 128], scalar=rq[:P, tt, hh:hh + 1],
                                    in1=gqp[:P, 0:128], op0=ALU.mult, op1=ALU.mult),
                                    r=["B%d" % qbk, ("rq", tt), "gqp"], w=[("qb", b3)])
                                S.dve(lambda e, P=P, hh=hh, tt=tt, b3=b3, qbk=qbk: e.scalar_tensor_tensor(
                                    out=qr[b3][:P, hh, :], in0=B[qbk][:P, hh * 192 + 128:hh * 192 + 192], scalar=rq[:P, tt, hh:hh + 1],
                                    in1=gqp[:P, 128:192], op0=ALU.mult, op1=ALU.mult),
                                    r=["B%d" % qbk, ("rq", tt), "gqp"], w=[("qr", b3)])
                            cb = lambda P, tt: ropeT[:P, tt, 0:32].unsqueeze(1).to_broadcast([P, 2, 32])
                            sn = lambda P, tt: ropeT[:P, tt, 32:64].unsqueeze(1).to_broadcast([P, 2, 32])
                            S.pool(lambda e, P=P, tt=tt, b3=b3: e.tensor_tensor(out=rt4[:P, 0, :, :], in0=qr[b3][:P, :, 0:32], in1=cb(P, tt), op=ALU.mult),
                                  r=[("qr", b3)], w=[("rt4", 0)])
                            S.pool(lambda e, P=P, tt=tt, b3=b3: e.tensor_tensor(out=rt4[:P, 1, :, :], in0=qr[b3][:P, :, 32:64], in1=sn(P, tt), op=ALU.mult),
                                  r=[("qr", b3)], w=[("rt4", 1)])
                            S.pool(lambda e, P=P, tt=tt, b3=b3: e.tensor_tensor(out=rt4[:P, 2, :, :], in0=qr[b3][:P, :, 32:64], in1=cb(P, tt), op=ALU.mult),
                                  r=[("qr", b3)], w=[("rt4", 2)])
                            S.pool(lambda e, P=P, tt=tt, b3=b3: e.tensor_tensor(out=rt4[:P, 3, :, :], in0=qr[b3][:P, :, 0:32], in1=sn(P, tt), op=ALU.mult),
                                  r=[("qr", b3)], w=[("rt4", 3)])
                            S.pool(lambda e, P=P, b3=b3: e.tensor_tensor(out=qb_[b3][:P, :, 128:160], in0=rt4[:P, 0, :, :], in1=rt4[:P, 1, :, :], op=ALU.subtract),
                                  r=[("rt4", 0), ("rt4", 1)], w=[("qb", b3)])
                            S.pool(lambda e, P=P, b3=b3: e.tensor_tensor(out=qb_[b3][:P, :, 160:192], in0=rt4[:P, 2, :, :], in1=rt4[:P, 3, :, :], op=ALU.add),
                                  r=[("rt4", 2), ("rt4", 3)], w=[("qb", b3)])

                            def later(tt=tt, P=P, c0=c0, b=b, b3=b3):
                                for hh in range(2):
                                    S.pe(lambda e, hh=hh: e.transpose(out=Tv[b][:, hh * 128:hh * 128 + P], in_=qb_[b3][:P, hh, 0:128], identity=ident[:P, :P]),
                                         r=[("qb", b3)], w=["B%d" % (6 + b)])
                                    S.pe(lambda e, hh=hh: e.transpose(out=Tv[b][0:64, (2 + hh) * 128:(2 + hh) * 128 + P], in_=qb_[b3][:P, hh, 128:192],
                                                                      identity=ident[:P, :P]), r=[("qb", b3)], w=["B%d" % (6 + b)])
                                S.dve(lambda e: e.tensor_copy(out=qTn[:, :, c0:c0 + P],
                                                              in_=Tv[b][:, 0:256].rearrange("p (h t) -> p h t", h=2)[:, :, 0:P]),
                                      r=["B%d" % (6 + b)], w=[("qTn", tt)])
                                if True:
                                    S.dve(lambda e: e.tensor_copy(out=qTr[0:64, :, c0:c0 + P],
                                                                  in_=Tv[b][0:64, 256:512].rearrange("p (h t) -> p h t", h=2)[:, :, 0:P]),
                                          r=["B%d" % (6 + b)], w=[("qTr", tt)])
                                else:
                                    S.act(lambda e: e.activation(out=qTr[0:64, :, c0:c0 + P],
                                                                 in_=Tv[b][0:64, 256:512].rearrange("p (h t) -> p h t", h=2)[:, :, 0:P], func=AF.Copy),
                                          r=["B%d" % (6 + b)], w=[("qTr", tt)])
                            deferred.append((tt, later))
                    blocks = [(0, 512), (512, 512), (1024, 512), (1536, 512), (2048, 16)]
                    bi = 0
                    for (c0, n) in blocks:
                        for hh in range(2):
                            bank = 4 + (bi % 2)
                            for k in range(4):
                                S.pe(lambda e, c0=c0, n=n, hh=hh, k=k, s=s, bank=bank: e.matmul(
                                    B[bank][:, 0:n], lhsT=wkv[s][:, k, hh * 256:hh * 256 + 128], rhs=ckvnT[:, k, c0:c0 + n], start=(k == 0), stop=(k == 3)),
                                    r=[("wkv", s)], w=["B%d" % bank])
                            if bi % 2 == 0:
                                S.dve(lambda e, c0=c0, n=n, hh=hh, bank=bank: e.tensor_copy(out=kTn[:, hh, c0:c0 + n], in_=B[bank][:, 0:n]),
                                      r=["B%d" % bank], w=[("kTn", hh, c0)])
                            else:
                                S.act(lambda e, c0=c0, n=n, hh=hh, bank=bank: e.activation(out=kTn[:, hh, c0:c0 + n], in_=B[bank][:, 0:n], func=AF.Copy),
                                      r=["B%d" % bank], w=[("kTn", hh, c0)])
                            bi += 1
                    while deferred:
                        deferred.pop(0)[1]()
                    for hh in range(2):
                        h = h0 + hh
                        if KDBG & 8:
                            wload(wz[hh], w_in_v[:, :, OFF_ZA + h * 128:OFF_ZA + (h + 1) * 128], 16, w=[("wz", hh)], sem="wz%d" % hh)
                        for qb in range(4):
                            q0 = qb * 512
                            rq_keys = [("qTn", t) for t in range(4 * qb, 4 * qb + 4)] + [("qTr", t) for t in range(4 * qb, 4 * qb + 4)]
                            for k in range(16):
                                S.pe(lambda e, k=k, hh=hh, q0=q0: e.matmul(B[5][:, :], lhsT=wz[hh][:, k, :], rhs=uT[:, k, q0:q0 + 512],
                                                                           start=(k == 0), stop=(k == 15)),
                                     r=[("wz", hh)], w=["B5"])
                            S.act(lambda e: e.activation(out=ez[:, :], in_=B[5][:, :], func=AF.Exp, scale=-1.0), r=["B5"], w=["ez"])
                            if KDBG & 1:
                                S.dve(lambda e: e.tensor_scalar_add(out=ez[:, :], in0=ez[:, :], scalar1=1.0), r=["ez"], w=["ez"])
                                S.dve(lambda e: e.reciprocal(out=ez[:, :], in_=ez[:, :]), r=["ez"], w=["ez"])
                            else:
                                S.act(lambda e: e.activation(out=ez[:, :], in_=ez[:, :], func=AF.Ln, bias=1.0, scale=1.0), r=["ez"], w=["ez"])
                                S.act(lambda e: e.activation(out=ez[:, :], in_=ez[:, :], func=AF.Exp, scale=-1.0), r=["ez"], w=["ez"])
                            S.dve(lambda e: e.tensor_tensor(out=sz[:, :], in0=B[5][:, :], in1=ez[:, :], op=ALU.mult), r=["B5", "ez"], w=["sz"])

                            def S_mm(c, hh=hh, q0=q0, rq_keys=rq_keys):
                                P, c0 = tok(c)
                                bank = c % 3
                                S.pe(lambda e: e.matmul(B[bank][:P, :], lhsT=kTn[:, hh, c0:c0 + P], rhs=qTn[:, hh, q0:q0 + 512], start=True, stop=False),
                                     r=rq_keys + [("kTn", hh, (c0 // 512) * 512)], w=["B%d" % bank])
                                S.pe(lambda e: e.matmul(B[bank][:P, :], lhsT=krT[:, c0:c0 + P], rhs=qTr[:, hh, q0:q0 + 512], start=False, stop=True),
                                     r=rq_keys + ["qTrpad"], w=["B%d" % bank])

                            def EXPc(c, hh=hh):
                                P, c0 = tok(c)
                                bank = c % 3
                                S.act(lambda e: e.activation(out=PT[c % NPT][:P, :], in_=B[bank][:P, :], func=AF.Exp, scale=rstdk[:P, c, hh:hh + 1]),
                                      r=["B%d" % bank, ("rstdk", c)], w=[("PT", c % NPT)])

                            def PVc(c, hh=hh):
                                P, c0 = tok(c)
                                S.pe(lambda e: e.matmul(B[3][:, :], lhsT=Vt[:P, c, hh, :], rhs=PT[c % NPT][:P, :], start=(c == 0), stop=(c == 16)),
                                     r=[("PT", c % NPT), ("Vt", c)], w=["B3"])
                                S.pe(lambda e: e.matmul(B[4][:, :], lhsT=ones[:P, :], rhs=PT[c % NPT][:P, :], start=(c == 0), stop=(c == 16)),
                                     r=[("PT", c % NPT)], w=["B4"])

                            S_mm(0)
                            S_mm(1)
                            for c in range(17):
                                EXPc(c)
                                if c + 2 < 17:
                                    S_mm(c + 2)
                                PVc(c)
                            oi = og_i % 2
                            og_i += 1
                            if KDBG & 1:
                                S.dve(lambda e: e.reciprocal(out=rec[:, :], in_=B[4][:, :]), r=["B4"], w=["rec"])
                            else:
                                S.act(lambda e: e.activation(out=rec[:, :], in_=B[4][:, :], func=AF.Ln), r=["B4"], w=["rec"])
                                S.act(lambda e: e.activation(out=rec[:, :], in_=rec[:, :], func=AF.Exp, scale=-1.0), r=["rec"], w=["rec"])
                            S.dve(lambda e: e.tensor_tensor(out=rec[:, :], in0=rec[:, :], in1=sz[:, :], op=ALU.mult), r=["rec", "sz"], w=["rec"])
                            S.dve(lambda e, oi=oi: e.tensor_tensor(out=og[oi][:, :], in0=B[3][:, :], in1=rec[:, :], op=ALU.mult), r=["B3", "rec"], w=[("og", oi)])
                            dma("sp", oa_scr[h * 128:(h + 1) * 128, q0:q0 + 512], og[oi][:, :], r=[("og", oi)], w=[("oa", h, qb)], sem="og%d" % oi)
                S.flush()
        if stop == "C":
            return nc

        with ExitStack() as sD:
            wr = [sb("wr%d" % i, [128, 16, 256], BF16, sD) for i in range(4)]
            Jf = sb("Jf", [128, 128], F32, sD)
            qdT = sb("qdT", [128, 2, SEQ], BF16, sD)
            kdT = sb("kdT", [128, 2, L], BF16, sD)
            Vd = sb("Vd", [128, 17, 256], BF16, sD)
            Hk = sb("Hk", [128, GW], F32, sD)
            G = sb("G", [128, GW], F32, sD)
            junkd = sb("junkd", [128, 2, 128], BF16, sD)
            ss4 = sb("ss4", [128, 17, 4], F32, sD)
            ln4 = sb("ln4", [128, 17, 4], F32, sD)
            r4 = sb("r4", [128, 17, 4], F32, sD)
            qdb = [sb("qdb%d" % i, [128, 256], F32, sD) for i in range(2)]
            NSB = 3
            sbias = [sb("sbias%d" % i, [128, 512], F32, sD) for i in range(NSB)]
            NPT = 5
            PT = [sb("PTd%d" % i, [128, 512], BF16, sD) for i in range(NPT)]
            szb = [sb("szb%d" % i, [128, 2, 512], F32, sD) for i in range(2)]
            sraw = sb("sraw", [128, 512], F32, sD)
            ogb_cnt = [0]
            ezb = [sb("ezb%d" % i, [128, 512], F32, sD) for i in range(2)]
            oraw = sb("oraw", [128, 2, 512], F32, sD)
            recd = sb("recd", [128, 512], F32, sD)
            t1 = sb("t1", [128, 2, 512], F32, sD)
            t2 = sb("t2", [128, 512], F32, sD)
            od = sb("od", [128, 2, 512], F32, sD)
            sqb = sb("sqb", [128, 2, 512], BF16, sD)
            rsb = sb("rsb", [128, 512], F32, sD)
            onn = sb("onn", [128, 512], F32, sD)
            ogb = [sb("ogb%d" % i, [128, 2, 512], BF16, sD) for i in range(2)]
            B = [ps("dB%d" % i, [128, 512], F32, sD) for i in range(8)]
            Tv = [B[6], B[7]]
            SBK = (0, 1, 5)
            LA = 2

            WOFF = (OFF_QD, OFF_KD, OFF_VD, OFF_ZB)

            def load_w(h, i):
                wload(wr[i], w_in_v[:, :, WOFF[i] + h * 256:WOFF[i] + (h + 1) * 256], 16, w=[("wr", i)], sem="wr%d" % i)

            S.pool(lambda e: e.memset(Jf[:, :], 0.0), w=["Jf"])
            S.pool(lambda e: e.affine_select(out=Jf[:, :], in_=Jf[:, :], pattern=[[1, 128]],
                                             compare_op=ALU.not_equal, fill=1.0, base=-127, channel_multiplier=1),
                   r=["Jf"], w=["Jf"])
            for i in range(4):
                load_w(0, i)
            ogb_i = 0
            for h in range(n_diff_heads):
                S.op("sp", lambda e, h=h: e.dma_start(out=Hk[:, :], in_=bass.AP(tensor=tA_h, offset=h * NA, ap=[[1, 128], [1, GW]])),
                     [], ["Hk"], dma="Hk")
                for j, (a0, a1) in enumerate(((0, 512), (512, 1024), (1024, GW))):
                    bank = 3 + j
                    S.pe(lambda e, a0=a0, a1=a1, bank=bank: e.matmul(B[bank][:, 0:a1 - a0], lhsT=Jf[:, :], rhs=Hk[:, a0:a1], start=True, stop=True),
                         r=["Hk", "Jf"], w=["B%d" % bank])
                    S.dve(lambda e, a0=a0, a1=a1, bank=bank: e.tensor_copy(out=G[:, a0:a1], in_=B[bank][:, 0:a1 - a0]), r=["B%d" % bank], w=["G"])
                if dbg and h == 0:
                    dma("sp", dbg_out["d_G"], G[:, :], r=["G"], w=["dbgG"], sem="dbg")
                for st in range(3):
                    deferred = []
                    for tt in range(17):
                        if st == 0 and tt == 16:
                            continue
                        P, c0 = tok(tt)
                        b = tt % 2
                        bank = tt % 2
                        for k in range(16):
                            S.pe(lambda e, P=P, c0=c0, k=k, bank=bank, st=st: e.matmul(B[bank][:P, 0:256], lhsT=uT[:, k, c0:c0 + P], rhs=wr[st][:, k, :],
                                                                                         start=(k == 0), stop=(k == 15)),
                                 r=[("wr", st)], w=["B%d" % bank])
                        for f in deferred:
                            f()
                        deferred = []
                        if st == 2:
                            if tt % 2 == 0:
                                S.act(lambda e, P=P, tt=tt, bank=bank: e.activation(out=Vd[:P, tt, :], in_=B[bank][:P, 0:256], func=AF.Copy), r=["B%d" % bank], w=[("Vd", tt)])
                            else:
                                S.dve(lambda e, P=P, tt=tt, bank=bank: e.tensor_copy(out=Vd[:P, tt, :], in_=B[bank][:P, 0:256]), r=["B%d" % bank], w=[("Vd", tt)])
                            continue
                        for m in range(2):
                            S.act(lambda e, P=P, tt=tt, bank=bank, m=m, st=st: e.activation(out=junkd[:P, m, :], in_=B[bank][:P, m * 128:(m + 1) * 128], func=AF.Square,
                                                                                            accum_out=ss4[:P, tt, 2 * st + m:2 * st + m + 1]),
                                  r=["B%d" % bank], w=[("junkd", m), ("ss4", tt, st, m)])
                        S.act(lambda e, P=P, tt=tt, st=st: e.activation(out=ln4[:P, tt, 2 * st:2 * st + 2], in_=ss4[:P, tt, 2 * st:2 * st + 2], func=AF.Ln, bias=128 * EPS, scale=1.0),
                              r=[("ss4", tt, st, 0), ("ss4", tt, st, 1)], w=[("ln4", tt, st)])
                        S.act(lambda e, P=P, tt=tt, st=st: e.activation(out=r4[:P, tt, 2 * st:2 * st + 2], in_=ln4[:P, tt, 2 * st:2 * st + 2], func=AF.Exp, scale=-0.5),
                              r=[("ln4", tt, st)], w=[("r4", tt, st)])
                        for m in range(2):
                            if st == 0:
                                S.dve(lambda e, P=P, tt=tt, m=m, b=b, bank=bank: e.scalar_tensor_tensor(out=qdb[b][:P, m * 128:(m + 1) * 128], in0=B[bank][:P, m * 128:(m + 1) * 128],
                                                                                                   scalar=r4[:P, tt, m:m + 1], in1=gsm[:P, G_DQ:G_DK], op0=ALU.mult, op1=ALU.mult),
                                      r=["B%d" % bank, ("r4", tt, st)], w=[("qdb", b)])
                            else:
                                S.dve(lambda e, P=P, tt=tt, m=m, b=b, bank=bank: e.tensor_scalar_mul(out=qdb[b][:P, m * 128:(m + 1) * 128], in0=B[bank][:P, m * 128:(m + 1) * 128],
                                                                                                scalar1=r4[:P, tt, 2 + m:3 + m]),
                                      r=["B%d" % bank, ("r4", tt, st)], w=[("qdb", b)])

                        def later(tt=tt, P=P, c0=c0, b=b, st=st):
                            for m in range(2):
                                S.pe(lambda e, m=m: e.transpose(out=Tv[b][:, m * 128:m * 128 + P], in_=qdb[b][:P, m * 128:(m + 1) * 128], identity=identf[:P, :P]),
                                     r=[("qdb", b)], w=["B%d" % (6 + b)])
                            dst = qdT if st == 0 else kdT
                            nm = "qdT" if st == 0 else "kdT"
                            if b == 0:
                                S.dve(lambda e: e.tensor_copy(out=dst[:, :, c0:c0 + P], in_=Tv[b][:, 0:256].rearrange("p (h t) -> p h t", h=2)[:, :, 0:P]),
                                      r=["B%d" % (6 + b)], w=[(nm, tt)])
                            else:
                                S.act(lambda e: e.activation(out=dst[:, :, c0:c0 + P], in_=Tv[b][:, 0:256].rearrange("p (h t) -> p h t", h=2)[:, :, 0:P], func=AF.Copy),
                                      r=["B%d" % (6 + b)], w=[(nm, tt)])
                        deferred.append(later)
                    for f in deferred:
                        f()
                    if h + 1 < n_diff_heads:
                        load_w(h + 1, st)
                def zgate(qb, par):
                    q0 = qb * 512
                    outs = []
                    for j in range(2):
                        zb = 6 + j

                        def mm(j=j, zb=zb, q0=q0):
                            for k in range(16):
                                S.pe(lambda e, k=k: e.matmul(B[zb][:, :], lhsT=wr[3][:, k, j * 128:(j + 1) * 128], rhs=uT[:, k, q0:q0 + 512],
                                                             start=(k == 0), stop=(k == 15)), r=[("wr", 3)], w=["B%d" % zb])

                        def gate(j=j, zb=zb, par=par):
                            S.act(lambda e: e.activation(out=ezb[j][:, :], in_=B[zb][:, :], func=AF.Exp, scale=-1.0), r=["B%d" % zb], w=[("ezb", j)])
                            S.act(lambda e: e.activation(out=ezb[j][:, :], in_=ezb[j][:, :], func=AF.Ln, bias=1.0, scale=1.0), r=[("ezb", j)], w=[("ezb", j)])
                            S.act(lambda e: e.activation(out=ezb[j][:, :], in_=ezb[j][:, :], func=AF.Exp, scale=-1.0), r=[("ezb", j)], w=[("ezb", j)])
                            S.dve(lambda e: e.tensor_tensor(out=szb[par][:, j, :], in0=B[zb][:, :], in1=ezb[j][:, :], op=ALU.mult),
                                  r=["B%d" % zb, ("ezb", j)], w=[("szb", par, j)])
                        outs.append((mm, gate))
                    return outs

                def seg_post(qb, m, h=h):
                    q0 = qb * 512
                    par = qb % 2
                    S.act(lambda e: e.activation(out=recd[:, :], in_=sraw[:, :], func=AF.Ln), r=["sraw"], w=["recd"])
                    S.act(lambda e: e.activation(out=recd[:, :], in_=recd[:, :], func=AF.Exp, scale=-1.0), r=["recd"], w=["recd"])
                    for j in range(2):
                        if m == 0:
                            S.dve(lambda e, j=j: e.tensor_tensor(out=t1[:, j, :], in0=oraw[:, j, :], in1=recd[:, :], op=ALU.mult),
                                  r=[("oraw", j), "recd"], w=[("t1", j)])
                        else:
                            S.dve(lambda e, j=j: e.tensor_tensor(out=t2[:, :], in0=oraw[:, j, :], in1=recd[:, :], op=ALU.mult),
                                  r=[("oraw", j), "recd"], w=["t2"])
                            S.dve(lambda e, j=j: e.scalar_tensor_tensor(out=od[:, j, :], in0=t2[:, :], scalar=neglam[:, 0:1], in1=t1[:, j, :],
                                                                        op0=ALU.mult, op1=ALU.add), r=["t2", ("t1", j)], w=[("od", j)])
                    if m == 0:
                        return
                    for j in range(2):
                        S.dve(lambda e, j=j: e.tensor_tensor(out=sqb[:, j, :], in0=od[:, j, :], in1=od[:, j, :], op=ALU.mult), r=[("od", j)], w=[("sqb", j)])
                    for j in range(2):
                        S.pe(lambda e, j=j: e.matmul(B[6][:, :], lhsT=ones[:, :], rhs=sqb[:, j, :], start=(j == 0), stop=(j == 1)), r=[("sqb", j)], w=["B6"])
                    S.act(lambda e: e.activation(out=rsb[:, :], in_=B[6][:, :], func=AF.Ln, bias=256 * EPS, scale=1.0), r=["B6"], w=["rsb"])
                    S.act(lambda e: e.activation(out=rsb[:, :], in_=rsb[:, :], func=AF.Exp, scale=-0.5), r=["rsb"], w=["rsb"])
                    oi = ogb_cnt[0] % 2
                    ogb_cnt[0] += 1
                    for j in range(2):
                        S.dve(lambda e, j=j: e.scalar_tensor_tensor(out=onn[:, :], in0=od[:, j, :], scalar=gsub[:, j:j + 1], in1=rsb[:, :], op0=ALU.mult, op1=ALU.mult),
                              r=[("od", j), "rsb"], w=["onn"])
                        S.dve(lambda e, j=j, oi=oi: e.tensor_tensor(out=ogb[oi][:, j, :], in0=onn[:, :], in1=szb[par][:, j, :], op=ALU.mult),
                              r=["onn", ("szb", par, j)], w=[("ogb", oi)])
                    dma("sp", ob_scr[h * 256:(h + 1) * 256, q0:q0 + 512].rearrange("(j p) t -> p j t", p=128), ogb[oi][:, :, :],
                        r=[("ogb", oi)], w=[("ob", h, qb)], sem="ogb%d" % oi)

                for mm, gate in zgate(0, 0):
                    mm()
                    gate()
                pending = None
                for qb in range(4):
                    q0 = qb * 512
                    qk = [("qdT", t) for t in range(4 * qb, 4 * qb + 4)]
                    c_post = {0: 5, 1: 9, 2: 0, 3: 0}[qb]
                    znext = zgate(qb + 1, (qb + 1) % 2) if qb + 1 < 4 else []
                    for m in range(2):
                        def S_mm(c, m=m, q0=q0, qk=qk):
                            P, c0 = tok(c)
                            bank = SBK[c % 3]
                            S.pe(lambda e: e.matmul(B[bank][:P, :], lhsT=kdT[:, m, c0:c0 + P], rhs=qdT[:, m, q0:q0 + 512], start=True, stop=True),
                                 r=qk + [("kdT", c)], w=["B%d" % bank])

                        def EXPc(c, qb=qb):
                            P, c0 = tok(c)
                            bank = SBK[c % 3]
                            x0 = (512 * qb - 128 * c) if c < 16 else (16 + 512 * qb)
                            if x0 <= -602 or x0 >= 218:
                                col = 0 if x0 <= -602 else GW - 1
                                S.act(lambda e: e.activation(out=PT[c % NPT][:P, :], in_=B[bank][:P, :], func=AF.Exp, bias=G[:P, col:col + 1], scale=1.0),
                                      r=["B%d" % bank, "G"], w=[("PTd", c % NPT)])
                                return
                            w0 = x0 - XLO
                            S.dve(lambda e: e.tensor_tensor(out=sbias[c % NSB][:P, :], in0=B[bank][:P, :], in1=G[:P, w0:w0 + 512], op=ALU.add),
                                  r=["B%d" % bank, "G"], w=[("sbias", c % NSB)])
                            S.act(lambda e: e.activation(out=PT[c % NPT][:P, :], in_=sbias[c % NSB][:P, :], func=AF.Exp),
                                  r=[("sbias", c % NSB)], w=[("PTd", c % NPT)])

                        def PVc(c):
                            P, c0 = tok(c)
                            for j in range(2):
                                S.pe(lambda e, j=j: e.matmul(B[2 + j][:, :], lhsT=Vd[:P, c, j * 128:(j + 1) * 128], rhs=PT[c % NPT][:P, :], start=(c == 0), stop=(c == 16)),
                                     r=[("PTd", c % NPT), ("Vd", c)], w=["B%d" % (2 + j)])
                            S.pe(lambda e: e.matmul(B[4][:, :], lhsT=ones[:P, :], rhs=PT[c % NPT][:P, :], start=(c == 0), stop=(c == 16)),
                                 r=[("PTd", c % NPT)], w=["B4"])

                        for c in range(LA):
                            S_mm(c)
                        for c in range(17):
                            EXPc(c)
                            if c + LA < 17:
                                S_mm(c + LA)
                            PVc(c)
                            if pending is not None and c == c_post:
                                pending()
                                pending = None
                            if m == 1 and znext:
                                if c == 3:
                                    znext[0][0]()
                                elif c == 6:
                                    znext[0][1]()
                                elif c == 9:
                                    znext[1][0]()
                                elif c == 12:
                                    znext[1][1]()
                        for j in range(2):
                            S.dve(lambda e, j=j: e.tensor_copy(out=oraw[:, j, :], in_=B[2 + j][:, :]), r=["B%d" % (2 + j)], w=[("oraw", j)])
                        S.dve(lambda e: e.tensor_copy(out=sraw[:, :], in_=B[4][:, :]), r=["B4"], w=["sraw"])
                        pending = (lambda qb=qb, m=m: seg_post(qb, m))
                if pending is not None:
                    pending()
                    pending = None
                if h + 1 < n_diff_heads:
                    load_w(h + 1, 3)
            S.flush()
        if stop == "D":
            return nc

        with ExitStack() as sF:
            oh = sb("oh", [128, 16, 1024], BF16, sF)
            mT = sb("mT", [128, 16, 1024], BF16, sF)
            wsb = sb("wsb", [128, 16384], BF16, sF)
            eg = [sb("eg%d" % i, [128, 512], F32, sF) for i in range(2)]
            tm = [sb("tm%d" % i, [128, 512], F32, sF) for i in range(2)]
            xin = [sb("xin%d" % i, [128, 512], F32, sF) for i in range(2)]
            yo = [sb("yo%d" % i, [128, 512], F32, sF) for i in range(2)]
            B = [ps("fB%d" % i, [128, 512], F32, sF) for i in range(8)]

            def ws(i):
                return wsb[:, i * 2048:(i + 1) * 2048].rearrange("p (k c) -> p k c", k=16)

            def wo(s_):
                return wsb[:, s_ * 8192:(s_ + 1) * 8192].rearrange("p (k c) -> p k c", k=16)

            w_a_v = w_a_d.rearrange("(k p) n -> p k n", p=128)
            w_b_v = w_b_d.rearrange("(k p) n -> p k n", p=128)
            w_o_v = w_o_d.rearrange("(k p) n -> p k n", p=128)
            wi = 0
            oi = 0
            xi = 0
            for hf in range(2):
                t0 = hf * 1024
                for br in range(2):
                    scr = (oa_scr, ob_scr)[br]
                    wv = (w_a_v, w_b_v)[br]
                    goff = (OFF_GA, OFF_GB)[br]
                    sv = scr.rearrange("(k p) t -> p k t", p=128)

                    def ldo(e, sv=sv, t0=t0):
                        return [e.dma_start(out=oh[:, a:a + 4, :], in_=sv[:, a:a + 4, t0:t0 + 1024]) for a in range(0, 16, 4)]
                    if not (hf == 1 and br == 0):
                        S.op("pool", ldo, [], ["oh"], dma="oh", ndma=4)
                    slots = {}

                    def ldw(j, wv=wv, goff=goff):
                        nonlocal wi
                        a, b2 = wi % 8, (wi + 1) % 8
                        wi += 2
                        wload(ws(a), wv[:, :, j * 128:(j + 1) * 128], 16, w=[("ws", a)], sem="ws%d" % a)
                        wload(ws(b2), w_in_v[:, :, goff + j * 128:goff + (j + 1) * 128], 16, w=[("ws", b2)], sem="ws%d" % b2)
                        slots[j] = (a, b2)

                    ldw(0)
                    ldw(1)
                    for j in range(16):
                        if j + 2 < 16:
                            ldw(j + 2)
                        a, b2 = slots[j]
                        for blk in range(2):
                            st = (j * 2 + blk) % 4
                            mb, gb = 2 * st, 2 * st + 1
                            c0 = blk * 512
                            ei = (j * 2 + blk) % 2
                            for k in range(16):
                                S.pe(lambda e, k=k, a=a, mb=mb, c0=c0: e.matmul(B[mb][:, :], lhsT=ws(a)[:, k, :], rhs=oh[:, k, c0:c0 + 512], start=(k == 0), stop=(k == 15)),
                                     r=[("ws", a), "oh"], w=["B%d" % mb])
                            for k in range(16):
                                S.pe(lambda e, k=k, b2=b2, gb=gb, c0=c0, t0=t0: e.matmul(B[gb][:, :], lhsT=ws(b2)[:, k, :], rhs=uT[:, k, t0 + c0:t0 + c0 + 512],
                                                                                           start=(k == 0), stop=(k == 15)),
                                     r=[("ws", b2)], w=["B%d" % gb])
                            S.act(lambda e, gb=gb, ei=ei: e.activation(out=eg[ei][:, :], in_=B[gb][:, :], func=AF.Exp, scale=-1.0), r=["B%d" % gb], w=[("eg", ei)])
                            S.act(lambda e, ei=ei: e.activation(out=eg[ei][:, :], in_=eg[ei][:, :], func=AF.Ln, bias=1.0, scale=1.0), r=[("eg", ei)], w=[("eg", ei)])
                            S.act(lambda e, ei=ei: e.activation(out=eg[ei][:, :], in_=eg[ei][:, :], func=AF.Exp, scale=-1.0), r=[("eg", ei)], w=[("eg", ei)])
                            if br == 0:
                                S.dve(lambda e, ei=ei, mb=mb, j=j, c0=c0: e.tensor_tensor(out=mT[:, j, c0:c0 + 512], in0=B[mb][:, :], in1=eg[ei][:, :], op=ALU.mult),
                                      r=["B%d" % mb, ("eg", ei)], w=[("mT", j, blk)])
                            else:
                                S.dve(lambda e, ei=ei, mb=mb: e.tensor_tensor(out=tm[ei][:, :], in0=B[mb][:, :], in1=eg[ei][:, :], op=ALU.mult),
                                      r=["B%d" % mb, ("eg", ei)], w=[("tm", ei)])
                                S.dve(lambda e, ei=ei, j=j, c0=c0: e.tensor_tensor(out=mT[:, j, c0:c0 + 512], in0=tm[ei][:, :], in1=mT[:, j, c0:c0 + 512], op=ALU.add),
                                      r=[("tm", ei), ("mT", j, blk)], w=[("mT", j, blk)])
                if hf == 0:
                    sv0 = oa_scr.rearrange("(k p) t -> p k t", p=128)
                    S.op("pool", lambda e: [e.dma_start(out=oh[:, a:a + 4, :], in_=sv0[:, a:a + 4, 1024:2048]) for a in range(0, 16, 4)],
                         [], ["oh"], dma="oh", ndma=4)
                for cg in range(4):
                    so = cg % 2
                    wload(wo(so), w_o_v[:, :, cg * 512:(cg + 1) * 512], 16, w=[("ws", 4 * so + i) for i in range(4)], sem="wo%d" % so, nsplit=4)
                    for t in range(8):
                        r0 = t0 + t * 128
                        xs = xi % 2
                        xi += 1
                        bank = xi % 8
                        dma("sp", xin[xs][:, :], x_d[r0:r0 + 128, cg * 512:(cg + 1) * 512], w=[("xin", xs)], sem="xin%d" % xs)
                        for k in range(16):
                            S.pe(lambda e, k=k, t=t, so=so, bank=bank: e.matmul(B[bank][:, :], lhsT=mT[:, k, t * 128:(t + 1) * 128], rhs=wo(so)[:, k, :],
                                                                              start=(k == 0), stop=(k == 15)),
                                 r=[("mT", k, t // 4)] + [("ws", 4 * so + i) for i in range(4)], w=["B%d" % bank])
                        S.dve(lambda e, xs=xs, bank=bank: e.tensor_tensor(out=yo[xs][:, :], in0=B[bank][:, :], in1=xin[xs][:, :], op=ALU.add),
                              r=["B%d" % bank, ("xin", xs)], w=[("yo", xs)])
                        dma("sp", out_d[r0:r0 + 128, cg * 512:(cg + 1) * 512], yo[xs][:, :], r=[("yo", xs)], w=[("out", r0, cg)], sem="yo%d" % xs)
            S.flush()
    return nc


def _t5_bucket_np(rel):
    nb, max_exact = 16, 8
    ret = np.where(rel > 0, nb, 0)
    n = np.abs(rel)
    nf = np.maximum(n, 1).astype(np.float32)
    large = max_exact + (np.log(nf / np.float32(max_exact)) / np.float32(math.log(128 / max_exact))
                         * np.float32(nb - max_exact)).astype(np.int32)
    large = np.minimum(large, nb - 1)
    return ret + np.where(n < max_exact, n, large)


def _const_tables():
    half = 32
    inv = (np.float32(10000.0) ** (-(np.arange(half, dtype=np.float32) / np.float32(half)))).astype(np.float32)
    rope = np.zeros((128, 17, 64), np.float32)
    for tt in range(17):
        pos = (16 + 128 * tt + np.arange(128)) if tt < 16 else np.arange(128)
        ang = (pos.astype(np.float32)[:, None] * inv[None, :]).astype(np.float32)
        rope[:, tt, 0:32] = np.cos(ang.astype(np.float64))
        rope[:, tt, 32:64] = np.sin(ang.astype(np.float64))
    dvals = 767 - np.arange(NA)
    bk = _t5_bucket_np(dvals.astype(np.int32))
    oneh = np.zeros((32, NA), np.float32)
    oneh[bk, np.arange(NA)] = 1.0
    return rope.reshape(128, 17 * 64), oneh


_NC_CACHE = {}


def make_in_maps(inputs, n=8):
    f = lambda a: np.ascontiguousarray(np.asarray(a, dtype=np.float32))
    rope, oneh = _const_tables()
    gsm = np.concatenate([f(inputs["q_a_norm"])[0], f(inputs["kv_a_norm"])[0], f(inputs["mla_q_norm"])[0], f(inputs["mla_k_norm"])[0],
                          f(inputs["diff_q_norm"])[0], f(inputs["diff_k_norm"])[0], f(inputs["diff_lambda"])[0].reshape(-1)])
    shared = {
        "meta": f(inputs["meta_tokens"]),
        "w_in": f(inputs["w_in"])[0], "w_uq": f(inputs["w_uq"])[0], "w_ukv": f(inputs["w_ukv"])[0],
        "w_a": f(inputs["w_branch_a"])[0], "w_b": f(inputs["w_branch_b"])[0], "w_o": f(inputs["w_out"])[0],
        "gA": np.ascontiguousarray(np.broadcast_to(f(inputs["norm_in"])[0][None, :], (128, D))),
        "gsm": np.ascontiguousarray(np.broadcast_to(gsm[None, :], (128, G_END))),
        "gcol": np.ascontiguousarray(f(inputs["diff_subln"])[0].reshape(2, 128).T),
        "relb": f(inputs["rel_bias"]),
        "ropeT": rope, "onehot": oneh,
    }
    x = f(inputs["x"])
    return [dict(shared, x=x[i]) for i in range(n)]


def kernel(**inputs):
    if "nc" not in _NC_CACHE:
        _NC_CACHE["nc"] = build()
    nc = _NC_CACHE["nc"]
    in_maps = make_in_maps(inputs, 8)
    res = run_bass_kernel_spmd(nc, in_maps, core_ids=list(range(8)))
    return np.stack([np.asarray(r["out"], dtype=np.float32) for r in res.results], axis=0)
```

```python
import math
from contextlib import ExitStack

import numpy as np
import concourse.bass as bass
import concourse.mybir as mybir
from concourse.bass_utils import run_bass_kernel_spmd

F32 = mybir.dt.float32
BF16 = mybir.dt.bfloat16
AF = mybir.ActivationFunctionType
ALU = mybir.AluOpType
AX = mybir.AxisListType

D = 2048
SEQ = 2048
NMETA = 16
L = SEQ + NMETA
EPS = 1e-6
IN_W = 15680
OFF_CQ, OFF_CKV, OFF_KR, OFF_ZA, OFF_QD, OFF_KD, OFF_VD, OFF_ZB, OFF_GA, OFF_GB = (
    0, 768, 1280, 1344, 3392, 5440, 7488, 9536, 11584, 13632)
G_QA, G_KVA, G_MQ, G_MK, G_DQ, G_DK, G_LAM, G_END = 0, 768, 1280, 1472, 1664, 1792, 1920, 2432
NA = 1535
GW = 1408
XLO = -640
LAM_INIT = 0.8 - 0.6 * math.exp(-0.3 * 0)
STRICT = False
KDBG = 0


class _Op:
    __slots__ = ("eng", "fn", "waits", "signal", "sigval", "dma", "pos", "dcount")


class Sched:
    ENGS = ("pe", "act", "dve", "pool", "sp")
    MAP = {"pe": "tensor", "act": "scalar", "dve": "vector", "pool": "gpsimd", "sp": "sync"}

    def __init__(self, nc, es):
        self.nc, self.es = nc, es
        self.sem = {e: es.enter_context(nc.semaphore("sg_" + e)) for e in self.ENGS}
        self.sigcnt = {e: 0 for e in self.ENGS}
        self.npos = {e: 0 for e in self.ENGS}
        self.dsem, self.dcnt = {}, {}
        self.waited = {e: {} for e in self.ENGS}
        self._reset()

    def _reset(self):
        self.ops = {e: [] for e in self.ENGS}
        self.lw, self.rd = {}, {}

    def _add_wait(self, o, ref, kind):
        eng = o.eng
        if ref[0] == "c":
            _, e2, op2 = ref
            if e2 == eng and o.dma is None and (eng == "pe" or (kind != "raw" and not STRICT)):
                return
            if op2.pos <= self.waited[eng].get(e2, -1):
                return
            self.waited[eng][e2] = op2.pos
            op2.signal = True
            o.waits.append(ref)
        else:
            _, sk, cnt = ref
            if cnt <= self.waited[eng].get(("d", sk), 0):
                return
            self.waited[eng][("d", sk)] = cnt
            o.waits.append(ref)

    def op(self, eng, fn, reads=(), writes=(), dma=None, ndma=1):
        o = _Op()
        o.eng, o.fn, o.waits, o.signal, o.sigval, o.dma = eng, fn, [], False, None, dma
        o.pos = self.npos[eng]
        self.npos[eng] += 1
        for k in reads:
            for ref in self.lw.get(k, {}).values():
                self._add_wait(o, ref, "raw")
        for k in writes:
            for ref in self.lw.get(k, {}).values():
                self._add_wait(o, ref, "waw")
            for ref in self.rd.get(k, {}).values():
                self._add_wait(o, ref, "war")
        if dma is not None:
            if dma not in self.dsem:
                self.dsem[dma] = self.es.enter_context(self.nc.semaphore("sd%d" % len(self.dsem)))
                self.dcnt[dma] = 0
            self.dcnt[dma] += ndma
            me, mk = ("d", dma, self.dcnt[dma]), ("d", dma)
        else:
            me, mk = ("c", eng, o), eng
        for k in reads:
            self.rd.setdefault(k, {})[mk] = me
        for k in writes:
            self.lw[k] = {mk: me}
            self.rd[k] = {}
        self.ops[eng].append(o)
        return o

    def pe(self, fn, r=(), w=()):
        return self.op("pe", fn, r, w)

    def act(self, fn, r=(), w=()):
        return self.op("act", fn, r, w)

    def dve(self, fn, r=(), w=()):
        return self.op("dve", fn, r, w)

    def pool(self, fn, r=(), w=()):
        return self.op("pool", fn, r, w)

    def flush(self):
        lastc = {}
        for e in self.ENGS:
            for o in reversed(self.ops[e]):
                if o.dma is None:
                    lastc[e] = o
                    break
        for o in lastc.values():
            o.signal = True
        for e in self.ENGS:
            for o in self.ops[e]:
                if o.dma is None and o.signal:
                    self.sigcnt[e] += 1
                    o.sigval = self.sigcnt[e]
        dcnt = dict(self.dcnt)
        with self.nc.Block() as block:
            for e in self.ENGS:
                ops = self.ops[e]

                def body(eng, e=e, ops=ops):
                    for o in ops:
                        for ref in o.waits:
                            if ref[0] == "c":
                                eng.wait_ge(self.sem[ref[1]], ref[2].sigval)
                            else:
                                eng.wait_ge(self.dsem[ref[1]], 16 * ref[2])
                        r = o.fn(eng)
                        if o.dma is not None:
                            for ins in (r if isinstance(r, (list, tuple)) else [r]):
                                ins.then_inc(self.dsem[o.dma], 16)
                        elif o.signal:
                            r.then_inc(self.sem[e], 1)
                    for e2, o2 in lastc.items():
                        if e2 != e:
                            eng.wait_ge(self.sem[e2], o2.sigval)
                    for sk, c in dcnt.items():
                        eng.wait_ge(self.dsem[sk], 16 * c)

                getattr(block, self.MAP[e])(body)
        for e in self.ENGS:
            for e2, o2 in lastc.items():
                self.waited[e][e2] = max(self.waited[e].get(e2, -1), o2.pos)
            for sk, c in dcnt.items():
                self.waited[e][("d", sk)] = c
        self._reset()


def tok(tt):
    return (128, tt * 128) if tt < 16 else (NMETA, SEQ)


def build(dbg=False, stop=None, n_mla_groups=8, n_diff_heads=8):
    nc = bass.Bass("TRN2", target_bir_lowering=False)

    def din(name, shape, dtype=F32):
        return nc.dram_tensor(name, shape, dtype, kind="ExternalInput")

    x_d = din("x", [SEQ, D]).ap()
    meta_d = din("meta", [NMETA, D]).ap()
    w_in_d = din("w_in", [D, IN_W]).ap()
    w_uq_d = din("w_uq", [768, 3072]).ap()
    w_ukv_d = din("w_ukv", [512, 4096]).ap()
    w_a_d = din("w_a", [D, D]).ap()
    w_b_d = din("w_b", [D, D]).ap()
    w_o_d = din("w_o", [D, D]).ap()
    gA_d = din("gA", [128, D]).ap()
    gsm_d = din("gsm", [128, G_END]).ap()
    gcol_d = din("gcol", [128, 2]).ap()
    relb_d = din("relb", [32, 8]).ap()
    rope_d = din("ropeT", [128, 17 * 64]).ap()
    oneh_d = din("onehot", [32, NA]).ap()
    out_d = nc.dram_tensor("out", [SEQ, D], F32, kind="ExternalOutput").ap()
    skind = "ExternalOutput" if dbg else "Internal"
    oa_scr = nc.dram_tensor("oa_scr", [D, SEQ], BF16, kind=skind).ap()
    ob_scr = nc.dram_tensor("ob_scr", [D, SEQ], BF16, kind=skind).ap()
    tA_h = nc.dram_tensor("tA", [8, NA], F32, kind="Internal")
    tA_d = tA_h.ap()
    dbg_out = {}
    if dbg:
        dbg_out["d_uT"] = nc.dram_tensor("d_uT", [128, 16 * L], BF16, kind="ExternalOutput").ap()
        dbg_out["d_G"] = nc.dram_tensor("d_G", [128, GW], F32, kind="ExternalOutput").ap()

    w_in_v = w_in_d.rearrange("(k p) n -> p k n", p=128)
    w_uq_v = w_uq_d.rearrange("(k p) n -> p k n", p=128)
    w_ukv_v = w_ukv_d.rearrange("(k p) n -> p k n", p=128)

    with ExitStack() as es:
        S = Sched(nc, es)

        def sb(name, shape, dtype, stack=es):
            return stack.enter_context(nc.sbuf_tensor("s_" + name, shape, dtype))

        def ps(name, shape, dtype, stack):
            return stack.enter_context(nc.psum_tensor("p_" + name, shape, dtype))

        def dma(q, out, in_, r=(), w=(), sem=None):
            return S.op(q, lambda e: e.dma_start(out=out, in_=in_), r, w, dma=sem)

        def wload(dst, src, nk, r=(), w=(), sem=None, nsplit=2):
            step = max(1, nk // nsplit)
            parts = [(k0, min(nk, k0 + step)) for k0 in range(0, nk, step)]

            def fn(e):
                return [e.dma_start(out=dst[:, a:b, :], in_=src[:, a:b, :]) for a, b in parts]
            return S.op("pool", fn, r, w, dma=sem, ndma=len(parts))

        uT = sb("uT", [128, 16, L], BF16)
        ident = sb("ident", [128, 128], BF16)
        identf = sb("identF", [128, 128], F32)
        ones = sb("ones", [128, 128], BF16)
        gsm = sb("gsm", [128, G_END], F32)
        gqp = sb("gqp", [128, 192], F32)
        gsub = sb("gsub", [128, 2], F32)
        ropeT = sb("ropeT", [128, 17, 64], F32)
        neglam = sb("neglam", [128, 1], F32)

        def ukeys(c0, n):
            return [("uT", t) for t in range(c0 // 128, (c0 + n - 1) // 128 + 1)]

        with ExitStack() as sAC:
            cqnT = sb("cqnT", [128, 6, SEQ], BF16, sAC)
            ckvnT = sb("ckvnT", [128, 4, L], BF16, sAC)
            krT = sb("krT", [128, L], BF16, sAC)
            sskr = sb("sskr", [128, 17], F32, sAC)

            with ExitStack() as sAB:
                gA = sb("gA", [128, D], F32, sAB)
                relb = sb("relb", [32, 8], F32, sAB)
                oneh = sb("oneh", [32, NA], F32, sAB)
                tsb = sb("tsb", [8, NA], F32, sAB)
                lamt = sb("lamt", [128, 256], F32, sAB)
                lams = sb("lams", [128, 4], F32, sAB)
                xt = [sb("xt%d" % i, [128, D], F32, sAB) for i in range(2)]
                ub = [sb("ub%d" % i, [128, D], BF16, sAB) for i in range(2)]
                junk = sb("junk", [128, D], BF16, sAB)
                ssx = sb("ssx", [128, 17], F32, sAB)
                lnx = sb("lnx", [128, 17], F32, sAB)
                rsx = sb("rsx", [128, 17], F32, sAB)
                B = [ps("pB%d" % i, [128, 512], F32, sAB) for i in range(4)]
                T = [ps("pT%d" % i, [128, 1024], BF16, sAB) for i in range(2)]

                dma("sp", gsm[:, :], gsm_d, w=["gsm"], sem="k1")
                dma("sp", gsub[:, :], gcol_d, w=["gsub"], sem="k2")
                dma("sp", ropeT[:, :, :], rope_d.rearrange("p (t f) -> p t f", f=64), w=["rope"], sem="k3")
                dma("sp", relb[:, :], relb_d, w=["relb"], sem="k4")
                dma("sp", oneh[:, :], oneh_d, w=["oneh"], sem="k5")
                dma("sp", gA[:, :], gA_d, w=["gA"], sem="k6")
                S.pool(lambda e: e.memset(identf[:, :], 0.0), w=["identf"])
                S.pool(lambda e: e.affine_select(out=identf[:, :], in_=identf[:, :], pattern=[[-1, 128]],
                                                 compare_op=ALU.not_equal, fill=1.0, base=0, channel_multiplier=1),
                       r=["identf"], w=["identf"])
                S.dve(lambda e: e.tensor_copy(out=ident[:, :], in_=identf[:, :]), r=["identf"], w=["ident"])
                S.dve(lambda e: e.memset(ones[:, :], 1.0), w=["ones"])
                S.dve(lambda e: e.memset(krT[64:128, :], 0.0), w=["krTpad"])
                S.dve(lambda e: e.tensor_scalar_mul(out=gA[:, :], in0=gA[:, :], scalar1=math.sqrt(D)), r=["gA"], w=["gA"])
                S.dve(lambda e: e.tensor_scalar_mul(out=gsm[:, G_QA:G_KVA], in0=gsm[:, G_QA:G_KVA], scalar1=math.sqrt(768.0)),
                      r=["gsm"], w=["gsm"])
                S.dve(lambda e: e.tensor_scalar_mul(out=gsm[:, G_KVA:G_MQ], in0=gsm[:, G_KVA:G_MQ], scalar1=math.sqrt(512.0)),
                      r=["gsm"], w=["gsm"])
                S.dve(lambda e: e.scalar_tensor_tensor(out=gqp[:, 0:128], in0=gsm[:, G_MQ:G_MQ + 128], scalar=math.sqrt(192.0),
                                                       in1=gsm[:, G_MK:G_MK + 128], op0=ALU.mult, op1=ALU.mult),
                      r=["gsm"], w=["gqp"])
                S.dve(lambda e: e.tensor_scalar_mul(out=gqp[:, 128:192], in0=gsm[:, G_MQ + 128:G_MQ + 192], scalar1=math.sqrt(192.0)),
                      r=["gsm"], w=["gqp"])
                S.dve(lambda e: e.scalar_tensor_tensor(out=gsm[:, G_DQ:G_DK], in0=gsm[:, G_DQ:G_DK], scalar=math.sqrt(128.0),
                                                       in1=gsm[:, G_DK:G_LAM], op0=ALU.mult, op1=ALU.mult),
                      r=["gsm"], w=["gsm"])
                S.dve(lambda e: e.tensor_scalar_mul(out=gsub[:, :], in0=gsub[:, :], scalar1=(1.0 - LAM_INIT) * 16.0),
                      r=["gsub"], w=["gsub"])
                S.dve(lambda e: e.tensor_tensor(out=lamt[:, 0:128], in0=gsm[:, G_LAM:G_LAM + 128], in1=gsm[:, G_LAM + 128:G_LAM + 256],
                                                op=ALU.mult), r=["gsm"], w=["lamt"])
                S.dve(lambda e: e.tensor_tensor(out=lamt[:, 128:256], in0=gsm[:, G_LAM + 256:G_LAM + 384], in1=gsm[:, G_LAM + 384:G_LAM + 512],
                                                op=ALU.mult), r=["gsm"], w=["lamt"])
                S.dve(lambda e: e.tensor_reduce(out=lams[:, 0:2], in_=lamt[:, :].rearrange("p (a f) -> p a f", a=2), axis=AX.X, op=ALU.add),
                      r=["lamt"], w=["lams"])
                S.act(lambda e: e.activation(out=lams[:, 2:4], in_=lams[:, 0:2], func=AF.Exp), r=["lams"], w=["lams2"])
                S.dve(lambda e: e.tensor_scalar(out=neglam[:, :], in0=lams[:, 3:4], scalar1=lams[:, 2:3], scalar2=-LAM_INIT,
                                                op0=ALU.subtract, op1=ALU.add), r=["lams2"], w=["neglam"])
                for j, (a0, a1) in enumerate(((0, 512), (512, 1024), (1024, NA))):
                    S.pe(lambda e, a0=a0, a1=a1, j=j: e.matmul(B[j][0:8, 0:a1 - a0], lhsT=relb[:, :], rhs=oneh[:, a0:a1], start=True, stop=True),
                         r=["relb", "oneh"], w=["B%d" % j])
                    S.dve(lambda e, a0=a0, a1=a1, j=j: e.tensor_copy(out=tsb[:, a0:a1], in_=B[j][0:8, 0:a1 - a0]), r=["B%d" % j], w=["tsb"])
                dma("sp", tA_d, tsb[:, :], r=["tsb"], w=["tA"], sem="c1")

                for tt in range(17):
                    P, c0 = tok(tt)
                    b = tt % 2
                    src = x_d[tt * 128:(tt + 1) * 128, :] if tt < 16 else meta_d
                    dma("sp", xt[b][:P, :], src, w=[("xt", b)], sem="xt%d" % b)
                    S.act(lambda e, P=P, b=b, tt=tt: e.activation(out=junk[:P, :], in_=xt[b][:P, :], func=AF.Square, accum_out=ssx[:P, tt:tt + 1]),
                          r=[("xt", b)], w=["junk", ("ssx", tt)])
                    S.act(lambda e, P=P, tt=tt: e.activation(out=lnx[:P, tt:tt + 1], in_=ssx[:P, tt:tt + 1], func=AF.Ln, bias=D * EPS, scale=1.0),
                          r=[("ssx", tt)], w=[("lnx", tt)])
                    S.act(lambda e, P=P, tt=tt: e.activation(out=rsx[:P, tt:tt + 1], in_=lnx[:P, tt:tt + 1], func=AF.Exp, scale=-0.5),
                          r=[("lnx", tt)], w=[("rsx", tt)])
                    S.dve(lambda e, P=P, b=b, tt=tt: e.scalar_tensor_tensor(out=ub[b][:P, :], in0=xt[b][:P, :], scalar=rsx[:P, tt:tt + 1],
                                                                          in1=gA[:P, :], op0=ALU.mult, op1=ALU.mult),
                          r=[("xt", b), ("rsx", tt), "gA"], w=[("ub", b)])
                    for half in range(2):
                        for k in range(8):
                            kk = half * 8 + k
                            S.pe(lambda e, P=P, b=b, k=k, kk=kk, half=half: e.transpose(
                                out=T[half][:, k * 128:k * 128 + P], in_=ub[b][:P, kk * 128:(kk + 1) * 128], identity=ident[:P, :P]),
                                r=[("ub", b), "ident"], w=[("T", half)])
                        cp = (lambda e, P=P, c0=c0, half=half: e.tensor_copy(
                            out=uT[:, half * 8:(half + 1) * 8, c0:c0 + P],
                            in_=T[half][:, :].rearrange("p (k t) -> p k t", k=8)[:, :, 0:P]))
                        ca = (lambda e, P=P, c0=c0, half=half: e.activation(
                            out=uT[:, half * 8:(half + 1) * 8, c0:c0 + P],
                            in_=T[half][:, :].rearrange("p (k t) -> p k t", k=8)[:, :, 0:P], func=AF.Copy))
                        if half == 0:
                            S.dve(cp, r=[("T", half)], w=[("uT", tt)])
                        else:
                            S.act(ca, r=[("T", half)], w=[("uT", tt)])

                if dbg:
                    dma("sp", dbg_out["d_uT"], uT[:, :, :].rearrange("p k t -> p (k t)"), r=[("uT", t) for t in range(17)], w=["dbg"], sem="dbg")
                S.flush()
            if stop == "A":
                return nc
            with ExitStack() as sAB:
                wB = sb("wB", [128, 16, 1344], BF16, sAB)
                cB = [sb("cB%d" % i, [128, 1344], F32, sAB) for i in range(2)]
                junk = sb("junkB", [128, 1344], BF16, sAB)
                ss3 = sb("ss3", [128, 17, 4], F32, sAB)
                ln3 = sb("ln3", [128, 17, 2], F32, sAB)
                rs3 = sb("rs3", [128, 17, 2], F32, sAB)
                cqb = [sb("cqb%d" % i, [128, 768], BF16, sAB) for i in range(2)]
                ckvb = [sb("ckvb%d" % i, [128, 512], BF16, sAB) for i in range(2)]
                krg = sb("krg", [128, 64], F32, sAB)
                rt = sb("rt", [128, 4, 32], F32, sAB)
                krb = [sb("krb%d" % i, [128, 64], BF16, sAB) for i in range(2)]
                B = [ps("bB%d" % i, [128, 512], F32, sAB) for i in range(4)]
                T = [ps("bT%d" % i, [128, 1024], BF16, sAB) for i in range(2)]
                wload(wB, w_in_v[:, :, 0:1344], 16, w=["wB"], sem="wB", nsplit=4)
                groups = ((0, 384), (384, 768), (768, 1280), (1280, 1344))
                deferred = []
                for tt in range(17):
                    P, c0 = tok(tt)
                    b = tt % 2
                    for g, (a0, a1) in enumerate(groups):
                        if tt == 16 and g < 2:
                            continue
                        for k in range(16):
                            S.pe(lambda e, P=P, c0=c0, g=g, a0=a0, a1=a1, k=k: e.matmul(
                                B[g][:P, 0:a1 - a0], lhsT=uT[:, k, c0:c0 + P], rhs=wB[:, k, a0:a1], start=(k == 0), stop=(k == 15)),
                                r=[("uT", tt), "wB"], w=["B%d" % g])
                        S.act(lambda e, P=P, g=g, a0=a0, a1=a1, b=b: e.activation(out=cB[b][:P, a0:a1], in_=B[g][:P, 0:a1 - a0], func=AF.Copy),
                              r=["B%d" % g], w=[("cB", b, g)])
                    for f in deferred:
                        f()
                    deferred = []
                    if tt < 16:
                        S.act(lambda e, P=P, b=b, tt=tt: e.activation(out=junk[:P, 0:768], in_=cB[b][:P, 0:768], func=AF.Square,
                                                                     accum_out=ss3[:P, tt, 0:1]),
                              r=[("cB", b, 0), ("cB", b, 1)], w=[("junk", 0), ("ss3", tt, 0)])
                        S.act(lambda e, P=P, tt=tt: e.activation(out=ln3[:P, tt, 0:1], in_=ss3[:P, tt, 0:1], func=AF.Ln, bias=768 * EPS, scale=1.0),
                              r=[("ss3", tt, 0)], w=[("ln3", tt, 0)])
                        S.act(lambda e, P=P, tt=tt: e.activation(out=rs3[:P, tt, 0:1], in_=ln3[:P, tt, 0:1], func=AF.Exp, scale=-0.5),
                              r=[("ln3", tt, 0)], w=[("rs3", tt, 0)])
                    S.act(lambda e, P=P, b=b, tt=tt: e.activation(out=junk[:P, 768:1280], in_=cB[b][:P, 768:1280], func=AF.Square,
                                                                 accum_out=ss3[:P, tt, 1:2]),
                          r=[("cB", b, 2)], w=[("junk", 1), ("ss3", tt, 1)])
                    S.act(lambda e, P=P, b=b, tt=tt: e.activation(out=junk[:P, 1280:1344], in_=cB[b][:P, 1280:1344], func=AF.Square,
                                                                 accum_out=ss3[:P, tt, 2:3]),
                          r=[("cB", b, 3)], w=[("junk", 2), ("ss3", tt, 2)])
                    S.act(lambda e, P=P, tt=tt: e.activation(out=ln3[:P, tt, 1:2], in_=ss3[:P, tt, 1:2], func=AF.Ln, bias=512 * EPS, scale=1.0),
                          r=[("ss3", tt, 1)], w=[("ln3", tt, 1)])
                    S.act(lambda e, P=P, tt=tt: e.activation(out=rs3[:P, tt, 1:2], in_=ln3[:P, tt, 1:2], func=AF.Exp, scale=-0.5),
                          r=[("ln3", tt, 1)], w=[("rs3", tt, 1)])
                    S.dve(lambda e, P=P, tt=tt: e.tensor_scalar_add(out=sskr[:P, tt:tt + 1], in0=ss3[:P, tt, 2:3], scalar1=192 * EPS),
                          r=[("ss3", tt, 2)], w=[("sskr", tt)])
                    if tt < 16:
                        S.dve(lambda e, P=P, b=b, tt=tt: e.scalar_tensor_tensor(out=cqb[b][:P, :], in0=cB[b][:P, 0:768], scalar=rs3[:P, tt, 0:1],
                                                                              in1=gsm[:P, G_QA:G_KVA], op0=ALU.mult, op1=ALU.mult),
                              r=[("cB", b, 0), ("cB", b, 1), ("rs3", tt, 0), "gsm"], w=[("cqb", b)])
                    S.dve(lambda e, P=P, b=b, tt=tt: e.scalar_tensor_tensor(out=ckvb[b][:P, :], in0=cB[b][:P, 768:1280], scalar=rs3[:P, tt, 1:2],
                                                                          in1=gsm[:P, G_KVA:G_MQ], op0=ALU.mult, op1=ALU.mult),
                          r=[("cB", b, 2), ("rs3", tt, 1), "gsm"], w=[("ckvb", b)])
                    S.dve(lambda e, P=P, b=b: e.tensor_tensor(out=krg[:P, :], in0=cB[b][:P, 1280:1344], in1=gsm[:P, G_MK + 128:G_MK + 192], op=ALU.mult),
                          r=[("cB", b, 3), "gsm"], w=["krg"])
                    cosv = lambda P=P, tt=tt: ropeT[:P, tt, 0:32]
                    sinv = lambda P=P, tt=tt: ropeT[:P, tt, 32:64]
                    S.dve(lambda e, P=P, tt=tt: e.tensor_tensor(out=rt[:P, 0, :], in0=krg[:P, 0:32], in1=ropeT[:P, tt, 0:32], op=ALU.mult),
                          r=["krg", "rope"], w=[("rt", 0)])
                    S.dve(lambda e, P=P, tt=tt: e.tensor_tensor(out=rt[:P, 1, :], in0=krg[:P, 32:64], in1=ropeT[:P, tt, 32:64], op=ALU.mult),
                          r=["krg", "rope"], w=[("rt", 1)])
                    S.dve(lambda e, P=P, tt=tt: e.tensor_tensor(out=rt[:P, 2, :], in0=krg[:P, 32:64], in1=ropeT[:P, tt, 0:32], op=ALU.mult),
                          r=["krg", "rope"], w=[("rt", 2)])
                    S.dve(lambda e, P=P, tt=tt: e.tensor_tensor(out=rt[:P, 3, :], in0=krg[:P, 0:32], in1=ropeT[:P, tt, 32:64], op=ALU.mult),
                          r=["krg", "rope"], w=[("rt", 3)])
                    S.dve(lambda e, P=P, b=b: e.tensor_tensor(out=krb[b][:P, 0:32], in0=rt[:P, 0, :], in1=rt[:P, 1, :], op=ALU.subtract),
                          r=[("rt", 0), ("rt", 1)], w=[("krb", b)])
                    S.dve(lambda e, P=P, b=b: e.tensor_tensor(out=krb[b][:P, 32:64], in0=rt[:P, 2, :], in1=rt[:P, 3, :], op=ALU.add),
                          r=[("rt", 2), ("rt", 3)], w=[("krb", b)])
                    def later(tt=tt, P=P, c0=c0, b=b):
                        if tt < 16:
                            for k in range(6):
                                S.pe(lambda e, P=P, b=b, k=k: e.transpose(out=T[0][:, k * 128:k * 128 + P], in_=cqb[b][:P, k * 128:(k + 1) * 128],
                                                                          identity=ident[:P, :P]), r=[("cqb", b), "ident"], w=[("T", 0)])
                            S.dve(lambda e, P=P, c0=c0: e.tensor_copy(out=cqnT[:, 0:6, c0:c0 + P],
                                                                      in_=T[0][:, 0:768].rearrange("p (k t) -> p k t", k=6)[:, :, 0:P]),
                                  r=[("T", 0)], w=[("cqnT", tt)])
                        for k in range(4):
                            S.pe(lambda e, P=P, b=b, k=k: e.transpose(out=T[1][:, k * 128:k * 128 + P], in_=ckvb[b][:P, k * 128:(k + 1) * 128],
                                                                      identity=ident[:P, :P]), r=[("ckvb", b), "ident"], w=[("T", 1)])
                        S.pe(lambda e, P=P, b=b: e.transpose(out=T[1][0:64, 512:512 + P], in_=krb[b][:P, 0:64], identity=ident[:P, :P]),
                             r=[("krb", b), "ident"], w=[("T", 1)])
                        S.act(lambda e, P=P, c0=c0: e.activation(out=ckvnT[:, 0:4, c0:c0 + P],
                                                                 in_=T[1][:, 0:512].rearrange("p (k t) -> p k t", k=4)[:, :, 0:P], func=AF.Copy),
                              r=[("T", 1)], w=[("ckvnT", tt)])
                        S.act(lambda e, P=P, c0=c0: e.activation(out=krT[0:64, c0:c0 + P], in_=T[1][0:64, 512:512 + P], func=AF.Copy),
                              r=[("T", 1)], w=[("krT", tt)])
                    deferred.append(later)
                for f in deferred:
                    f()
                S.flush()
            if stop == "B":
                return nc

            with ExitStack() as sC:
                wq = [sb("wq%d" % i, [128, 6, 384], BF16, sC) for i in range(2)]
                wkv = [sb("wkv%d" % i, [128, 4, 512], BF16, sC) for i in range(2)]
                wz = [sb("wz%d" % i, [128, 16, 128], BF16, sC) for i in range(2)]
                qTn = sb("qTn", [128, 2, SEQ], BF16, sC)
                qTr = sb("qTr", [128, 2, SEQ], BF16, sC)
                kTn = sb("kTn", [128, 2, L], BF16, sC)
                Vt = sb("Vt", [128, 17, 2, 128], BF16, sC)
                rstdk = sb("rstdk", [128, 17, 2], F32, sC)
                junkq = sb("junkq", [128, 4, 192], BF16, sC)
                ssq = sb("ssq", [128, 17, 4], F32, sC)
                lnq = sb("lnq", [128, 17, 4], F32, sC)
                rq = sb("rq", [128, 17, 2], F32, sC)
                qb_ = [sb("qb%d" % i, [128, 2, 192], BF16, sC) for i in range(3)]
                qr = [sb("qr%d" % i, [128, 2, 64], F32, sC) for i in range(3)]
                rt4 = sb("rt4", [128, 4, 2, 32], F32, sC)
                NPT = 4
                PT = [sb("PT%d" % i, [128, 512], BF16, sC) for i in range(NPT)]
                ez = sb("ez", [128, 512], F32, sC)
                sz = sb("sz", [128, 512], F32, sC)
                rec = sb("rec", [128, 512], F32, sC)
                og = [sb("og%d" % i, [128, 512], BF16, sC) for i in range(2)]
                B = [ps("qB%d" % i, [128, 512], F32, sC) for i in range(6)]
                Tv = [ps("qT%d" % i, [128, 1024], BF16, sC) for i in range(2)]

                def load_group(g):
                    s = g % 2
                    h0 = 2 * g
                    wload(wq[s], w_uq_v[:, :, h0 * 192:(h0 + 2) * 192], 6, w=[("wq", s)], sem="wq%d" % s)
                    wload(wkv[s], w_ukv_v[:, :, h0 * 256:(h0 + 2) * 256], 4, w=[("wkv", s)], sem="wkv%d" % s)

                S.dve(lambda e: e.memset(qTr[64:128, :, :], 0.0), w=["qTrpad"])
                load_group(0)
                og_i = 0
                for g in range(n_mla_groups):
                    s = g % 2
                    h0 = 2 * g
                    if g + 1 < n_mla_groups:
                        load_group(g + 1)
                    for hh in range(2):
                        if not (KDBG & 8):
                            wload(wz[hh], w_in_v[:, :, OFF_ZA + (h0 + hh) * 128:OFF_ZA + (h0 + hh + 1) * 128], 16, w=[("wz", hh)], sem="wz%d" % hh)
                    deferred = []
                    for tt in range(17):
                        P, c0 = tok(tt)
                        b = tt % 2
                        b3 = tt % 3
                        qbk, kvbk = tt % 2, 2 + tt % 2
                        if tt < 16:
                            for k in range(6):
                                S.pe(lambda e, P=P, c0=c0, k=k, s=s, qbk=qbk: e.matmul(B[qbk][:P, 0:384], lhsT=cqnT[:, k, c0:c0 + P], rhs=wq[s][:, k, :],
                                                                                          start=(k == 0), stop=(k == 5)),
                                     r=[("wq", s)], w=["B%d" % qbk])
                        for k in range(4):
                            S.pe(lambda e, P=P, c0=c0, k=k, s=s, kvbk=kvbk: e.matmul(B[kvbk][:P, 0:512], lhsT=ckvnT[:, k, c0:c0 + P], rhs=wkv[s][:, k, :],
                                                                                        start=(k == 0), stop=(k == 3)),
                                 r=[("wkv", s)], w=["B%d" % kvbk])
                        while deferred and deferred[0][0] <= tt - 2:
                            deferred.pop(0)[1]()
                        if tt < 16:
                            for hh in range(2):
                                S.act(lambda e, P=P, hh=hh, tt=tt, qbk=qbk: e.activation(out=junkq[:P, hh, :], in_=B[qbk][:P, hh * 192:(hh + 1) * 192], func=AF.Square,
                                                                                         accum_out=ssq[:P, tt, hh:hh + 1]),
                                      r=["B%d" % qbk], w=[("junkq", hh), ("ssq", tt, hh)])
                        S.act(lambda e, P=P, tt=tt, kvbk=kvbk: e.activation(out=Vt[:P, tt, :, :], in_=B[kvbk][:P, :].rearrange("p (h f) -> p h f", h=2)[:, :, 128:256],
                                                                            func=AF.Copy), r=["B%d" % kvbk], w=[("Vt", tt)])
                        for hh in range(2):
                            S.act(lambda e, P=P, hh=hh, tt=tt, kvbk=kvbk: e.activation(out=junkq[:P, 2 + hh, 0:128], in_=B[kvbk][:P, hh * 256:hh * 256 + 128], func=AF.Square,
                                                                                       accum_out=ssq[:P, tt, 2 + hh:3 + hh]),
                                  r=["B%d" % kvbk], w=[("junkq", 2 + hh), ("ssk", tt, hh)])
                        if tt < 16:
                            S.act(lambda e, P=P, tt=tt: e.activation(out=lnq[:P, tt, 0:2], in_=ssq[:P, tt, 0:2], func=AF.Ln, bias=192 * EPS, scale=1.0),
                                  r=[("ssq", tt, 0), ("ssq", tt, 1)], w=[("lnq", tt)])
                        S.act(lambda e, P=P, tt=tt: e.activation(out=lnq[:P, tt, 2:4], in_=ssq[:P, tt, 2:4], func=AF.Ln, bias=sskr[:P, tt:tt + 1], scale=1.0),
                              r=[("ssk", tt, 0), ("ssk", tt, 1)], w=[("lnk", tt)])
                        if tt < 16:
                            S.act(lambda e, P=P, tt=tt: e.activation(out=rq[:P, tt, 0:2], in_=lnq[:P, tt, 0:2], func=AF.Exp, scale=-0.5),
                                  r=[("lnq", tt)], w=[("rq", tt)])
                        S.act(lambda e, P=P, tt=tt: e.activation(out=rstdk[:P, tt, 0:2], in_=lnq[:P, tt, 2:4], func=AF.Exp, scale=-0.5),
                              r=[("lnk", tt)], w=[("rstdk", tt)])
                        if tt < 16:
                            for hh in range(2):
                                S.dve(lambda e, P=P, hh=hh, tt=tt, b3=b3, qbk=qbk: e.scalar_tensor_tensor(
                                    out=qb_[b3][:P, hh, 0:128], in0=B[qbk][:P, hh * 192:hh * 192 + 128], scalar=rq[:P, tt, hh:hh + 1],
                                    in1=gqp[:P, 0:128], op0=ALU.mult, op1=ALU.mult),
                                    r=["B%d" % qbk, ("rq", tt), "gqp"], w=[("qb", b3)])
                                S.dve(lambda e, P=P, hh=hh, tt=tt, b3=b3, qbk=qbk: e.scalar_tensor_tensor(
                                    out=qr[b3][:P, hh, :], in0=B[qbk][:P, hh * 192 + 128:hh * 192 + 192], scalar=rq[:P, tt, hh:hh + 1],
                                    in1=gqp[:P, 128:192], op0=ALU.mult, op1=ALU.mult),
                                    r=["B%d" % qbk, ("rq", tt), "gqp"], w=[("qr", b3)])
                            cb = lambda P, tt: ropeT[:P, tt, 0:32].unsqueeze(1).to_broadcast([P, 2, 32])
                            sn = lambda P, tt: ropeT[:P, tt, 32:64].unsqueeze(1).to_broadcast([P, 2, 32])
                            S.pool(lambda e, P=P, tt=tt, b3=b3: e.tensor_tensor(out=rt4[:P, 0, :, :], in0=qr[b3][:P, :, 0:32], in1=cb(P, tt), op=ALU.mult),
                                  r=[("qr", b3)], w=[("rt4", 0)])
                            S.pool(lambda e, P=P, tt=tt, b3=b3: e.tensor_tensor(out=rt4[:P, 1, :, :], in0=qr[b3][:P, :, 32:64], in1=sn(P, tt), op=ALU.mult),
                                  r=[("qr", b3)], w=[("rt4", 1)])
                            S.pool(lambda e, P=P, tt=tt, b3=b3: e.tensor_tensor(out=rt4[:P, 2, :, :], in0=qr[b3][:P, :, 32:64], in1=cb(P, tt), op=ALU.mult),
                                  r=[("qr", b3)], w=[("rt4", 2)])
                            S.pool(lambda e, P=P, tt=tt, b3=b3: e.tensor_tensor(out=rt4[:P, 3, :, :], in0=qr[b3][:P, :, 0:32], in1=sn(P, tt), op=ALU.mult),
                                  r=[("qr", b3)], w=[("rt4", 3)])
                            S.pool(lambda e, P=P, b3=b3: e.tensor_tensor(out=qb_[b3][:P, :, 128:160], in0=rt4[:P, 0, :, :], in1=rt4[:P, 1, :, :], op=ALU.subtract),
                                  r=[("rt4", 0), ("rt4", 1)], w=[("qb", b3)])
                            S.pool(lambda e, P=P, b3=b3: e.tensor_tensor(out=qb_[b3][:P, :, 160:192], in0=rt4[:P, 2, :, :], in1=rt4[:P, 3, :, :], op=ALU.add),
                                  r=[("rt4", 2), ("rt4", 3)], w=[("qb", b3)])

                            def later(tt=tt, P=P, c0=c0, b=b, b3=b3):
                                for hh in range(2):
                                    S.pe(lambda e, hh=hh: e.transpose(out=Tv[b][:, hh * 128:hh * 128 + P], in_=qb_[b3][:P, hh, 0:128], identity=ident[:P, :P]),
                                         r=[("qb", b3)], w=["B%d" % (6 + b)])
                                    S.pe(lambda e, hh=hh: e.transpose(out=Tv[b][0:64, (2 + hh) * 128:(2 + hh) * 128 + P], in_=qb_[b3][:P, hh, 128:192],
                                                                      identity=ident[:P, :P]), r=[("qb", b3)], w=["B%d" % (6 + b)])
                                S.dve(lambda e: e.tensor_copy(out=qTn[:, :, c0:c0 + P],
                                                              in_=Tv[b][:, 0:256].rearrange("p (h t) -> p h t", h=2)[:, :, 0:P]),
                                      r=["B%d" % (6 + b)], w=[("qTn", tt)])
                                if True:
                                    S.dve(lambda e: e.tensor_copy(out=qTr[0:64, :, c0:c0 + P],
                                                                  in_=Tv[b][0:64, 256:512].rearrange("p (h t) -> p h t", h=2)[:, :, 0:P]),
                                          r=["B%d" % (6 + b)], w=[("qTr", tt)])
                                else:
                                    S.act(lambda e: e.activation(out=qTr[0:64, :, c0:c0 + P],
                                                                 in_=Tv[b][0:64, 256:512].rearrange("p (h t) -> p h t", h=2)[:, :, 0:P], func=AF.Copy),
                                          r=["B%d" % (6 + b)], w=[("qTr", tt)])
                            deferred.append((tt, later))
                    blocks = [(0, 512), (512, 512), (1024, 512), (1536, 512), (2048, 16)]
                    bi = 0
                    for (c0, n) in blocks:
                        for hh in range(2):
                            bank = 4 + (bi % 2)
                            for k in range(4):
                                S.pe(lambda e, c0=c0, n=n, hh=hh, k=k, s=s, bank=bank: e.matmul(
                                    B[bank][:, 0:n], lhsT=wkv[s][:, k, hh * 256:hh * 256 + 128], rhs=ckvnT[:, k, c0:c0 + n], start=(k == 0), stop=(k == 3)),
                                    r=[("wkv", s)], w=["B%d" % bank])
                            if bi % 2 == 0:
                                S.dve(lambda e, c0=c0, n=n, hh=hh, bank=bank: e.tensor_copy(out=kTn[:, hh, c0:c0 + n], in_=B[bank][:, 0:n]),
                                      r=["B%d" % bank], w=[("kTn", hh, c0)])
                            else:
                                S.act(lambda e, c0=c0, n=n, hh=hh, bank=bank: e.activation(out=kTn[:, hh, c0:c0 + n], in_=B[bank][:, 0:n], func=AF.Copy),
                                      r=["B%d" % bank], w=[("kTn", hh, c0)])
                            bi += 1
                    while deferred:
                        deferred.pop(0)[1]()
                    for hh in range(2):
                        h = h0 + hh
                        if KDBG & 8:
                            wload(wz[hh], w_in_v[:, :, OFF_ZA + h * 128:OFF_ZA + (h + 1) * 128], 16, w=[("wz", hh)], sem="wz%d" % hh)
                        for qb in range(4):
                            q0 = qb * 512
                            rq_keys = [("qTn", t) for t in range(4 * qb, 4 * qb + 4)] + [("qTr", t) for t in range(4 * qb, 4 * qb + 4)]
                            for k in range(16):
                                S.pe(lambda e, k=k, hh=hh, q0=q0: e.matmul(B[5][:, :], lhsT=wz[hh][:, k, :], rhs=uT[:, k, q0:q0 + 512],
                                                                           start=(k == 0), stop=(k == 15)),
                                     r=[("wz", hh)], w=["B5"])
                            S.act(lambda e: e.activation(out=ez[:, :], in_=B[5][:, :], func=AF.Exp, scale=-1.0), r=["B5"], w=["ez"])
                            if KDBG & 1:
                                S.dve(lambda e: e.tensor_scalar_add(out=ez[:, :], in0=ez[:, :], scalar1=1.0), r=["ez"], w=["ez"])
                                S.dve(lambda e: e.reciprocal(out=ez[:, :], in_=ez[:, :]), r=["ez"], w=["ez"])
                            else:
                                S.act(lambda e: e.activation(out=ez[:, :], in_=ez[:, :], func=AF.Ln, bias=1.0, scale=1.0), r=["ez"], w=["ez"])
                                S.act(lambda e: e.activation(out=ez[:, :], in_=ez[:, :], func=AF.Exp, scale=-1.0), r=["ez"], w=["ez"])
                            S.dve(lambda e: e.tensor_tensor(out=sz[:, :], in0=B[5][:, :], in1=ez[:, :], op=ALU.mult), r=["B5", "ez"], w=["sz"])

                            def S_mm(c, hh=hh, q0=q0, rq_keys=rq_keys):
                                P, c0 = tok(c)
                                bank = c % 3
                                S.pe(lambda e: e.matmul(B[bank][:P, :], lhsT=kTn[:, hh, c0:c0 + P], rhs=qTn[:, hh, q0:q0 + 512], start=True, stop=False),
                                     r=rq_keys + [("kTn", hh, (c0 // 512) * 512)], w=["B%d" % bank])
                                S.pe(lambda e: e.matmul(B[bank][:P, :], lhsT=krT[:, c0:c0 + P], rhs=qTr[:, hh, q0:q0 + 512], start=False, stop=True),
                                     r=rq_keys + ["qTrpad"], w=["B%d" % bank])

                            def EXPc(c, hh=hh):
                                P, c0 = tok(c)
                                bank = c % 3
                                S.act(lambda e: e.activation(out=PT[c % NPT][:P, :], in_=B[bank][:P, :], func=AF.Exp, scale=rstdk[:P, c, hh:hh + 1]),
                                      r=["B%d" % bank, ("rstdk", c)], w=[("PT", c % NPT)])

                            def PVc(c, hh=hh):
                                P, c0 = tok(c)
                                S.pe(lambda e: e.matmul(B[3][:, :], lhsT=Vt[:P, c, hh, :], rhs=PT[c % NPT][:P, :], start=(c == 0), stop=(c == 16)),
                                     r=[("PT", c % NPT), ("Vt", c)], w=["B3"])
                                S.pe(lambda e: e.matmul(B[4][:, :], lhsT=ones[:P, :], rhs=PT[c % NPT][:P, :], start=(c == 0), stop=(c == 16)),
                                     r=[("PT", c % NPT)], w=["B4"])

                            S_mm(0)
                            S_mm(1)
                            for c in range(17):
                                EXPc(c)
                                if c + 2 < 17:
                                    S_mm(c + 2)
                                PVc(c)
                            oi = og_i % 2
                            og_i += 1
                            if KDBG & 1:
                                S.dve(lambda e: e.reciprocal(out=rec[:, :], in_=B[4][:, :]), r=["B4"], w=["rec"])
                            else:
                                S.act(lambda e: e.activation(out=rec[:, :], in_=B[4][:, :], func=AF.Ln), r=["B4"], w=["rec"])
                                S.act(lambda e: e.activation(out=rec[:, :], in_=rec[:, :], func=AF.Exp, scale=-1.0), r=["rec"], w=["rec"])
                            S.dve(lambda e: e.tensor_tensor(out=rec[:, :], in0=rec[:, :], in1=sz[:, :], op=ALU.mult), r=["rec", "sz"], w=["rec"])
                            S.dve(lambda e, oi=oi: e.tensor_tensor(out=og[oi][:, :], in0=B[3][:, :], in1=rec[:, :], op=ALU.mult), r=["B3", "rec"], w=[("og", oi)])
                            dma("sp", oa_scr[h * 128:(h + 1) * 128, q0:q0 + 512], og[oi][:, :], r=[("og", oi)], w=[("oa", h, qb)], sem="og%d" % oi)
                S.flush()
        if stop == "C":
            return nc

        with ExitStack() as sD:
            wr = [sb("wr%d" % i, [128, 16, 256], BF16, sD) for i in range(4)]
            Jf = sb("Jf", [128, 128], F32, sD)
            qdT = sb("qdT", [128, 2, SEQ], BF16, sD)
            kdT = sb("kdT", [128, 2, L], BF16, sD)
            Vd = sb("Vd", [128, 17, 256], BF16, sD)
            Hk = sb("Hk", [128, GW], F32, sD)
            G = sb("G", [128, GW], F32, sD)
            junkd = sb("junkd", [128, 2, 128], BF16, sD)
            ss4 = sb("ss4", [128, 17, 4], F32, sD)
            ln4 = sb("ln4", [128, 17, 4], F32, sD)
            r4 = sb("r4", [128, 17, 4], F32, sD)
            qdb = [sb("qdb%d" % i, [128, 256], F32, sD) for i in range(2)]
            NSB = 3
            sbias = [sb("sbias%d" % i, [128, 512], F32, sD) for i in range(NSB)]
            NPT = 5
            PT = [sb("PTd%d" % i, [128, 512], BF16, sD) for i in range(NPT)]
            szb = [sb("szb%d" % i, [128, 2, 512], F32, sD) for i in range(2)]
            sraw = sb("sraw", [128, 512], F32, sD)
            ogb_cnt = [0]
            ezb = [sb("ezb%d" % i, [128, 512], F32, sD) for i in range(2)]
            oraw = sb("oraw", [128, 2, 512], F32, sD)
            recd = sb("recd", [128, 512], F32, sD)
            t1 = sb("t1", [128, 2, 512], F32, sD)
            t2 = sb("t2", [128, 512], F32, sD)
            od = sb("od", [128, 2, 512], F32, sD)
            sqb = sb("sqb", [128, 2, 512], BF16, sD)
            rsb = sb("rsb", [128, 512], F32, sD)
            onn = sb("onn", [128, 512], F32, sD)
            ogb = [sb("ogb%d" % i, [128, 2, 512], BF16, sD) for i in range(2)]
            B = [ps("dB%d" % i, [128, 512], F32, sD) for i in range(8)]
            Tv = [B[6], B[7]]
            SBK = (0, 1, 5)
            LA = 2

            WOFF = (OFF_QD, OFF_KD, OFF_VD, OFF_ZB)

            def load_w(h, i):
                wload(wr[i], w_in_v[:, :, WOFF[i] + h * 256:WOFF[i] + (h + 1) * 256], 16, w=[("wr", i)], sem="wr%d" % i)

            S.pool(lambda e: e.memset(Jf[:, :], 0.0), w=["Jf"])
            S.pool(lambda e: e.affine_select(out=Jf[:, :], in_=Jf[:, :], pattern=[[1, 128]],
                                             compare_op=ALU.not_equal, fill=1.0, base=-127, channel_multiplier=1),
                   r=["Jf"], w=["Jf"])
            for i in range(4):
                load_w(0, i)
            ogb_i = 0
            pendA, pendB = [None], [None]
            for h in range(n_diff_heads):
                for st in range(3):
                    deferred = []
                    for tt in range(17):
                        if st == 0 and tt == 16:
                            continue
                        P, c0 = tok(tt)
                        b = tt % 2
                        bank = tt % 2
                        for k in range(16):
                            S.pe(lambda e, P=P, c0=c0, k=k, bank=bank, st=st: e.matmul(B[bank][:P, 0:256], lhsT=uT[:, k, c0:c0 + P], rhs=wr[st][:, k, :],
                                                                                         start=(k == 0), stop=(k == 15)),
                                 r=[("wr", st)], w=["B%d" % bank])
                        for f in deferred:
                            f()
                        deferred = []
                        if st == 0 and tt == 2 and pendA[0] is not None:
                            pendA[0]()
                            pendA[0] = None
                        if st == 0 and tt == 9 and pendB[0] is not None:
                            pendB[0](5)
                            pendB[0] = None
                        if st == 2:
                            if tt % 2 == 0:
                                S.act(lambda e, P=P, tt=tt, bank=bank: e.activation(out=Vd[:P, tt, :], in_=B[bank][:P, 0:256], func=AF.Copy), r=["B%d" % bank], w=[("Vd", tt)])
                            else:
                                S.dve(lambda e, P=P, tt=tt, bank=bank: e.tensor_copy(out=Vd[:P, tt, :], in_=B[bank][:P, 0:256]), r=["B%d" % bank], w=[("Vd", tt)])
                            continue
                        for m in range(2):
                            S.act(lambda e, P=P, tt=tt, bank=bank, m=m, st=st: e.activation(out=junkd[:P, m, :], in_=B[bank][:P, m * 128:(m + 1) * 128], func=AF.Square,
                                                                                            accum_out=ss4[:P, tt, 2 * st + m:2 * st + m + 1]),
                                  r=["B%d" % bank], w=[("junkd", m), ("ss4", tt, st, m)])
                        S.act(lambda e, P=P, tt=tt, st=st: e.activation(out=ln4[:P, tt, 2 * st:2 * st + 2], in_=ss4[:P, tt, 2 * st:2 * st + 2], func=AF.Ln, bias=128 * EPS, scale=1.0),
                              r=[("ss4", tt, st, 0), ("ss4", tt, st, 1)], w=[("ln4", tt, st)])
                        S.act(lambda e, P=P, tt=tt, st=st: e.activation(out=r4[:P, tt, 2 * st:2 * st + 2], in_=ln4[:P, tt, 2 * st:2 * st + 2], func=AF.Exp, scale=-0.5),
                              r=[("ln4", tt, st)], w=[("r4", tt, st)])
                        for m in range(2):
                            if st == 0:
                                S.dve(lambda e, P=P, tt=tt, m=m, b=b, bank=bank: e.scalar_tensor_tensor(out=qdb[b][:P, m * 128:(m + 1) * 128], in0=B[bank][:P, m * 128:(m + 1) * 128],
                                                                                                   scalar=r4[:P, tt, m:m + 1], in1=gsm[:P, G_DQ:G_DK], op0=ALU.mult, op1=ALU.mult),
                                      r=["B%d" % bank, ("r4", tt, st)], w=[("qdb", b)])
                            else:
                                S.dve(lambda e, P=P, tt=tt, m=m, b=b, bank=bank: e.tensor_scalar_mul(out=qdb[b][:P, m * 128:(m + 1) * 128], in0=B[bank][:P, m * 128:(m + 1) * 128],
                                                                                                scalar1=r4[:P, tt, 2 + m:3 + m]),
                                      r=["B%d" % bank, ("r4", tt, st)], w=[("qdb", b)])

                        def later(tt=tt, P=P, c0=c0, b=b, st=st):
                            for m in range(2):
                                S.pe(lambda e, m=m: e.transpose(out=Tv[b][:, m * 128:m * 128 + P], in_=qdb[b][:P, m * 128:(m + 1) * 128], identity=identf[:P, :P]),
                                     r=[("qdb", b)], w=["B%d" % (6 + b)])
                            dst = qdT if st == 0 else kdT
                            nm = "qdT" if st == 0 else "kdT"
                            if b == 0:
                                S.dve(lambda e: e.tensor_copy(out=dst[:, :, c0:c0 + P], in_=Tv[b][:, 0:256].rearrange("p (h t) -> p h t", h=2)[:, :, 0:P]),
                                      r=["B%d" % (6 + b)], w=[(nm, tt)])
                            else:
                                S.act(lambda e: e.activation(out=dst[:, :, c0:c0 + P], in_=Tv[b][:, 0:256].rearrange("p (h t) -> p h t", h=2)[:, :, 0:P], func=AF.Copy),
                                      r=["B%d" % (6 + b)], w=[(nm, tt)])
                        deferred.append(later)
                    for f in deferred:
                        f()
                    if h + 1 < n_diff_heads:
                        load_w(h + 1, st)
                    if st == 0:
                        S.op("sp", lambda e, h=h: e.dma_start(out=Hk[:, :], in_=bass.AP(tensor=tA_h, offset=h * NA, ap=[[1, 128], [1, GW]])),
                             [], ["Hk"], dma="Hk")
                        for j, (a0, a1) in enumerate(((0, 512), (512, 1024), (1024, GW))):
                            bank = 3 + j
                            S.pe(lambda e, a0=a0, a1=a1, bank=bank: e.matmul(B[bank][:, 0:a1 - a0], lhsT=Jf[:, :], rhs=Hk[:, a0:a1], start=True, stop=True),
                                 r=["Hk", "Jf"], w=["B%d" % bank])
                            S.dve(lambda e, a0=a0, a1=a1, bank=bank: e.tensor_copy(out=G[:, a0:a1], in_=B[bank][:, 0:a1 - a0]), r=["B%d" % bank], w=["G"])
                        if dbg and h == 0:
                            dma("sp", dbg_out["d_G"], G[:, :], r=["G"], w=["dbgG"], sem="dbg")
                def zgate(qb, par):
                    q0 = qb * 512
                    outs = []
                    for j in range(2):
                        zb = 6 + j

                        def mm(j=j, zb=zb, q0=q0):
                            for k in range(16):
                                S.pe(lambda e, k=k: e.matmul(B[zb][:, :], lhsT=wr[3][:, k, j * 128:(j + 1) * 128], rhs=uT[:, k, q0:q0 + 512],
                                                             start=(k == 0), stop=(k == 15)), r=[("wr", 3)], w=["B%d" % zb])

                        def gate(j=j, zb=zb, par=par):
                            S.act(lambda e: e.activation(out=ezb[j][:, :], in_=B[zb][:, :], func=AF.Exp, scale=-1.0), r=["B%d" % zb], w=[("ezb", j)])
                            S.act(lambda e: e.activation(out=ezb[j][:, :], in_=ezb[j][:, :], func=AF.Ln, bias=1.0, scale=1.0), r=[("ezb", j)], w=[("ezb", j)])
                            S.act(lambda e: e.activation(out=ezb[j][:, :], in_=ezb[j][:, :], func=AF.Exp, scale=-1.0), r=[("ezb", j)], w=[("ezb", j)])
                            S.dve(lambda e: e.tensor_tensor(out=szb[par][:, j, :], in0=B[zb][:, :], in1=ezb[j][:, :], op=ALU.mult),
                                  r=["B%d" % zb, ("ezb", j)], w=[("szb", par, j)])
                        outs.append((mm, gate))
                    return outs

                def seg_post_A(qb, m):
                    S.act(lambda e: e.activation(out=recd[:, :], in_=sraw[:, :], func=AF.Ln), r=["sraw"], w=["recd"])
                    S.act(lambda e: e.activation(out=recd[:, :], in_=recd[:, :], func=AF.Exp, scale=-1.0), r=["recd"], w=["recd"])
                    for j in range(2):
                        if m == 0:
                            S.dve(lambda e, j=j: e.tensor_tensor(out=t1[:, j, :], in0=oraw[:, j, :], in1=recd[:, :], op=ALU.mult),
                                  r=[("oraw", j), "recd"], w=[("t1", j)])
                        else:
                            S.dve(lambda e, j=j: e.tensor_tensor(out=t2[:, :], in0=oraw[:, j, :], in1=recd[:, :], op=ALU.mult),
                                  r=[("oraw", j), "recd"], w=["t2"])
                            S.dve(lambda e, j=j: e.scalar_tensor_tensor(out=od[:, j, :], in0=t2[:, :], scalar=neglam[:, 0:1], in1=t1[:, j, :],
                                                                        op0=ALU.mult, op1=ALU.add), r=["t2", ("t1", j)], w=[("od", j)])
                    if m == 0:
                        return
                    for j in range(2):
                        S.dve(lambda e, j=j: e.tensor_tensor(out=sqb[:, j, :], in0=od[:, j, :], in1=od[:, j, :], op=ALU.mult), r=[("od", j)], w=[("sqb", j)])

                def seg_post_B(qb, bank, h=h):
                    q0 = qb * 512
                    par = qb % 2
                    for j in range(2):
                        S.pe(lambda e, j=j: e.matmul(B[bank][:, :], lhsT=ones[:, :], rhs=sqb[:, j, :], start=(j == 0), stop=(j == 1)), r=[("sqb", j)], w=["B%d" % bank])
                    S.act(lambda e: e.activation(out=rsb[:, :], in_=B[bank][:, :], func=AF.Ln, bias=256 * EPS, scale=1.0), r=["B%d" % bank], w=["rsb"])
                    S.act(lambda e: e.activation(out=rsb[:, :], in_=rsb[:, :], func=AF.Exp, scale=-0.5), r=["rsb"], w=["rsb"])
                    oi = ogb_cnt[0] % 2
                    ogb_cnt[0] += 1
                    for j in range(2):
                        S.dve(lambda e, j=j: e.scalar_tensor_tensor(out=onn[:, :], in0=od[:, j, :], scalar=gsub[:, j:j + 1], in1=rsb[:, :], op0=ALU.mult, op1=ALU.mult),
                              r=[("od", j), "rsb"], w=["onn"])
                        S.dve(lambda e, j=j, oi=oi: e.tensor_tensor(out=ogb[oi][:, j, :], in0=onn[:, :], in1=szb[par][:, j, :], op=ALU.mult),
                              r=["onn", ("szb", par, j)], w=[("ogb", oi)])
                    dma("sp", ob_scr[h * 256:(h + 1) * 256, q0:q0 + 512].rearrange("(j p) t -> p j t", p=128), ogb[oi][:, :, :],
                        r=[("ogb", oi)], w=[("ob", h, qb)], sem="ogb%d" % oi)

                for mm, gate in zgate(0, 0):
                    mm()
                    gate()
                for qb in range(4):
                    q0 = qb * 512
                    qk = [("qdT", t) for t in range(4 * qb, 4 * qb + 4)]
                    c_post = {0: 5, 1: 9, 2: 0, 3: 0}[qb]
                    c_postB = {0: 12, 1: 16, 2: 14, 3: 7}[qb]
                    znext = zgate(qb + 1, (qb + 1) % 2) if qb + 1 < 4 else []
                    for m in range(2):
                        def S_mm(c, m=m, q0=q0, qk=qk):
                            P, c0 = tok(c)
                            bank = SBK[c % 3]
                            S.pe(lambda e: e.matmul(B[bank][:P, :], lhsT=kdT[:, m, c0:c0 + P], rhs=qdT[:, m, q0:q0 + 512], start=True, stop=True),
                                 r=qk + [("kdT", c)], w=["B%d" % bank])

                        def EXPc(c, qb=qb):
                            P, c0 = tok(c)
                            bank = SBK[c % 3]
                            x0 = (512 * qb - 128 * c) if c < 16 else (16 + 512 * qb)
                            if x0 <= -602 or x0 >= 218:
                                col = 0 if x0 <= -602 else GW - 1
                                S.act(lambda e: e.activation(out=PT[c % NPT][:P, :], in_=B[bank][:P, :], func=AF.Exp, bias=G[:P, col:col + 1], scale=1.0),
                                      r=["B%d" % bank, "G"], w=[("PTd", c % NPT)])
                                return
                            w0 = x0 - XLO
                            S.dve(lambda e: e.tensor_tensor(out=sbias[c % NSB][:P, :], in0=B[bank][:P, :], in1=G[:P, w0:w0 + 512], op=ALU.add),
                                  r=["B%d" % bank, "G"], w=[("sbias", c % NSB)])
                            S.act(lambda e: e.activation(out=PT[c % NPT][:P, :], in_=sbias[c % NSB][:P, :], func=AF.Exp),
                                  r=[("sbias", c % NSB)], w=[("PTd", c % NPT)])

                        def PVc(c):
                            P, c0 = tok(c)
                            for j in range(2):
                                S.pe(lambda e, j=j: e.matmul(B[2 + j][:, :], lhsT=Vd[:P, c, j * 128:(j + 1) * 128], rhs=PT[c % NPT][:P, :], start=(c == 0), stop=(c == 16)),
                                     r=[("PTd", c % NPT), ("Vd", c)], w=["B%d" % (2 + j)])
                            S.pe(lambda e: e.matmul(B[4][:, :], lhsT=ones[:P, :], rhs=PT[c % NPT][:P, :], start=(c == 0), stop=(c == 16)),
                                 r=[("PTd", c % NPT)], w=["B4"])

                        for c in range(LA):
                            S_mm(c)
                        for c in range(17):
                            EXPc(c)
                            if c + LA < 17:
                                S_mm(c + LA)
                            PVc(c)
                            if pendA[0] is not None and c == c_post:
                                pendA[0]()
                                pendA[0] = None
                            if pendB[0] is not None and pendA[0] is None and c == c_postB:
                                pendB[0](6)
                                pendB[0] = None
                            if m == 1 and znext:
                                if c == 3:
                                    znext[0][0]()
                                elif c == 6:
                                    znext[0][1]()
                                elif c == 9:
                                    znext[1][0]()
                                elif c == 12:
                                    znext[1][1]()
                        for j in range(2):
                            S.dve(lambda e, j=j: e.tensor_copy(out=oraw[:, j, :], in_=B[2 + j][:, :]), r=["B%d" % (2 + j)], w=[("oraw", j)])
                        S.dve(lambda e: e.tensor_copy(out=sraw[:, :], in_=B[4][:, :]), r=["B4"], w=["sraw"])
                        pendA[0] = (lambda qb=qb, m=m: seg_post_A(qb, m))
                        if m == 1:
                            pendB[0] = (lambda bank, qb=qb, f=seg_post_B: f(qb, bank))
                if h + 1 < n_diff_heads:
                    load_w(h + 1, 3)
            if pendA[0] is not None:
                pendA[0]()
            if pendB[0] is not None:
                pendB[0](5)
            S.flush()
        if stop == "D":
            return nc

        with ExitStack() as sF:
            oh = sb("oh", [128, 16, 1024], BF16, sF)
            mT = sb("mT", [128, 16, 1024], BF16, sF)
            wsb = sb("wsb", [128, 16384], BF16, sF)
            eg = [sb("eg%d" % i, [128, 512], F32, sF) for i in range(2)]
            tm = [sb("tm%d" % i, [128, 512], F32, sF) for i in range(2)]
            xin = [sb("xin%d" % i, [128, 512], F32, sF) for i in range(2)]
            yo = [sb("yo%d" % i, [128, 512], F32, sF) for i in range(2)]
            B = [ps("fB%d" % i, [128, 512], F32, sF) for i in range(8)]

            def ws(i):
                return wsb[:, i * 2048:(i + 1) * 2048].rearrange("p (k c) -> p k c", k=16)

            def wo(s_):
                return wsb[:, s_ * 8192:(s_ + 1) * 8192].rearrange("p (k c) -> p k c", k=16)

            w_a_v = w_a_d.rearrange("(k p) n -> p k n", p=128)
            w_b_v = w_b_d.rearrange("(k p) n -> p k n", p=128)
            w_o_v = w_o_d.rearrange("(k p) n -> p k n", p=128)
            wi = 0
            oi = 0
            xi = 0
            for hf in range(2):
                t0 = hf * 1024
                for br in range(2):
                    scr = (oa_scr, ob_scr)[br]
                    wv = (w_a_v, w_b_v)[br]
                    goff = (OFF_GA, OFF_GB)[br]
                    sv = scr.rearrange("(k p) t -> p k t", p=128)

                    def ldo(e, sv=sv, t0=t0):
                        return [e.dma_start(out=oh[:, a:a + 4, :], in_=sv[:, a:a + 4, t0:t0 + 1024]) for a in range(0, 16, 4)]
                    if not (hf == 1 and br == 0):
                        S.op("pool", ldo, [], ["oh"], dma="oh", ndma=4)
                    slots = {}

                    def ldw(j, wv=wv, goff=goff):
                        nonlocal wi
                        a, b2 = wi % 8, (wi + 1) % 8
                        wi += 2
                        wload(ws(a), wv[:, :, j * 128:(j + 1) * 128], 16, w=[("ws", a)], sem="ws%d" % a)
                        wload(ws(b2), w_in_v[:, :, goff + j * 128:goff + (j + 1) * 128], 16, w=[("ws", b2)], sem="ws%d" % b2)
                        slots[j] = (a, b2)

                    ldw(0)
                    ldw(1)
                    for j in range(16):
                        if j + 2 < 16:
                            ldw(j + 2)
                        a, b2 = slots[j]
                        for blk in range(2):
                            st = (j * 2 + blk) % 4
                            mb, gb = 2 * st, 2 * st + 1
                            c0 = blk * 512
                            ei = (j * 2 + blk) % 2
                            for k in range(16):
                                S.pe(lambda e, k=k, a=a, mb=mb, c0=c0: e.matmul(B[mb][:, :], lhsT=ws(a)[:, k, :], rhs=oh[:, k, c0:c0 + 512], start=(k == 0), stop=(k == 15)),
                                     r=[("ws", a), "oh"], w=["B%d" % mb])
                            for k in range(16):
                                S.pe(lambda e, k=k, b2=b2, gb=gb, c0=c0, t0=t0: e.matmul(B[gb][:, :], lhsT=ws(b2)[:, k, :], rhs=uT[:, k, t0 + c0:t0 + c0 + 512],
                                                                                           start=(k == 0), stop=(k == 15)),
                                     r=[("ws", b2)], w=["B%d" % gb])
                            S.act(lambda e, gb=gb, ei=ei: e.activation(out=eg[ei][:, :], in_=B[gb][:, :], func=AF.Exp, scale=-1.0), r=["B%d" % gb], w=[("eg", ei)])
                            S.act(lambda e, ei=ei: e.activation(out=eg[ei][:, :], in_=eg[ei][:, :], func=AF.Ln, bias=1.0, scale=1.0), r=[("eg", ei)], w=[("eg", ei)])
                            S.act(lambda e, ei=ei: e.activation(out=eg[ei][:, :], in_=eg[ei][:, :], func=AF.Exp, scale=-1.0), r=[("eg", ei)], w=[("eg", ei)])
                            if br == 0:
                                S.dve(lambda e, ei=ei, mb=mb, j=j, c0=c0: e.tensor_tensor(out=mT[:, j, c0:c0 + 512], in0=B[mb][:, :], in1=eg[ei][:, :], op=ALU.mult),
                                      r=["B%d" % mb, ("eg", ei)], w=[("mT", j, blk)])
                            else:
                                S.dve(lambda e, ei=ei, mb=mb: e.tensor_tensor(out=tm[ei][:, :], in0=B[mb][:, :], in1=eg[ei][:, :], op=ALU.mult),
                                      r=["B%d" % mb, ("eg", ei)], w=[("tm", ei)])
                                S.dve(lambda e, ei=ei, j=j, c0=c0: e.tensor_tensor(out=mT[:, j, c0:c0 + 512], in0=tm[ei][:, :], in1=mT[:, j, c0:c0 + 512], op=ALU.add),
                                      r=[("tm", ei), ("mT", j, blk)], w=[("mT", j, blk)])
                if hf == 0:
                    sv0 = oa_scr.rearrange("(k p) t -> p k t", p=128)
                    S.op("pool", lambda e: [e.dma_start(out=oh[:, a:a + 4, :], in_=sv0[:, a:a + 4, 1024:2048]) for a in range(0, 16, 4)],
                         [], ["oh"], dma="oh", ndma=4)
                for cg in range(4):
                    so = cg % 2
                    wload(wo(so), w_o_v[:, :, cg * 512:(cg + 1) * 512], 16, w=[("ws", 4 * so + i) for i in range(4)], sem="wo%d" % so, nsplit=4)
                    for t in range(8):
                        r0 = t0 + t * 128
                        xs = xi % 2
                        xi += 1
                        bank = xi % 8
                        dma("sp", xin[xs][:, :], x_d[r0:r0 + 128, cg * 512:(cg + 1) * 512], w=[("xin", xs)], sem="xin%d" % xs)
                        for k in range(16):
                            S.pe(lambda e, k=k, t=t, so=so, bank=bank: e.matmul(B[bank][:, :], lhsT=mT[:, k, t * 128:(t + 1) * 128], rhs=wo(so)[:, k, :],
                                                                              start=(k == 0), stop=(k == 15)),
                                 r=[("mT", k, t // 4)] + [("ws", 4 * so + i) for i in range(4)], w=["B%d" % bank])
                        S.dve(lambda e, xs=xs, bank=bank: e.tensor_tensor(out=yo[xs][:, :], in0=B[bank][:, :], in1=xin[xs][:, :], op=ALU.add),
                              r=["B%d" % bank, ("xin", xs)], w=[("yo", xs)])
                        dma("sp", out_d[r0:r0 + 128, cg * 512:(cg + 1) * 512], yo[xs][:, :], r=[("yo", xs)], w=[("out", r0, cg)], sem="yo%d" % xs)
            S.flush()
    return nc


def _t5_bucket_np(rel):
    nb, max_exact = 16, 8
    ret = np.where(rel > 0, nb, 0)
    n = np.abs(rel)
    nf = np.maximum(n, 1).astype(np.float32)
    large = max_exact + (np.log(nf / np.float32(max_exact)) / np.float32(math.log(128 / max_exact))
                         * np.float32(nb - max_exact)).astype(np.int32)
    large = np.minimum(large, nb - 1)
    return ret + np.where(n < max_exact, n, large)


def _const_tables():
    half = 32
    inv = (np.float32(10000.0) ** (-(np.arange(half, dtype=np.float32) / np.float32(half)))).astype(np.float32)
    rope = np.zeros((128, 17, 64), np.float32)
    for tt in range(17):
        pos = (16 + 128 * tt + np.arange(128)) if tt < 16 else np.arange(128)
        ang = (pos.astype(np.float32)[:, None] * inv[None, :]).astype(np.float32)
        rope[:, tt, 0:32] = np.cos(ang.astype(np.float64))
        rope[:, tt, 32:64] = np.sin(ang.astype(np.float64))
    dvals = 767 - np.arange(NA)
    bk = _t5_bucket_np(dvals.astype(np.int32))
    oneh = np.zeros((32, NA), np.float32)
    oneh[bk, np.arange(NA)] = 1.0
    return rope.reshape(128, 17 * 64), oneh


_NC_CACHE = {}


def make_in_maps(inputs, n=8):
    f = lambda a: np.ascontiguousarray(np.asarray(a, dtype=np.float32))
    rope, oneh = _const_tables()
    gsm = np.concatenate([f(inputs["q_a_norm"])[0], f(inputs["kv_a_norm"])[0], f(inputs["mla_q_norm"])[0], f(inputs["mla_k_norm"])[0],
                          f(inputs["diff_q_norm"])[0], f(inputs["diff_k_norm"])[0], f(inputs["diff_lambda"])[0].reshape(-1)])
    shared = {
        "meta": f(inputs["meta_tokens"]),
        "w_in": f(inputs["w_in"])[0], "w_uq": f(inputs["w_uq"])[0], "w_ukv": f(inputs["w_ukv"])[0],
        "w_a": f(inputs["w_branch_a"])[0], "w_b": f(inputs["w_branch_b"])[0], "w_o": f(inputs["w_out"])[0],
        "gA": np.ascontiguousarray(np.broadcast_to(f(inputs["norm_in"])[0][None, :], (128, D))),
        "gsm": np.ascontiguousarray(np.broadcast_to(gsm[None, :], (128, G_END))),
        "gcol": np.ascontiguousarray(f(inputs["diff_subln"])[0].reshape(2, 128).T),
        "relb": f(inputs["rel_bias"]),
        "ropeT": rope, "onehot": oneh,
    }
    x = f(inputs["x"])
    return [dict(shared, x=x[i]) for i in range(n)]


def kernel(**inputs):
    if "nc" not in _NC_CACHE:
        _NC_CACHE["nc"] = build()
    nc = _NC_CACHE["nc"]
    in_maps = make_in_maps(inputs, 8)
    res = run_bass_kernel_spmd(nc, in_maps, core_ids=list(range(8)))
    return np.stack([np.asarray(r["out"], dtype=np.float32) for r in res.results], axis=0)
```
